# Optimizing a Trainium2 kernel written in Bass

```python
import jax, jax.numpy as jnp
from jax import lax
import numpy as np

D_MODEL = 1024
BATCH = 8
SEQ = 4096
DEPTH = 4

N_MIXERS = 2
N_MLA = (DEPTH + 1) // 2
N_SGU = DEPTH // 2
BRANCH_W = 2 * D_MODEL
MLA_HEADS = 16
NOPE_DIM = 128
ROPE_DIM = 64
V_DIM = BRANCH_W // MLA_HEADS
Q_LORA = 384
KV_LORA = 256
ROPE_THETA = 10000.0
Q_BLOCK = 128
ATTN_SCALE = (NOPE_DIM + ROPE_DIM) ** -0.5
MLA_IN_W = Q_LORA + KV_LORA + ROPE_DIM + BRANCH_W
CHUNK = 128
SGU_GROUPS = 16
SGU_GW = BRANCH_W // SGU_GROUPS
SGU_IN_W = 3 * BRANCH_W
EPS = 1e-6
LN_EPS = 1e-5

kernel_name = "hybrid_mla_chunked_sgu_trunk"


def rms_norm(x, g):
    xf = x.astype(jnp.float32)
    y = xf * lax.rsqrt(jnp.mean(xf * xf, axis=-1, keepdims=True) + EPS)
    return (y * g.astype(jnp.float32)).astype(x.dtype)


def layer_norm(x, g, b):
    xf = x.astype(jnp.float32)
    mu = jnp.mean(xf, axis=-1, keepdims=True)
    var = jnp.mean(jnp.square(xf - mu), axis=-1, keepdims=True)
    y = (xf - mu) * lax.rsqrt(var + LN_EPS)
    return (y * g.astype(jnp.float32) + b.astype(jnp.float32)).astype(x.dtype)


def rope_tables(positions, dtype):
    inv_freq = 1.0 / (ROPE_THETA ** (jnp.arange(0, ROPE_DIM, 2, dtype=jnp.float32) / ROPE_DIM))
    ang = positions.astype(jnp.float32)[..., None] * inv_freq
    return jnp.cos(ang).astype(dtype), jnp.sin(ang).astype(dtype)


def apply_rope(x, cos, sin):
    x1, x2 = jnp.split(x, 2, axis=-1)
    return jnp.concatenate([x1 * cos - x2 * sin, x2 * cos + x1 * sin], axis=-1)


def mla_branch(xn, w_in, q_norm_g, kv_norm_g, w_uq, w_ukv, cos, sin):
    B, S, _ = xn.shape
    h = xn @ w_in
    c_q, c_kv, k_r, gate = jnp.split(h, [Q_LORA, Q_LORA + KV_LORA, Q_LORA + KV_LORA + ROPE_DIM], axis=-1)
    c_q = rms_norm(c_q, q_norm_g)
    c_kv = rms_norm(c_kv, kv_norm_g)
    q = (c_q @ w_uq).reshape(B, S, MLA_HEADS, NOPE_DIM + ROPE_DIM)
    q_nope = q[..., :NOPE_DIM]
    q_rope = apply_rope(q[..., NOPE_DIM:], cos[:, :, None, :], sin[:, :, None, :])
    kv = (c_kv @ w_ukv).reshape(B, S, MLA_HEADS, NOPE_DIM + V_DIM)
    k_nope = kv[..., :NOPE_DIM]
    v = kv[..., NOPE_DIM:]
    k_rope = apply_rope(k_r, cos, sin)

    nb = S // Q_BLOCK
    qn_b = q_nope.reshape(B, nb, Q_BLOCK, MLA_HEADS, NOPE_DIM).transpose(1, 0, 2, 3, 4)
    qr_b = q_rope.reshape(B, nb, Q_BLOCK, MLA_HEADS, ROPE_DIM).transpose(1, 0, 2, 3, 4)
    k_pos = jnp.arange(S)
    neg = jnp.finfo(jnp.float32).min

    def attend(args):
        qn, qr, bi = args
        s = (jnp.einsum('bqhd,bkhd->bhqk', qn, k_nope, preferred_element_type=jnp.float32)
             + jnp.einsum('bqhr,bkr->bhqk', qr, k_rope, preferred_element_type=jnp.float32)) * ATTN_SCALE
        q_pos = bi * Q_BLOCK + jnp.arange(Q_BLOCK)
        causal = k_pos[None, :] <= q_pos[:, None]
        p = jax.nn.softmax(jnp.where(causal, s, neg), axis=-1).astype(v.dtype)
        return jnp.einsum('bhqk,bkhd->bqhd', p, v)

    o = lax.map(attend, (qn_b, qr_b, jnp.arange(nb)))
    o = o.transpose(1, 0, 2, 3, 4).reshape(B, S, MLA_HEADS * V_DIM)
    return o * jax.nn.silu(gate)


def sgu_branch(xn, w_in, ln_g, ln_b, w_s, b_s):
    B, S, _ = xn.shape
    u, v, gate = jnp.split(xn @ w_in, 3, axis=-1)
    u = jax.nn.gelu(u, approximate=False)
    v = layer_norm(jax.nn.gelu(v, approximate=False), ln_g, ln_b)
    vb = v.reshape(B, S // CHUNK, CHUNK, SGU_GROUPS, SGU_GW)
    tri = jnp.tril(jnp.ones((CHUNK, CHUNK), dtype=bool))
    w = jnp.where(tri[None], w_s, 0).astype(v.dtype)
    sv = jnp.einsum('gts,bcsgd->bctgd', w, vb) + b_s.T[None, None, :, :, None].astype(v.dtype)
    return u * sv.reshape(B, S, BRANCH_W) * jax.nn.silu(gate)


def setup_inputs(seed: int = 0) -> dict:
    key = jax.random.key(seed)
    ks = jax.random.split(key, 20)
    f32 = jnp.float32
    nrm = lambda k, shape, fan_in: jax.random.normal(k, shape, f32) * (fan_in ** -0.5)
    gain = lambda k, shape: 1.0 + 0.01 * jax.random.normal(k, shape, f32)
    x = jax.random.normal(ks[0], (BATCH, SEQ, D_MODEL), f32)
    positions = jnp.broadcast_to(jnp.arange(SEQ, dtype=jnp.int32)[None, :], (BATCH, SEQ))
    return {
        "x": x,
        "positions": positions,
        "norm_g": gain(ks[1], (DEPTH, D_MODEL)),
        "final_g": gain(ks[2], (D_MODEL,)),
        "mla_w_in": nrm(ks[3], (N_MLA, D_MODEL, MLA_IN_W), D_MODEL),
        "mla_q_norm_g": gain(ks[4], (N_MLA, Q_LORA)),
        "mla_kv_norm_g": gain(ks[5], (N_MLA, KV_LORA)),
        "mla_w_uq": nrm(ks[6], (N_MLA, Q_LORA, MLA_HEADS * (NOPE_DIM + ROPE_DIM)), Q_LORA),
        "mla_w_ukv": nrm(ks[7], (N_MLA, KV_LORA, MLA_HEADS * (NOPE_DIM + V_DIM)), KV_LORA),
        "mla_w_o": nrm(ks[8], (N_MLA, BRANCH_W, D_MODEL), BRANCH_W),
        "sgu_w_in": nrm(ks[9], (N_SGU, D_MODEL, SGU_IN_W), D_MODEL),
        "sgu_ln_g": gain(ks[10], (N_SGU, BRANCH_W)),
        "sgu_ln_b": 0.01 * jax.random.normal(ks[11], (N_SGU, BRANCH_W), f32),
        "sgu_w_s": nrm(ks[12], (N_SGU, SGU_GROUPS, CHUNK, CHUNK), CHUNK),
        "sgu_b_s": gain(ks[13], (N_SGU, SGU_GROUPS, CHUNK)),
        "sgu_w_o": nrm(ks[14], (N_SGU, BRANCH_W, D_MODEL), BRANCH_W),
    }


def reference(x, positions, norm_g, final_g, mla_w_in, mla_q_norm_g, mla_kv_norm_g,
              mla_w_uq, mla_w_ukv, mla_w_o, sgu_w_in, sgu_ln_g, sgu_ln_b, sgu_w_s,
              sgu_b_s, sgu_w_o):
    cos, sin = rope_tables(positions, x.dtype)
    for i in range(DEPTH):
        xn = rms_norm(x, norm_g[i])
        j = i // N_MIXERS
        if i % N_MIXERS == 0:
            y = mla_branch(xn, mla_w_in[j], mla_q_norm_g[j], mla_kv_norm_g[j],
                           mla_w_uq[j], mla_w_ukv[j], cos, sin) @ mla_w_o[j]
        else:
            y = sgu_branch(xn, sgu_w_in[j], sgu_ln_g[j], sgu_ln_b[j],
                           sgu_w_s[j], sgu_b_s[j]) @ sgu_w_o[j]
        x = x + y
    return rms_norm(x, final_g)
```

```python
import math
from contextlib import ExitStack, contextmanager

import numpy as np
import ml_dtypes

import concourse.bass as bass
import concourse.mybir as mybir
from concourse.bass_utils import run_bass_kernel_spmd

F32 = mybir.dt.float32
BF16 = mybir.dt.bfloat16
I32 = mybir.dt.int32
AF = mybir.ActivationFunctionType
ALU = mybir.AluOpType

S = 4096
D = 1024
NT = S // 128
NG = S // 512
DEPTH = 4
HEADS = 16
BW = 2048
Q_LORA = 384
KV_LORA = 256
ROPE = 64
MLA_IN_W = Q_LORA + KV_LORA + ROPE + BW
SGU_IN_W = 3 * BW
EPS = 1e-6
LN_EPS = 1e-5
ATTN_SCALE = (128 + 64) ** -0.5
TWO_PI = 2.0 * math.pi


class Buf:
    __slots__ = ("name", "lw", "rd", "psum")

    def __init__(self, name, psum=False):
        self.name = name
        self.lw = None
        self.rd = []
        self.psum = psum


class Sched:
    STREAMS = ("pe", "act", "dve", "pool", "sp")

    def __init__(self):
        self.ops = []

    def add(self, eng, fn, r=(), w=(), dma=None):
        self.ops.append((eng, fn, tuple(r), tuple(w), dma))

    def barrier(self):
        self.ops.append(("bar", None, (), (), None))

    def emit(self, nc, stack):
        ops = self.ops
        n = len(ops)
        deps = [None] * n
        last_dma = {}
        last_op = {}
        pending = {}
        for i, (eng, fn, r, w, dma) in enumerate(ops):
            if eng == "bar":
                deps[i] = []
                snap = set(last_op.values()) | set(last_dma.values())
                for s_ in self.STREAMS:
                    pending[s_] = snap
                continue
            kinds = {}
            for b in r:
                if b.lw is not None:
                    kinds.setdefault(b.lw, set()).add("raw")
                if b.psum:
                    for j in b.rd:
                        if ops[j][0] != eng:
                            kinds.setdefault(j, set()).add("rar")
            for b in w:
                if b.lw is not None:
                    kinds.setdefault(b.lw, set()).add("waw")
                for j in b.rd:
                    kinds.setdefault(j, set()).add("war")
            if dma is not None and dma in last_dma:
                kinds.setdefault(last_dma[dma], set()).add("raw")
            dd = []
            for j, ks in kinds.items():
                if j == i:
                    continue
                ej, dj = ops[j][0], ops[j][4]
                if dj is None and dma is None and ej == eng:
                    if eng == "pe" or "raw" not in ks:
                        continue
                dd.append(j)
            if eng in pending:
                for j in pending.pop(eng):
                    if ops[j][0] == eng and ops[j][4] is None:
                        continue
                    dd.append(j)
            if fn is not None:
                last_op[eng] = i
            best = {}
            for j in dd:
                k_ = ("d", ops[j][4]) if ops[j][4] is not None else ("e", ops[j][0])
                if k_ not in best or best[k_] < j:
                    best[k_] = j
            dd = list(best.values())
            deps[i] = dd
            for b in r:
                b.rd.append(i)
            for b in w:
                b.lw = i
                b.rd = []
            if dma is not None:
                last_dma[dma] = i
        needs = [False] * n
        for dd in deps:
            for j in dd:
                needs[j] = True
        sems = {}

        def get_sem(key):
            if key not in sems:
                sems[key] = stack.enter_context(nc.semaphore("s_%s" % key))
            return sems[key]

        token = [None] * n
        tick = {}
        for i, (eng, fn, r, w, dma) in enumerate(ops):
            if fn is None:
                continue
            if dma is not None:
                key = "d_" + dma.name
                tick[key] = tick.get(key, 0) + 16
                token[i] = (key, tick[key], 16)
            elif needs[i]:
                key = "e_" + eng
                tick[key] = tick.get(key, 0) + 1
                token[i] = (key, tick[key], 1)
        for key in tick:
            get_sem(key)
        self.n_sems = len(sems)
        self.max_tick = max(tick.values()) if tick else 0
        by_stream = {s: [] for s in self.STREAMS}
        for i, op in enumerate(ops):
            if op[0] != "bar":
                by_stream[op[0]].append(i)

        def simulate():
            val = {k: 0 for k in sems}
            pos = {s_: 0 for s_ in self.STREAMS}
            progress = True
            while progress:
                progress = False
                for s_ in self.STREAMS:
                    lst = by_stream[s_]
                    while pos[s_] < len(lst):
                        i = lst[pos[s_]]
                        ok = all(val[token[j][0]] >= token[j][1] for j in deps[i])
                        if not ok:
                            break
                        if token[i] is not None:
                            val[token[i][0]] += token[i][2]
                        pos[s_] += 1
                        progress = True
            stuck = {s_: (pos[s_], len(by_stream[s_])) for s_ in self.STREAMS if pos[s_] < len(by_stream[s_])}
            for s_, (p_, n_) in stuck.items():
                i = by_stream[s_][p_]
                print("STUCK", s_, p_, n_, "op", i, [(j, ops[j][0], token[j], val[token[j][0]]) for j in deps[i]
                                                      if val[token[j][0]] < token[j][1]])
            return not stuck

        self.sim_ok = simulate()

        def run_stream(e, stream):
            waited = {}
            for i in by_stream[stream]:
                eng, fn, r, w, dma = ops[i]
                need = {}
                for j in deps[i]:
                    key, val, _ = token[j]
                    if need.get(key, 0) < val:
                        need[key] = val
                for key, val in need.items():
                    if waited.get(key, 0) < val:
                        e.wait_ge(sems[key], val)
                        waited[key] = val
                if fn is None:
                    continue
                inst = fn(e)
                if token[i] is not None:
                    key, val, inc = token[i]
                    inst.then_inc(sems[key], inc)

        with nc.Block() as block:
            @block.tensor
            def _(e):
                run_stream(e, "pe")

            @block.scalar
            def _(e):
                run_stream(e, "act")

            @block.vector
            def _(e):
                run_stream(e, "dve")

            @block.gpsimd
            def _(e):
                run_stream(e, "pool")

            @block.sync
            def _(e):
                run_stream(e, "sp")


class Ring:
    def __init__(self, items):
        self.items = items
        self.i = 0

    def next(self):
        it = self.items[self.i % len(self.items)]
        self.i += 1
        return it


class Prog:
    def __init__(self, layers, first_from_x=True, do_final=True, part=None, slot=False):
        self.part = part
        self.slot = slot
        self.layers = layers
        self.do_final = do_final
        self.first_from_x = first_from_x
        self.nc = bass.Bass("TRN2", target_bir_lowering=False)
        self.sc = Sched()
        self.uid = 0

    @contextmanager
    def scope(self):
        with ExitStack() as st:
            yield st
        self.sc.barrier()

    def dram(self, name, shape, dt, kind):
        return self.nc.dram_tensor(name, list(shape), dt, kind=kind)

    def sb(self, st, name, shape, dt):
        self.uid += 1
        t = st.enter_context(self.nc.sbuf_tensor("%s_%d" % (name, self.uid), list(shape), dt))
        return t, Buf("%s_%d" % (name, self.uid))

    def ps(self, st, name, shape, dt):
        self.uid += 1
        t = st.enter_context(self.nc.psum_tensor("%s_%d" % (name, self.uid), list(shape), dt))
        return t, Buf("%s_%d" % (name, self.uid), psum=True)

    def dma(self, out, in_, key, r=(), w=(), eng="sp"):
        self.sc.add(eng, lambda e: e.dma_start(out=out, in_=in_), r=r, w=w, dma=key)

    def mm(self, out, lhsT, rhs, start, stop, r, w):
        self.sc.add("pe", lambda e: e.matmul(out, lhsT, rhs, start=start, stop=stop), r=r, w=w)

    def tr(self, out, in_, ident, r, w):
        self.sc.add("pe", lambda e: e.transpose(out, in_, ident), r=r, w=w)

    def act(self, out, in_, func, r, w, bias=None, scale=None, accum_out=None):
        kw = {}
        if bias is not None:
            kw["bias"] = bias
        if scale is not None:
            kw["scale"] = scale
        if accum_out is not None:
            kw["accum_out"] = accum_out
        self.sc.add("act", lambda e: e.activation(out, in_, func, **kw), r=r, w=w)

    def ts(self, eng, out, in0, s1, s2, op0, op1, r, w):
        if op1 is None:
            self.sc.add(eng, lambda e: e.tensor_scalar(out, in0, s1, None, op0), r=r, w=w)
        else:
            self.sc.add(eng, lambda e: e.tensor_scalar(out, in0, s1, s2, op0, op1), r=r, w=w)

    def tt(self, eng, out, in0, in1, op, r, w):
        self.sc.add(eng, lambda e: e.tensor_tensor(out, in0, in1, op), r=r, w=w)

    def stt(self, out, in0, scalar, in1, op0, op1, r, w):
        self.sc.add("dve", lambda e: e.scalar_tensor_tensor(out, in0, scalar, in1, op0, op1), r=r, w=w)

    def cp(self, eng, out, in_, r, w):
        if eng == "act":
            self.sc.add("act", lambda e: e.activation(out, in_, AF.Copy), r=r, w=w)
        else:
            self.sc.add(eng, lambda e: e.tensor_copy(out, in_), r=r, w=w)

    def build(self):
        nc = self.nc
        self.x_in = self.dram("x", [S, D], F32, "ExternalInput")
        self.out = self.dram("out", [S, D], F32, "ExternalOutput")
        self.xres = self.dram("xres", [S, D], F32, "Internal")
        self.gateT = self.dram("gateT", [HEADS, 128, S], BF16, "Internal")
        og_kind = {None: "Internal", "A": "ExternalOutput", "B": "ExternalInput"}[self.part]
        self.ogT = self.dram("ogT", [HEADS, 128, S], BF16, og_kind)
        N2 = 1 if self.slot else 2
        N4 = 1 if self.slot else DEPTH
        d = {}
        d["pos_tok"] = self.dram("pos_tok", [128, NT], I32, "ExternalInput")
        d["pos_row"] = self.dram("pos_row", [1, S], I32, "ExternalInput")
        d["invf_tok"] = self.dram("invf_tok", [128, 64], F32, "ExternalInput")
        d["invf_col"] = self.dram("invf_col", [64, 1], F32, "ExternalInput")
        d["ident"] = self.dram("ident", [128, 128], BF16, "ExternalInput")
        d["tri"] = self.dram("tri", [128, 128], BF16, "ExternalInput")
        d["norm_g"] = self.dram("norm_g", [N4, 128, 8], F32, "ExternalInput")
        d["final_g"] = self.dram("final_g", [1, D], F32, "ExternalInput")
        d["mla_w_in"] = self.dram("mla_w_in", [N2, D, MLA_IN_W], F32, "ExternalInput")
        d["mla_qg"] = self.dram("mla_qg", [N2, 128, 3], F32, "ExternalInput")
        d["mla_kvg"] = self.dram("mla_kvg", [N2, 128, 2], F32, "ExternalInput")
        d["mla_w_uq"] = self.dram("mla_w_uq", [N2, Q_LORA, HEADS * 192], F32, "ExternalInput")
        d["mla_w_ukv"] = self.dram("mla_w_ukv", [N2, KV_LORA, HEADS * 256], F32, "ExternalInput")
        d["mla_w_o"] = self.dram("mla_w_o", [N2, BW, D], F32, "ExternalInput")
        d["sgu_w_in"] = self.dram("sgu_w_in", [N2, D, SGU_IN_W], F32, "ExternalInput")
        d["sgu_ln_g"] = self.dram("sgu_ln_g", [N2, 1, BW], F32, "ExternalInput")
        d["sgu_ln_b"] = self.dram("sgu_ln_b", [N2, 1, BW], F32, "ExternalInput")
        d["sgu_w_sT"] = self.dram("sgu_w_sT", [N2, 128, 16, 128], F32, "ExternalInput")
        d["sgu_b_sT"] = self.dram("sgu_b_sT", [N2, 128, 16], F32, "ExternalInput")
        d["sgu_w_o"] = self.dram("sgu_w_o", [N2, BW, D], F32, "ExternalInput")
        self.d = d
        self.xres_b = [Buf("xres%d" % t) for t in range(NT)]
        self.out_b = [Buf("out%d" % t) for t in range(NT)]
        self.gate_b = [[Buf("gate%d_%d" % (h4, g)) for g in range(NG)] for h4 in range(4)]
        self.og_b = [[Buf("og%d_%d" % (h, g)) for g in range(NG)] for h in range(HEADS)]
        self.x_from_input = self.first_from_x

        with ExitStack() as top:
            self.ident, self.ident_b = self.sb(top, "ident", [128, 128], BF16)
            self.tri, self.tri_b = self.sb(top, "tri", [128, 128], BF16)
            self.epsn, self.epsn_b = self.sb(top, "epsn", [128, 1], F32)
            self.epsl, self.epsl_b = self.sb(top, "epsl", [128, 1], F32)
            self.dma(self.ident[:, :], d["ident"].ap(), self.ident_b, w=[self.ident_b])
            self.dma(self.tri[:, :], d["tri"].ap(), self.tri_b, w=[self.tri_b])
            self.sc.add("pool", lambda e: e.memset(self.epsn[:, :], EPS), w=[self.epsn_b])
            self.sc.add("pool", lambda e: e.memset(self.epsl[:, :], LN_EPS), w=[self.epsl_b])
            for li, l in enumerate(self.layers):
                last = (li == len(self.layers) - 1)
                mode = "res" if not last else ("final" if self.do_final else "out")
                ll, jj = (0, 0) if self.slot else (l, l // 2)
                if l % 2 == 0:
                    self.mla_layer(ll, jj, mode)
                else:
                    self.sgu_layer(ll, jj, mode)
            self.sc.add("sp", None, r=self.out_b)
            if self.part == "A":
                self.sc.add("sp", None, r=[b for row in self.og_b for b in row])
            self.sc.emit(nc, top)
        return nc

    def x_src(self, t):
        if self.x_from_input:
            return self.x_in.ap()[t * 128:(t + 1) * 128, :], []
        return self.xres.ap()[t * 128:(t + 1) * 128, :], [self.xres_b[t]]

    def rms_rstd(self, xt, xt_b, junk, junk_b, ss, ss_b, rstd, rstd_b, ncols, eps_t, eps_b):
        self.act(junk, xt, AF.Square, r=[xt_b], w=[junk_b, ss_b], scale=float(ncols ** -0.5), accum_out=ss)
        self.act(ss, ss, AF.Sqrt, r=[ss_b, eps_b], w=[ss_b], bias=eps_t)
        self.sc.add("dve", lambda e: e.reciprocal(rstd, ss), r=[ss_b], w=[rstd_b])

    def load_weight_rows(self, st_ring, dst, dst_b, src_ap, gcol, gcol_b, nk, ncols, piece):
        for kc in range(nk):
            for c0 in range(0, ncols, piece):
                c1 = min(ncols, c0 + piece)
                stg, stg_b = st_ring.next()
                self.dma(stg[:, 0:c1 - c0], src_ap[kc * 128:(kc + 1) * 128, c0:c1], stg_b, w=[stg_b])
                self._wl = getattr(self, "_wl", 0) + 1
                eng = "dve" if self._wl % 2 == 0 else "act"
                if gcol is None:
                    self.cp(eng, dst[:, kc, c0:c1], stg[:, 0:c1 - c0], r=[stg_b], w=[dst_b])
                elif eng == "dve":
                    self.ts("dve", dst[:, kc, c0:c1], stg[:, 0:c1 - c0], gcol[:, kc:kc + 1], None,
                            ALU.mult, None, r=[stg_b, gcol_b], w=[dst_b])
                else:
                    self.act(dst[:, kc, c0:c1], stg[:, 0:c1 - c0], AF.Copy, r=[stg_b, gcol_b], w=[dst_b],
                             scale=gcol[:, kc:kc + 1])

    def sgu_layer(self, l, j, mode):
        d = self.d
        final = (mode == 'final')
        with self.scope() as st:
            Win, Win_b = self.sb(st, "sWin", [128, 8, SGU_IN_W], BF16)
            Wo, Wo_b = self.sb(st, "sWo", [128, 16, D], BF16)
            WsT, WsT_b = self.sb(st, "sWsT", [128, 16, 128], BF16)
            Gb, Gb_b = self.sb(st, "sG", [128, BW], F32)
            Bb, Bb_b = self.sb(st, "sB", [128, BW], F32)
            bsT, bsT_b = self.sb(st, "sbs", [128, 16], F32)
            gcol, gcol_b = self.sb(st, "sgcol", [128, 8], F32)
            if final:
                Gf, Gf_b = self.sb(st, "sGf", [128, D], F32)
                self.dma(Gf[:, :], bass.AP(d["final_g"], 0, [[0, 128], [1, D]]), Gf_b, w=[Gf_b])
            self.dma(gcol[:, :], d["norm_g"].ap()[l], gcol_b, w=[gcol_b])
            self.dma(bsT[:, :], d["sgu_b_sT"].ap()[j], bsT_b, w=[bsT_b])
            self.dma(Gb[:, :], bass.AP(d["sgu_ln_g"], j * BW, [[0, 128], [1, BW]]), Gb_b, w=[Gb_b])
            self.dma(Bb[:, :], bass.AP(d["sgu_ln_b"], j * BW, [[0, 128], [1, BW]]), Bb_b, w=[Bb_b])
            with self.scope() as wst:
                stg = Ring([self.sb(wst, "sstg", [128, 2048], F32) for _ in range(4)])
                self.load_weight_rows(stg, Win, Win_b, d["sgu_w_in"].ap()[j], gcol, gcol_b, 8, SGU_IN_W, 2048)
                self.load_weight_rows(stg, Wo, Wo_b, d["sgu_w_o"].ap()[j], None, None, 16, D, 2048)
                sw, sw_b = stg.next()
                self.dma(sw[:, :], d["sgu_w_sT"].ap()[j].rearrange("s g t -> s (g t)"), sw_b, w=[sw_b])
                for g in range(16):
                    self.tt("pool", WsT[:, g, :], sw[:, g * 128:(g + 1) * 128], self.tri[:, :], ALU.mult,
                            r=[sw_b, self.tri_b], w=[WsT_b])

            xts = Ring([self.sb(st, "sxt", [128, D], F32) for _ in range(3)])
            xnr = Ring([self.sb(st, "sxn", [128, D], BF16) for _ in range(1)])
            xnTr = Ring([self.sb(st, "sxnT", [128, 8, 128], BF16) for _ in range(2)])
            gvr = Ring([self.sb(st, "sgv", [128, BW], F32) for _ in range(1)])
            sgr = Ring([self.sb(st, "ssg", [128, BW], BF16) for _ in range(1)])
            ur = Ring([self.sb(st, "su", [128, BW], BF16) for _ in range(2)])
            vlr = Ring([self.sb(st, "svl", [128, BW], BF16) for _ in range(2)])
            prr = Ring([self.sb(st, "spr", [128, BW], BF16) for _ in range(1)])
            prTr = Ring([self.sb(st, "sprT", [128, 16, 128], BF16) for _ in range(1)])
            str_ = Ring([self.sb(st, "sst", [128, 16], F32) for _ in range(3)])
            pT = Ring([self.ps(st, "spT", [128, 8, 128], BF16) for _ in range(2)])
            pm = Ring([self.ps(st, "spm", [128, 512], F32) for _ in range(3)])
            psp = Ring([self.ps(st, "spsp", [128, 512], F32) for _ in range(1)])
            py = [self.ps(st, "spy", [128, 512], F32) for _ in range(2)]

            state = {}
            order = [4, 5, 6, 7, 0, 1, 2, 3, 8, 9, 10, 11]

            def a1a(t):
                xt, xt_b = xts.next()
                xn, xn_b = xnr.next()
                stt_, st_b = str_.next()
                src, srcb = self.x_src(t)
                self.dma(xt[:, :], src, xt_b, r=srcb, w=[xt_b])
                state[t] = dict(xt=(xt, xt_b), xn=(xn, xn_b), st=(stt_, st_b))

            def a1n(t):
                xt, xt_b = state[t]["xt"]
                xn, xn_b = state[t]["xn"]
                stt_, st_b = state[t]["st"]
                self.rms_rstd(xt[:, :], xt_b, xn[:, :], xn_b, stt_[:, 0:1], st_b, stt_[:, 1:2], st_b, D,
                              self.epsn[:, 0:1], self.epsn_b)
                self.ts("dve", xn[:, :], xt[:, :], stt_[:, 1:2], None, ALU.mult, None, r=[xt_b, st_b], w=[xn_b])

            def a1b(t):
                xn, xn_b = state[t]["xn"]
                xnT, xnT_b = xnTr.next()
                p, p_b = pT.next()
                for kc in range(8):
                    self.tr(p[:, kc, :], xn[:, kc * 128:(kc + 1) * 128], self.ident[:, :],
                            r=[xn_b, self.ident_b], w=[p_b])
                self.cp("dve", xnT[:, :, :], p[:, :, :], r=[p_b], w=[xnT_b])
                state[t]["xnT"] = (xnT, xnT_b)
                state[t]["gv"] = gvr.next()
                state[t]["u"] = ur.next()
                state[t]["sg"] = sgr.next()
                state[t]["vl"] = vlr.next()

            def a_group(t, idx):
                stt = state[t]
                xnT, xnT_b = stt["xnT"]
                gv, gv_b = stt["gv"]
                u, u_b = stt["u"]
                sg, sg_b = stt["sg"]
                vl, vl_b = stt["vl"]
                nb = order[idx]
                pp, pp_b = pm.next()
                for kc in range(8):
                    self.mm(pp[:, :], xnT[:, kc, :], Win[:, kc, nb * 512:(nb + 1) * 512], kc == 0, kc == 7,
                            r=[xnT_b, Win_b], w=[pp_b])
                if nb < 4:
                    self.act(u[:, nb * 512:(nb + 1) * 512], pp[:, :], AF.Gelu, r=[pp_b], w=[u_b])
                elif nb < 8:
                    c = nb - 4
                    self.act(gv[:, c * 512:(c + 1) * 512], pp[:, :], AF.Gelu, r=[pp_b], w=[gv_b])
                else:
                    c = nb - 8
                    self.act(sg[:, c * 512:(c + 1) * 512], pp[:, :], AF.Silu, r=[pp_b], w=[sg_b])
                if nb == 7:
                    self.ln_chain(gv, gv_b, vl, vl_b, Gb, Gb_b, Bb, Bb_b, st)
                if idx == 11:
                    self.tt("pool", u[:, :], u[:, :], sg[:, :], ALU.mult, r=[u_b, sg_b], w=[u_b])

            def b1(t, q):
                stt = state[t]
                u, u_b = stt["u"]
                vl, vl_b = stt["vl"]
                if q == 0:
                    stt["pr"] = prr.next()
                    stt["prT"] = prTr.next()
                pr, pr_b = stt["pr"]
                sp, sp_b = psp.next()
                for gg in range(4):
                    g = q * 4 + gg
                    self.mm(sp[:, gg * 128:(gg + 1) * 128], WsT[:, g, :], vl[:, g * 128:(g + 1) * 128],
                            True, True, r=[WsT_b, vl_b], w=[sp_b])
                for gg in range(4):
                    g = q * 4 + gg
                    self.stt(pr[:, g * 128:(g + 1) * 128], sp[:, gg * 128:(gg + 1) * 128], bsT[:, g:g + 1],
                             u[:, g * 128:(g + 1) * 128], ALU.add, ALU.mult,
                             r=[sp_b, bsT_b, u_b], w=[pr_b])

            def b2(t, hf):
                stt = state[t]
                pr, pr_b = stt["pr"]
                prT, prT_b = stt["prT"]
                p, p_b = pT.next()
                for kk in range(8):
                    kc = hf * 8 + kk
                    self.tr(p[:, kk, :], pr[:, kc * 128:(kc + 1) * 128], self.ident[:, :],
                            r=[pr_b, self.ident_b], w=[p_b])
                self.cp("act" if hf == 0 else "dve", prT[:, hf * 8:(hf + 1) * 8, :], p[:, :, :],
                        r=[p_b], w=[prT_b])

            def b3(t, hf):
                stt = state[t]
                xt, xt_b = stt["xt"]
                pr, pr_b = stt["pr"]
                prT, prT_b = stt["prT"]
                y, y_b = py[hf]
                for kc in range(16):
                    self.mm(y[:, :], prT[:, kc, :], Wo[:, kc, hf * 512:(hf + 1) * 512], kc == 0, kc == 15,
                            r=[prT_b, Wo_b], w=[y_b])
                self.tt("dve", xt[:, hf * 512:(hf + 1) * 512], y[:, :], xt[:, hf * 512:(hf + 1) * 512], ALU.add,
                        r=[y_b, xt_b], w=[xt_b])
                if hf == 1:
                    if final:
                        self.final_norm_store(t, xt, xt_b, pr, pr_b, str_, Gf, Gf_b)
                    elif mode == 'out':
                        self.dma(self.out.ap()[t * 128:(t + 1) * 128, :], xt[:, :], xt_b, r=[xt_b], w=[self.out_b[t]])
                    else:
                        self.dma(self.xres.ap()[t * 128:(t + 1) * 128, :], xt[:, :], xt_b, r=[xt_b], w=[self.xres_b[t]])
                    del state[t]

            self._sgu_ln_tmp = Ring([self.sb(st, "slt", [128, 8], F32) for _ in range(2)])
            self._lnst = Ring([self.sb(st, "slnst", [128, 24], F32) for _ in range(2)])
            pieces = {0: ("b1", 0), 1: ("b1", 1), 2: ("b1", 2), 3: ("b1", 3), 4: ("b2", 0), 5: ("b2", 1),
                      8: ("b3", 0), 10: ("b3", 1)}
            a1a(0)
            a1n(0)
            a1b(0)
            for t in range(NT + 1):
                if t + 1 < NT:
                    a1a(t + 1)
                for idx in range(12):
                    if t < NT:
                        a_group(t, idx)
                    if t >= 1 and idx in pieces:
                        kind, arg = pieces[idx]
                        {"b1": b1, "b2": b2, "b3": b3}[kind](t - 1, arg)
                    if idx == 3 and t + 1 < NT:
                        a1n(t + 1)
                    if idx == 7 and t + 1 < NT:
                        a1b(t + 1)
        self.x_from_input = False

    def ln_chain(self, gv, gv_b, vl, vl_b, Gb, Gb_b, Bb, Bb_b, st):
        lt, lt_b = self._sgu_ln_tmp.next()
        stats, stats_b = self._lnst.next()
        for c in range(4):
            self.sc.add("dve", lambda e, c=c: e.bn_stats(stats[:, c * 6:(c + 1) * 6], gv[:, c * 512:(c + 1) * 512]),
                        r=[gv_b], w=[stats_b])
        self.sc.add("dve", lambda e: e.bn_aggr(lt[:, 0:2], stats[:, :]), r=[stats_b], w=[lt_b])
        self.act(lt[:, 2:3], lt[:, 1:2], AF.Sqrt, r=[lt_b, self.epsl_b], w=[lt_b], bias=self.epsl[:, 0:1])
        self.sc.add("dve", lambda e: e.reciprocal(lt[:, 3:4], lt[:, 2:3]), r=[lt_b], w=[lt_b])
        self.ts("dve", gv[:, :], gv[:, :], lt[:, 0:1], lt[:, 3:4], ALU.subtract, ALU.mult, r=[gv_b, lt_b], w=[gv_b])
        self.tt("pool", gv[:, :], gv[:, :], Gb[:, :], ALU.mult, r=[gv_b, Gb_b], w=[gv_b])
        self.tt("pool", vl[:, :], gv[:, :], Bb[:, :], ALU.add, r=[gv_b, Bb_b], w=[vl_b])

    def final_norm_store(self, t, xt, xt_b, junk, junk_b, str_, Gf, Gf_b):
        stt_, st_b = str_.next()
        self.rms_rstd(xt[:, :], xt_b, junk[:, 0:D], junk_b, stt_[:, 0:1], st_b, stt_[:, 1:2], st_b, D,
                      self.epsn[:, 0:1], self.epsn_b)
        self.stt(xt[:, :], xt[:, :], stt_[:, 1:2], Gf[:, :], ALU.mult, ALU.mult, r=[xt_b, st_b, Gf_b], w=[xt_b])
        self.dma(self.out.ap()[t * 128:(t + 1) * 128, :], xt[:, :], xt_b, r=[xt_b], w=[self.out_b[t]])

    def rope_tables(self, st):
        d = self.d
        cc, cc_b = self.sb(st, "cc", [128, NT, 64], F32)
        ss, ss_b = self.sb(st, "ss", [128, NT, 64], F32)
        c2T, c2T_b = self.sb(st, "c2T", [64, S], F32)
        s2T, s2T_b = self.sb(st, "s2T", [64, S], F32)
        with self.scope() as tmp:
            pi_, pi_b = self.sb(tmp, "posi", [128, NT], I32)
            pf, pf_b = self.sb(tmp, "posf", [128, NT], F32)
            ivt, ivt_b = self.sb(tmp, "ivt", [128, 64], F32)
            ang, ang_b = self.sb(tmp, "ang", [128, NT, 64], F32)
            ki, ki_b = self.sb(tmp, "ki", [128, NT * 64], I32)
            kf, kf_b = self.sb(tmp, "kf", [128, NT * 64], F32)
            self.dma(pi_[:, :], d["pos_tok"].ap(), pi_b, w=[pi_b])
            self.dma(ivt[:, :], d["invf_tok"].ap(), ivt_b, w=[ivt_b])
            self.cp("dve", pf[:, :], pi_[:, :], r=[pi_b], w=[pf_b])
            for t in range(NT):
                self.ts("dve", ang[:, t, :], ivt[:, :], pf[:, t:t + 1], None, ALU.mult, None,
                        r=[ivt_b, pf_b], w=[ang_b])
            angf = ang[:, :, :].rearrange("p t f -> p (t f)")
            self.trig(angf, ang_b, ki, ki_b, kf, kf_b, cc[:, :, :].rearrange("p t f -> p (t f)"), cc_b,
                      ss[:, :, :].rearrange("p t f -> p (t f)"), ss_b, 128, NT * 64)
        with self.scope() as tmp:
            pr_, pr_b = self.sb(tmp, "pri", [64, S], I32)
            ivc, ivc_b = self.sb(tmp, "ivc", [64, 1], F32)
            ang2, ang2_b = self.sb(tmp, "ang2", [64, S], F32)
            ki, ki_b = self.sb(tmp, "ki2", [64, S], I32)
            kf, kf_b = self.sb(tmp, "kf2", [64, S], F32)
            self.dma(pr_[:, :], bass.AP(d["pos_row"], 0, [[0, 64], [1, S]]), pr_b, w=[pr_b])
            self.dma(ivc[:, :], d["invf_col"].ap(), ivc_b, w=[ivc_b])
            self.cp("dve", ang2[:, :], pr_[:, :], r=[pr_b], w=[ang2_b])
            self.ts("dve", ang2[:, :], ang2[:, :], ivc[:, 0:1], None, ALU.mult, None, r=[ang2_b, ivc_b], w=[ang2_b])
            self.trig(ang2[:, :], ang2_b, ki, ki_b, kf, kf_b, c2T[:, :], c2T_b, s2T[:, :], s2T_b, 64, S)
        return (cc, cc_b, ss, ss_b, c2T, c2T_b, s2T, s2T_b)

    def trig(self, ang, ang_b, ki, ki_b, kf, kf_b, cosd, cos_b, sind, sin_b, P, N):
        C1 = 6.28125
        C2 = TWO_PI - C1
        PI_LO = 3.1415925
        self.ts("dve", kf[:, :], ang, 1.0 / TWO_PI, None, ALU.mult, None, r=[ang_b], w=[kf_b])
        self.cp("dve", ki[:, :], kf[:, :], r=[kf_b], w=[ki_b])
        self.cp("dve", kf[:, :], ki[:, :], r=[ki_b], w=[kf_b])
        self.stt(ang, kf[:, :], -C1, ang, ALU.mult, ALU.add, r=[kf_b, ang_b], w=[ang_b])
        self.stt(ang, kf[:, :], -C2, ang, ALU.mult, ALU.add, r=[kf_b, ang_b], w=[ang_b])
        self.ts("dve", kf[:, :], ang, math.pi, -TWO_PI, ALU.is_gt, ALU.mult, r=[ang_b], w=[kf_b])
        self.tt("dve", kf[:, :], kf[:, :], ang, ALU.add, r=[kf_b, ang_b], w=[kf_b])
        self.ts("dve", kf[:, :], kf[:, :], PI_LO, -PI_LO, ALU.min, ALU.max, r=[kf_b], w=[kf_b])
        self.act(sind, kf[:, :], AF.Sin, r=[kf_b], w=[sin_b])
        self.ts("dve", kf[:, :], ang, math.pi / 2, -TWO_PI, ALU.is_gt, ALU.mult, r=[ang_b, sin_b], w=[kf_b])
        self.stt(kf[:, :], ang, math.pi / 2, kf[:, :], ALU.add, ALU.add, r=[kf_b, ang_b], w=[kf_b])
        self.ts("dve", kf[:, :], kf[:, :], PI_LO, -PI_LO, ALU.min, ALU.max, r=[kf_b], w=[kf_b])
        self.act(cosd, kf[:, :], AF.Sin, r=[kf_b], w=[cos_b])

    def mla_layer(self, l, j, mode):
        d = self.d
        if self.part == "B":
            self.mla_phase3(l, j, mode)
            self.x_from_input = False
            return
        with self.scope() as st:
            cqT, cqT_b = self.sb(st, "cqT", [128, 3, S], BF16)
            ckvT, ckvT_b = self.sb(st, "ckvT", [128, 2, S], BF16)
            krT, krT_b = self.sb(st, "krT", [64, S], BF16)
            cc, cc_b, ss, ss_b, c2T, c2T_b, s2T, s2T_b = self.rope_tables(st)
            qg, qg_b = self.sb(st, "qg", [128, 3], F32)
            kvg, kvg_b = self.sb(st, "kvg", [128, 2], F32)
            gcol, gcol_b = self.sb(st, "mgcol", [128, 8], F32)
            self.dma(qg[:, :], d["mla_qg"].ap()[j], qg_b, w=[qg_b])
            self.dma(kvg[:, :], d["mla_kvg"].ap()[j], kvg_b, w=[kvg_b])
            self.dma(gcol[:, :], d["norm_g"].ap()[l], gcol_b, w=[gcol_b])
            import os
            stop = os.environ.get("MLA_STOP", "")
            if stop == "rope":
                return
            self.mla_phase1(st, l, j, cqT, cqT_b, ckvT, ckvT_b, krT, krT_b, cc, cc_b, ss, ss_b, gcol, gcol_b)
            if stop == "p1":
                return
            if stop != "skip2":
                self.mla_phase2(st, l, j, cqT, cqT_b, ckvT, ckvT_b, krT, krT_b, c2T, c2T_b, s2T, s2T_b,
                                qg, qg_b, kvg, kvg_b)
        if stop == "p2" or self.part == "A":
            return
        self.mla_phase3(l, j, mode)
        self.x_from_input = False

    def mla_phase1(self, st0, l, j, cqT, cqT_b, ckvT, ckvT_b, krT, krT_b, cc, cc_b, ss, ss_b, gcol, gcol_b):
        d = self.d
        with self.scope() as st:
            Win, Win_b = self.sb(st, "mWin", [128, 8, MLA_IN_W], BF16)
            with self.scope() as wst:
                stg = Ring([self.sb(wst, "mstg", [128, MLA_IN_W], F32) for _ in range(3)])
                self.load_weight_rows(stg, Win, Win_b, d["mla_w_in"].ap()[j], gcol, gcol_b, 8, MLA_IN_W, MLA_IN_W)
            xts = Ring([self.sb(st, "mxt", [128, D], F32) for _ in range(4)])
            xnr = Ring([self.sb(st, "mxn", [128, D], BF16) for _ in range(2)])
            xnTr = Ring([self.sb(st, "mxnT", [128, 8, 512], BF16) for _ in range(2)])
            str_ = Ring([self.sb(st, "mst", [128, 8], F32) for _ in range(4)])
            cqn_r = Ring([self.sb(st, "mcqn", [128, 704], BF16) for _ in range(2)])
            rtmp = Ring([self.sb(st, "mrt", [128, 128], F32) for _ in range(2)])
            gt_r = Ring([self.sb(st, "mgt", [128, 4, 512], BF16) for _ in range(2)])
            pT = Ring([self.ps(st, "mpT", [128, 8, 128], BF16) for _ in range(2)])
            pA = Ring([self.ps(st, "mpA", [128, 512], F32) for _ in range(2)])
            pB = Ring([self.ps(st, "mpB", [128, 512], F32) for _ in range(2)])
            pG = Ring([self.ps(st, "mpG", [128, 512], F32) for _ in range(2)])
            import os
            p1s = os.environ.get("P1_STOP", "")
            if p1s == "w":
                return
            xt_of = {}

            def issue_load(t):
                xt, xt_b = xts.next()
                src, srcb = self.x_src(t)
                self.dma(xt[:, :], src, xt_b, r=srcb, w=[xt_b])
                xt_of[t] = (xt, xt_b)

            issue_load(0)
            issue_load(1)
            for g in range(NG):
                xnT, xnT_b = xnTr.next()
                for tt_ in range(4):
                    t = g * 4 + tt_
                    if t + 2 < NT:
                        issue_load(t + 2)
                    xt, xt_b = xt_of.pop(t)
                    xn, xn_b = xnr.next()
                    s4, s4_b = str_.next()
                    self.rms_rstd(xt[:, :], xt_b, xn[:, :], xn_b, s4[:, 0:1], s4_b, s4[:, 1:2], s4_b, D,
                                  self.epsn[:, 0:1], self.epsn_b)
                    self.ts("dve", xn[:, :], xt[:, :], s4[:, 1:2], None, ALU.mult, None, r=[xt_b, s4_b], w=[xn_b])
                    p, p_b = pT.next()
                    for kc in range(8):
                        self.tr(p[:, kc, :], xn[:, kc * 128:(kc + 1) * 128], self.ident[:, :],
                                r=[xn_b, self.ident_b], w=[p_b])
                    self.cp("dve", xnT[:, :, tt_ * 128:(tt_ + 1) * 128], p[:, :, :], r=[p_b], w=[xnT_b])
                if p1s == "n":
                    return
                for tt_ in range(4):
                    t = g * 4 + tt_
                    a, a_b = pA.next()
                    b, b_b = pB.next()
                    for kc in range(8):
                        self.mm(a[:, 0:384], xnT[:, kc, tt_ * 128:(tt_ + 1) * 128], Win[:, kc, 0:384], kc == 0, kc == 7,
                                r=[xnT_b, Win_b], w=[a_b])
                    for kc in range(8):
                        self.mm(b[:, 0:320], xnT[:, kc, tt_ * 128:(tt_ + 1) * 128], Win[:, kc, 384:704], kc == 0, kc == 7,
                                r=[xnT_b, Win_b], w=[b_b])
                    s4, s4_b = str_.next()
                    cqn, cqn_b = cqn_r.next()
                    rt, rt_b = rtmp.next()
                    self.act(cqn[:, 0:384], a[:, 0:384], AF.Square, r=[a_b], w=[cqn_b, s4_b],
                             scale=float(Q_LORA ** -0.5), accum_out=s4[:, 0:1])
                    self.act(cqn[:, 384:640], b[:, 0:256], AF.Square, r=[b_b], w=[cqn_b, s4_b],
                             scale=float(KV_LORA ** -0.5), accum_out=s4[:, 1:2])
                    self.act(s4[:, 2:4], s4[:, 0:2], AF.Sqrt, r=[s4_b, self.epsn_b], w=[s4_b], bias=self.epsn[:, 0:1])
                    self.sc.add("dve", lambda e, s4=s4: e.reciprocal(s4[:, 4:6], s4[:, 2:4]), r=[s4_b], w=[s4_b])
                    self.ts("dve", cqn[:, 0:384], a[:, 0:384], s4[:, 4:5], None, ALU.mult, None,
                            r=[a_b, s4_b], w=[cqn_b])
                    self.ts("dve", cqn[:, 384:640], b[:, 0:256], s4[:, 5:6], None, ALU.mult, None,
                            r=[b_b, s4_b], w=[cqn_b])
                    self.tt("dve", rt[:, 0:64], b[:, 256:320], cc[:, t, :], ALU.mult, r=[b_b, cc_b], w=[rt_b])
                    self.tt("dve", rt[:, 64:128], b[:, 256:320], ss[:, t, :], ALU.mult, r=[b_b, ss_b], w=[rt_b])
                    self.tt("dve", cqn[:, 640:672], rt[:, 0:32], rt[:, 96:128], ALU.subtract, r=[rt_b], w=[cqn_b])
                    self.tt("dve", cqn[:, 672:704], rt[:, 32:64], rt[:, 64:96], ALU.add, r=[rt_b], w=[cqn_b])
                    p, p_b = pT.next()
                    for c in range(5):
                        self.tr(p[:, c, :], cqn[:, c * 128:(c + 1) * 128], self.ident[:, :],
                                r=[cqn_b, self.ident_b], w=[p_b])
                    self.tr(p[0:64, 5, :], cqn[:, 640:704], self.ident[:, :], r=[cqn_b, self.ident_b], w=[p_b])
                    self.cp("act", cqT[:, :, t * 128:(t + 1) * 128], p[:, 0:3, :], r=[p_b], w=[cqT_b])
                    self.cp("dve", ckvT[:, :, t * 128:(t + 1) * 128], p[:, 3:5, :], r=[p_b], w=[ckvT_b])
                    self.cp("dve", krT[0:64, t * 128:(t + 1) * 128], p[0:64, 5, :], r=[p_b], w=[krT_b])
                if p1s == "l":
                    return
                for h4 in range(4):
                    gt, gt_b = gt_r.next()
                    for hh in range(4):
                        h = h4 * 4 + hh
                        pg, pg_b = pG.next()
                        c0 = 704 + h * 128
                        for kc in range(8):
                            self.mm(pg[:, :], Win[:, kc, c0:c0 + 128], xnT[:, kc, :], kc == 0, kc == 7,
                                    r=[Win_b, xnT_b], w=[pg_b])
                        self.act(gt[:, hh, :], pg[:, :], AF.Silu, r=[pg_b], w=[gt_b])
                    self.dma(self.gateT.ap()[h4 * 4:(h4 + 1) * 4, :, g * 512:(g + 1) * 512].rearrange("h p n -> p h n"),
                             gt[:, :, :], gt_b, r=[gt_b], w=[self.gate_b[h4][g]])

    def mla_phase2(self, st0, l, j, cqT, cqT_b, ckvT, ckvT_b, krT, krT_b, c2T, c2T_b, s2T, s2T_b,
                   qg, qg_b, kvg, kvg_b):
        d = self.d
        with self.scope() as st:
            ones, ones_b = self.sb(st, "ones", [128, 128], F32)
            self.sc.add("pool", lambda e: e.memset(ones[:, :], 1.0), w=[ones_b])
            accD_r = Ring([self.sb(st, "accD", [128, 512], F32) for _ in range(2)])
            accP_r = Ring([self.sb(st, "accP", [128, 512], F32) for _ in range(2)])
            hb = Ring([(self.sb(st, "KT", [128, S], BF16), self.sb(st, "V", [128, NT, 128], BF16),
                        self.sb(st, "QT", [128, S], BF16), self.sb(st, "QrT", [64, S], BF16)) for _ in range(2)])
            wq_r = Ring([self.sb(st, "wq", [128, 3, 256], BF16) for _ in range(2)])
            wkv_r = Ring([self.sb(st, "wkv", [128, 2, 256], BF16) for _ in range(2)])
            sq_r = Ring([self.sb(st, "sq", [128, 3, 192], F32) for _ in range(2)])
            skv_r = Ring([self.sb(st, "skv", [128, 2, 256], F32) for _ in range(2)])
            pt_r = Ring([self.sb(st, "pt", [128, 512], BF16) for _ in range(4)])
            rtm = Ring([self.sb(st, "rtm", [64, 2, 512], F32) for _ in range(2)])
            rc_r = Ring([self.sb(st, "rc", [128, 512], F32) for _ in range(2)])
            gl_r = Ring([self.sb(st, "gl", [128, 512], BF16) for _ in range(2)])
            og_r = Ring([self.sb(st, "ogs", [128, 512], BF16) for _ in range(2)])
            pw = Ring([self.ps(st, "pw", [128, 512], F32) for _ in range(4)])
            po_r = Ring([self.ps(st, "po", [128, 512], F32) for _ in range(2)])
            prs_r = Ring([self.ps(st, "prs", [128, 512], F32) for _ in range(2)])

            def prep(h):
                (KT, KT_b), (V, V_b), (QT, QT_b), (QrT, QrT_b) = hbufs[h]
                wq, wq_b = wq_r.next()
                wkv, wkv_b = wkv_r.next()
                sq, sq_b = sq_r.next()
                skv, skv_b = skv_r.next()
                self.dma(sq[:, :, :], d["mla_w_uq"].ap()[j][:, h * 192:(h + 1) * 192].rearrange("(c p) n -> p c n", p=128),
                         sq_b, w=[sq_b])
                self.dma(skv[:, :, :], d["mla_w_ukv"].ap()[j][:, h * 256:(h + 1) * 256].rearrange("(c p) n -> p c n", p=128),
                         skv_b, w=[skv_b])
                for c in range(3):
                    self.ts("pool", wq[:, c, 0:192], sq[:, c, :], qg[:, c:c + 1], None, ALU.mult, None,
                            r=[sq_b, qg_b], w=[wq_b])
                    self.ts("pool", wq[:, c, 192:224], sq[:, c, 160:192], qg[:, c:c + 1], -1.0, ALU.mult, ALU.mult,
                            r=[sq_b, qg_b], w=[wq_b])
                    self.ts("pool", wq[:, c, 224:256], sq[:, c, 128:160], qg[:, c:c + 1], None, ALU.mult, None,
                            r=[sq_b, qg_b], w=[wq_b])
                for c in range(2):
                    self.ts("pool", wkv[:, c, :], skv[:, c, :], kvg[:, c:c + 1], None, ALU.mult, None,
                            r=[skv_b, kvg_b], w=[wkv_b])
                for g in range(NG):
                    cs = slice(g * 512, (g + 1) * 512)
                    p1, p1_b = pw.next()
                    for c in range(3):
                        self.mm(p1[:, :], wq[:, c, 0:128], cqT[:, c, cs], c == 0, c == 2, r=[wq_b, cqT_b], w=[p1_b])
                    self.cp("act", QT[:, cs], p1[:, :], r=[p1_b], w=[QT_b])
                    p2, p2_b = pw.next()
                    for c in range(3):
                        self.mm(p2[0:64, :], wq[:, c, 128:192], cqT[:, c, cs], c == 0, c == 2, r=[wq_b, cqT_b], w=[p2_b])
                    p3, p3_b = pw.next()
                    for c in range(3):
                        self.mm(p3[0:64, :], wq[:, c, 192:256], cqT[:, c, cs], c == 0, c == 2, r=[wq_b, cqT_b], w=[p3_b])
                    rt, rt_b = rtm.next()
                    self.tt("dve", rt[:, 0, :], p2[0:64, :], c2T[:, cs], ALU.mult, r=[p2_b, c2T_b], w=[rt_b])
                    self.tt("dve", rt[:, 1, :], p3[0:64, :], s2T[:, cs], ALU.mult, r=[p3_b, s2T_b], w=[rt_b])
                    self.tt("pool", QrT[:, cs], rt[:, 0, :], rt[:, 1, :], ALU.add, r=[rt_b], w=[QrT_b])
                    p4, p4_b = pw.next()
                    for c in range(2):
                        self.mm(p4[:, :], wkv[:, c, 0:128], ckvT[:, c, cs], c == 0, c == 1, r=[wkv_b, ckvT_b], w=[p4_b])
                    self.cp("act", KT[:, cs], p4[:, :], r=[p4_b], w=[KT_b])
                    p5, p5_b = pw.next()
                    for tt_ in range(4):
                        t = g * 4 + tt_
                        for c in range(2):
                            self.mm(p5[:, tt_ * 128:(tt_ + 1) * 128], ckvT[:, c, t * 128:(t + 1) * 128],
                                    wkv[:, c, 128:256], c == 0, c == 1, r=[ckvT_b, wkv_b], w=[p5_b])
                    self.cp("dve", V[:, g * 4:(g + 1) * 4, :].rearrange("p t d -> p (t d)"), p5[:, :],
                            r=[p5_b], w=[V_b])

            def attention(h):
                (KT, KT_b), (V, V_b), (QT, QT_b), (QrT, QrT_b) = hbufs[h]
                blocks = []
                for jq in range(NG):
                    nkb = 4 * jq + 4
                    for kb in range(nkb):
                        blocks.append((jq, kb, nkb))
                LA = 2
                acc = {}
                info = {}
                for idx in range(len(blocks) + LA):
                    if idx < len(blocks):
                        jq, kb, nkb = blocks[idx]
                        i = kb - 4 * jq
                        off = 128 * i if i > 0 else 0
                        qs = slice(jq * 512 + off, (jq + 1) * 512)
                        ks = slice(kb * 128, (kb + 1) * 128)
                        sp_, sp_b = pw.next()
                        self.mm(sp_[:, off:512], KT[:, ks], QT[:, qs], True, False, r=[KT_b, QT_b], w=[sp_b])
                        self.mm(sp_[:, off:512], krT[0:64, ks], QrT[0:64, qs], False, True, r=[krT_b, QrT_b], w=[sp_b])
                        pt, pt_b = pt_r.next()
                        self.act(pt[:, off:512], sp_[:, off:512], AF.Exp, r=[sp_b], w=[pt_b], scale=float(ATTN_SCALE))
                        if i >= 0:
                            self.tt("pool", pt[:, off:off + 128], pt[:, off:off + 128], self.tri[:, :], ALU.mult,
                                    r=[pt_b, self.tri_b], w=[pt_b])
                        info[idx] = (pt, pt_b, off)
                    k = idx - LA
                    if k >= 0:
                        jq, kb, nkb = blocks[k]
                        pt, pt_b, off = info.pop(k)
                        if kb == 0:
                            acc[jq] = (po_r.next(), prs_r.next(), accD_r.next(), accP_r.next())
                            aD, aD_b = acc[jq][2]
                            aP, aP_b = acc[jq][3]
                            self.sc.add("dve", lambda e, aD=aD: e.memset(aD[:, :], 0.0), w=[aD_b])
                            self.sc.add("pool", lambda e, aP=aP: e.memset(aP[:, :], 0.0), w=[aP_b])
                        (po, po_b), (prs, prs_b), (aD, aD_b), (aP, aP_b) = acc[jq]
                        first = (kb == 0)
                        lastb = (kb == nkb - 1)
                        self.mm(po[:, off:512], V[:, kb, :], pt[:, off:512], first, lastb, r=[V_b, pt_b], w=[po_b])
                        if kb % 2 == 0:
                            self.tt("dve", aD[:, off:512], aD[:, off:512], pt[:, off:512], ALU.add,
                                    r=[aD_b, pt_b], w=[aD_b])
                        else:
                            self.tt("pool", aP[:, off:512], aP[:, off:512], pt[:, off:512], ALU.add,
                                    r=[aP_b, pt_b], w=[aP_b])
                        if lastb:
                            self.mm(prs[:, :], ones[:, :], aD[:, :], True, False, r=[ones_b, aD_b], w=[prs_b])
                            self.mm(prs[:, :], ones[:, :], aP[:, :], False, True, r=[ones_b, aP_b], w=[prs_b])
                        if lastb:
                            rc, rc_b = rc_r.next()
                            gl, gl_b = gl_r.next()
                            og, og_b = og_r.next()
                            cs = slice(jq * 512, (jq + 1) * 512)
                            self.dma(gl[:, :], self.gateT.ap()[h, :, cs], gl_b, r=[self.gate_b[h // 4][jq]], w=[gl_b])
                            self.sc.add("dve", lambda e, rc=rc, prs=prs: e.reciprocal(rc[:, :], prs[:, :]),
                                        r=[prs_b], w=[rc_b])
                            self.tt("dve", rc[:, :], po[:, :], rc[:, :], ALU.mult, r=[po_b, rc_b], w=[rc_b])
                            self.tt("pool", og[:, :], rc[:, :], gl[:, :], ALU.mult, r=[rc_b, gl_b], w=[og_b])
                            self.dma(self.ogT.ap()[h, :, cs], og[:, :], og_b, r=[og_b], w=[self.og_b[h][jq]])
                            del acc[jq]

            hbufs = {}
            hbufs[0] = hb.next()
            prep(0)
            for h in range(HEADS):
                if h + 1 < HEADS:
                    hbufs[h + 1] = hb.next()
                    prep(h + 1)
                attention(h)
                del hbufs[h]

    def mla_phase3(self, l, j, mode):
        d = self.d
        final = (mode == 'final')
        with self.scope() as st:
            Wo, Wo_b = self.sb(st, "mWo", [128, 16, D], BF16)
            with self.scope() as wst:
                stg = Ring([self.sb(wst, "mostg", [128, D], F32) for _ in range(3)])
                self.load_weight_rows(stg, Wo, Wo_b, d["mla_w_o"].ap()[j], None, None, 16, D, D)
            if final:
                Gf, Gf_b = self.sb(st, "mGf", [128, D], F32)
                self.dma(Gf[:, :], bass.AP(d["final_g"], 0, [[0, 128], [1, D]]), Gf_b, w=[Gf_b])
                junk, junk_b = self.sb(st, "mjunk", [128, D], BF16)
                str_ = Ring([self.sb(st, "m3st", [128, 8], F32) for _ in range(3)])
            ogr = Ring([self.sb(st, "og3", [128, 16, 512], BF16) for _ in range(2)])
            xts = Ring([self.sb(st, "m3xt", [128, D], F32) for _ in range(3)])
            py = Ring([self.ps(st, "m3py", [128, 512], F32) for _ in range(4)])
            for g in range(NG):
                og, og_b = ogr.next()
                cs = slice(g * 512, (g + 1) * 512)
                for q in range(4):
                    self.dma(og[:, q * 4:(q + 1) * 4, :],
                             self.ogT.ap()[q * 4:(q + 1) * 4, :, cs].rearrange("h p n -> p h n"), og_b,
                             r=[self.og_b[q * 4 + hh][g] for hh in range(4)], w=[og_b])
                for tt_ in range(4):
                    t = g * 4 + tt_
                    xt, xt_b = xts.next()
                    src, srcb = self.x_src(t)
                    self.dma(xt[:, :], src, xt_b, r=srcb, w=[xt_b])
                    for hf in range(2):
                        y, y_b = py.next()
                        for h in range(16):
                            self.mm(y[:, :], og[:, h, tt_ * 128:(tt_ + 1) * 128], Wo[:, h, hf * 512:(hf + 1) * 512],
                                    h == 0, h == 15, r=[og_b, Wo_b], w=[y_b])
                        self.tt("dve", xt[:, hf * 512:(hf + 1) * 512], y[:, :], xt[:, hf * 512:(hf + 1) * 512], ALU.add,
                                r=[y_b, xt_b], w=[xt_b])
                    if final:
                        self.final_norm_store(t, xt, xt_b, junk, junk_b, str_, Gf, Gf_b)
                    elif mode == 'out':
                        self.dma(self.out.ap()[t * 128:(t + 1) * 128, :], xt[:, :], xt_b, r=[xt_b],
                                 w=[self.out_b[t]])
                    else:
                        self.dma(self.xres.ap()[t * 128:(t + 1) * 128, :], xt[:, :], xt_b, r=[xt_b],
                                 w=[self.xres_b[t]])


def _host_inputs(inp, b):
    f32 = np.float32
    pos = np.ascontiguousarray(inp["positions"][b]).astype(np.int32)
    inv_freq = (1.0 / (10000.0 ** (np.arange(0, ROPE, 2, dtype=np.float32) / ROPE))).astype(f32)
    m = {}
    m["x"] = np.ascontiguousarray(inp["x"][b], dtype=f32)
    m["pos_tok"] = np.ascontiguousarray(pos.reshape(NT, 128).T)
    m["pos_row"] = np.ascontiguousarray(pos.reshape(1, S))
    m["invf_tok"] = np.ascontiguousarray(np.broadcast_to(np.concatenate([inv_freq, inv_freq])[None, :], (128, 64)), dtype=f32)
    m["invf_col"] = np.ascontiguousarray(np.concatenate([inv_freq, inv_freq]).reshape(64, 1), dtype=f32)
    m["ident"] = np.eye(128, dtype=f32).astype(ml_dtypes.bfloat16)
    m["tri"] = np.triu(np.ones((128, 128), dtype=f32)).astype(ml_dtypes.bfloat16)
    m["norm_g"] = np.ascontiguousarray(np.asarray(inp["norm_g"], f32).reshape(DEPTH, 8, 128).transpose(0, 2, 1))
    m["final_g"] = np.ascontiguousarray(np.asarray(inp["final_g"], f32).reshape(1, D))
    m["mla_w_in"] = np.ascontiguousarray(inp["mla_w_in"], dtype=f32)
    m["mla_qg"] = np.ascontiguousarray(np.asarray(inp["mla_q_norm_g"], f32).reshape(2, 3, 128).transpose(0, 2, 1))
    m["mla_kvg"] = np.ascontiguousarray(np.asarray(inp["mla_kv_norm_g"], f32).reshape(2, 2, 128).transpose(0, 2, 1))
    m["mla_w_uq"] = np.ascontiguousarray(inp["mla_w_uq"], dtype=f32)
    m["mla_w_ukv"] = np.ascontiguousarray(inp["mla_w_ukv"], dtype=f32)
    m["mla_w_o"] = np.ascontiguousarray(inp["mla_w_o"], dtype=f32)
    m["sgu_w_in"] = np.ascontiguousarray(inp["sgu_w_in"], dtype=f32)
    m["sgu_ln_g"] = np.ascontiguousarray(np.asarray(inp["sgu_ln_g"], f32).reshape(2, 1, BW))
    m["sgu_ln_b"] = np.ascontiguousarray(np.asarray(inp["sgu_ln_b"], f32).reshape(2, 1, BW))
    m["sgu_w_sT"] = np.ascontiguousarray(np.asarray(inp["sgu_w_s"], f32).transpose(0, 3, 1, 2))
    m["sgu_b_sT"] = np.ascontiguousarray(np.asarray(inp["sgu_b_s"], f32).transpose(0, 2, 1))
    m["sgu_w_o"] = np.ascontiguousarray(inp["sgu_w_o"], dtype=f32)
    return m


_CACHE = {}
_LAYERED = ["mla_w_in", "mla_qg", "mla_kvg", "mla_w_uq", "mla_w_ukv", "mla_w_o", "sgu_w_in", "sgu_ln_g", "sgu_ln_b",
            "sgu_w_sT", "sgu_b_sT", "sgu_w_o"]


def get_prog(layers, first_from_x=True, do_final=True, part=None, slot=False):
    key = (tuple(layers), first_from_x, do_final, part, slot)
    if key not in _CACHE:
        p = Prog(list(layers), first_from_x, do_final, part, slot)
        p.build()
        _CACHE[key] = p
    return _CACHE[key]


def _slot_map(m, l):
    j = l // 2
    m2 = dict(m)
    for k in _LAYERED:
        m2[k] = np.ascontiguousarray(m[k][j:j + 1])
    m2["norm_g"] = np.ascontiguousarray(m["norm_g"][l:l + 1])
    return m2


def _launch(prog, maps):
    res = run_bass_kernel_spmd(prog.nc, maps, core_ids=list(range(len(maps))))
    return res.results


def kernel_unfused(**inputs):
    inp = {k: np.asarray(v) for k, v in inputs.items()}
    B = inp["x"].shape[0]
    base = _host_inputs(inp, 0)
    per_core = []
    for b in range(B):
        m = dict(base)
        pos = np.ascontiguousarray(inp["positions"][b]).astype(np.int32)
        m["pos_tok"] = np.ascontiguousarray(pos.reshape(NT, 128).T)
        m["pos_row"] = np.ascontiguousarray(pos.reshape(1, S))
        per_core.append(m)
    xs = [np.ascontiguousarray(inp["x"][b], dtype=np.float32) for b in range(B)]
    for l in range(DEPTH):
        last = (l == DEPTH - 1)
        if l % 2 == 0:
            pa = get_prog([l % 2], True, False, "A", True)
            maps = [dict(_slot_map(per_core[b], l), x=xs[b]) for b in range(B)]
            ra = _launch(pa, maps)
            pb = get_prog([l % 2], True, last, "B", True)
            maps = [dict(maps[b], ogT=np.asarray(ra[b]["ogT"])) for b in range(B)]
            rb = _launch(pb, maps)
            xs = [np.asarray(rb[b]["out"], dtype=np.float32) for b in range(B)]
        else:
            ps_ = get_prog([l % 2], True, last, None, True)
            maps = [dict(_slot_map(per_core[b], l), x=xs[b]) for b in range(B)]
            r = _launch(ps_, maps)
            xs = [np.asarray(r[b]["out"], dtype=np.float32) for b in range(B)]
    return np.stack(xs, axis=0)


def kernel(**inputs):
    inp = {k: np.asarray(v) for k, v in inputs.items()}
    B = inp["x"].shape[0]
    base = _host_inputs(inp, 0)
    maps = []
    for b in range(B):
        m = dict(base)
        pos = np.ascontiguousarray(inp["positions"][b]).astype(np.int32)
        m["pos_tok"] = np.ascontiguousarray(pos.reshape(NT, 128).T)
        m["pos_row"] = np.ascontiguousarray(pos.reshape(1, S))
        m["x"] = np.ascontiguousarray(inp["x"][b], dtype=np.float32)
        maps.append(m)
    prog = get_prog(range(DEPTH), True, True, None, False)
    res = _launch(prog, maps)
    return np.stack([np.asarray(res[b]["out"], dtype=np.float32) for b in range(B)], axis=0)
```

```python
import math
from contextlib import ExitStack, contextmanager

import numpy as np
import ml_dtypes

import concourse.bass as bass
import concourse.mybir as mybir
from concourse.bass_utils import run_bass_kernel_spmd

F32 = mybir.dt.float32
BF16 = mybir.dt.bfloat16
I32 = mybir.dt.int32
AF = mybir.ActivationFunctionType
ALU = mybir.AluOpType

S = 4096
D = 1024
NT = S // 128
NG = S // 512
DEPTH = 4
HEADS = 16
BW = 2048
Q_LORA = 384
KV_LORA = 256
ROPE = 64
MLA_IN_W = Q_LORA + KV_LORA + ROPE + BW
SGU_IN_W = 3 * BW
EPS = 1e-6
LN_EPS = 1e-5
ATTN_SCALE = (128 + 64) ** -0.5
TWO_PI = 2.0 * math.pi


class Buf:
    __slots__ = ("name", "lw", "rd", "psum")

    def __init__(self, name, psum=False):
        self.name = name
        self.lw = None
        self.rd = []
        self.psum = psum


class Sched:
    STREAMS = ("pe", "act", "dve", "pool", "sp")

    def __init__(self):
        self.ops = []

    def add(self, eng, fn, r=(), w=(), dma=None):
        self.ops.append((eng, fn, tuple(r), tuple(w), dma))

    def barrier(self):
        self.ops.append(("bar", None, (), (), None))

    def emit(self, nc, stack):
        ops = self.ops
        n = len(ops)
        deps = [None] * n
        last_dma = {}
        last_op = {}
        pending = {}
        for i, (eng, fn, r, w, dma) in enumerate(ops):
            if eng == "bar":
                deps[i] = []
                snap = set(last_op.values()) | set(last_dma.values())
                for s_ in self.STREAMS:
                    pending[s_] = snap
                continue
            kinds = {}
            for b in r:
                if b.lw is not None:
                    kinds.setdefault(b.lw, set()).add("raw")
                if b.psum:
                    for j in b.rd:
                        if ops[j][0] != eng:
                            kinds.setdefault(j, set()).add("rar")
            for b in w:
                if b.lw is not None:
                    kinds.setdefault(b.lw, set()).add("waw")
                for j in b.rd:
                    kinds.setdefault(j, set()).add("war")
            if dma is not None and dma in last_dma:
                kinds.setdefault(last_dma[dma], set()).add("raw")
            dd = []
            for j, ks in kinds.items():
                if j == i:
                    continue
                ej, dj = ops[j][0], ops[j][4]
                if dj is None and dma is None and ej == eng:
                    if eng == "pe" or "raw" not in ks:
                        continue
                dd.append(j)
            if eng in pending:
                for j in pending.pop(eng):
                    if ops[j][0] == eng and ops[j][4] is None:
                        continue
                    dd.append(j)
            if fn is not None:
                last_op[eng] = i
            best = {}
            for j in dd:
                k_ = ("d", ops[j][4]) if ops[j][4] is not None else ("e", ops[j][0])
                if k_ not in best or best[k_] < j:
                    best[k_] = j
            dd = list(best.values())
            deps[i] = dd
            for b in r:
                b.rd.append(i)
            for b in w:
                b.lw = i
                b.rd = []
            if dma is not None:
                last_dma[dma] = i
        needs = [False] * n
        for dd in deps:
            for j in dd:
                needs[j] = True
        sems = {}

        def get_sem(key):
            if key not in sems:
                sems[key] = stack.enter_context(nc.semaphore("s_%s" % key))
            return sems[key]

        token = [None] * n
        tick = {}
        for i, (eng, fn, r, w, dma) in enumerate(ops):
            if fn is None:
                continue
            if dma is not None:
                key = "d_" + dma.name
                tick[key] = tick.get(key, 0) + 16
                token[i] = (key, tick[key], 16)
            elif needs[i]:
                key = "e_" + eng
                tick[key] = tick.get(key, 0) + 1
                token[i] = (key, tick[key], 1)
        for key in tick:
            get_sem(key)
        self.n_sems = len(sems)
        self.max_tick = max(tick.values()) if tick else 0
        by_stream = {s: [] for s in self.STREAMS}
        for i, op in enumerate(ops):
            if op[0] != "bar":
                by_stream[op[0]].append(i)

        def simulate():
            val = {k: 0 for k in sems}
            pos = {s_: 0 for s_ in self.STREAMS}
            progress = True
            while progress:
                progress = False
                for s_ in self.STREAMS:
                    lst = by_stream[s_]
                    while pos[s_] < len(lst):
                        i = lst[pos[s_]]
                        ok = all(val[token[j][0]] >= token[j][1] for j in deps[i])
                        if not ok:
                            break
                        if token[i] is not None:
                            val[token[i][0]] += token[i][2]
                        pos[s_] += 1
                        progress = True
            stuck = {s_: (pos[s_], len(by_stream[s_])) for s_ in self.STREAMS if pos[s_] < len(by_stream[s_])}
            for s_, (p_, n_) in stuck.items():
                i = by_stream[s_][p_]
                print("STUCK", s_, p_, n_, "op", i, [(j, ops[j][0], token[j], val[token[j][0]]) for j in deps[i]
                                                      if val[token[j][0]] < token[j][1]])
            return not stuck

        self.sim_ok = simulate()

        def run_stream(e, stream):
            waited = {}
            for i in by_stream[stream]:
                eng, fn, r, w, dma = ops[i]
                need = {}
                for j in deps[i]:
                    key, val, _ = token[j]
                    if need.get(key, 0) < val:
                        need[key] = val
                for key, val in need.items():
                    if waited.get(key, 0) < val:
                        e.wait_ge(sems[key], val)
                        waited[key] = val
                if fn is None:
                    continue
                inst = fn(e)
                if token[i] is not None:
                    key, val, inc = token[i]
                    inst.then_inc(sems[key], inc)

        with nc.Block() as block:
            @block.tensor
            def _(e):
                run_stream(e, "pe")

            @block.scalar
            def _(e):
                run_stream(e, "act")

            @block.vector
            def _(e):
                run_stream(e, "dve")

            @block.gpsimd
            def _(e):
                run_stream(e, "pool")

            @block.sync
            def _(e):
                run_stream(e, "sp")


class Ring:
    def __init__(self, items):
        self.items = items
        self.i = 0

    def next(self):
        it = self.items[self.i % len(self.items)]
        self.i += 1
        return it


class Prog:
    def __init__(self, layers, first_from_x=True, do_final=True, part=None, slot=False):
        self.part = part
        self.slot = slot
        self.layers = layers
        self.do_final = do_final
        self.first_from_x = first_from_x
        self.nc = bass.Bass("TRN2", target_bir_lowering=False)
        self.sc = Sched()
        self.uid = 0

    @contextmanager
    def scope(self):
        with ExitStack() as st:
            yield st
        self.sc.barrier()

    def dram(self, name, shape, dt, kind):
        return self.nc.dram_tensor(name, list(shape), dt, kind=kind)

    def sb(self, st, name, shape, dt):
        self.uid += 1
        t = st.enter_context(self.nc.sbuf_tensor("%s_%d" % (name, self.uid), list(shape), dt))
        return t, Buf("%s_%d" % (name, self.uid))

    def ps(self, st, name, shape, dt):
        self.uid += 1
        t = st.enter_context(self.nc.psum_tensor("%s_%d" % (name, self.uid), list(shape), dt))
        return t, Buf("%s_%d" % (name, self.uid), psum=True)

    def dma(self, out, in_, key, r=(), w=(), eng="sp"):
        self.sc.add(eng, lambda e: e.dma_start(out=out, in_=in_), r=r, w=w, dma=key)

    def mm(self, out, lhsT, rhs, start, stop, r, w):
        self.sc.add("pe", lambda e: e.matmul(out, lhsT, rhs, start=start, stop=stop), r=r, w=w)

    def tr(self, out, in_, ident, r, w):
        self.sc.add("pe", lambda e: e.transpose(out, in_, ident), r=r, w=w)

    def act(self, out, in_, func, r, w, bias=None, scale=None, accum_out=None):
        kw = {}
        if bias is not None:
            kw["bias"] = bias
        if scale is not None:
            kw["scale"] = scale
        if accum_out is not None:
            kw["accum_out"] = accum_out
        self.sc.add("act", lambda e: e.activation(out, in_, func, **kw), r=r, w=w)

    def ts(self, eng, out, in0, s1, s2, op0, op1, r, w):
        if op1 is None:
            self.sc.add(eng, lambda e: e.tensor_scalar(out, in0, s1, None, op0), r=r, w=w)
        else:
            self.sc.add(eng, lambda e: e.tensor_scalar(out, in0, s1, s2, op0, op1), r=r, w=w)

    def tt(self, eng, out, in0, in1, op, r, w):
        self.sc.add(eng, lambda e: e.tensor_tensor(out, in0, in1, op), r=r, w=w)

    def stt(self, out, in0, scalar, in1, op0, op1, r, w):
        self.sc.add("dve", lambda e: e.scalar_tensor_tensor(out, in0, scalar, in1, op0, op1), r=r, w=w)

    def cp(self, eng, out, in_, r, w):
        if eng == "act":
            self.sc.add("act", lambda e: e.activation(out, in_, AF.Copy), r=r, w=w)
        else:
            self.sc.add(eng, lambda e: e.tensor_copy(out, in_), r=r, w=w)

    def build(self):
        nc = self.nc
        self.x_in = self.dram("x", [S, D], F32, "ExternalInput")
        self.out = self.dram("out", [S, D], F32, "ExternalOutput")
        self.xres = self.dram("xres", [S, D], F32, "Internal")
        self.gateT = self.dram("gateT", [HEADS, 128, S], BF16, "Internal")
        og_kind = {None: "Internal", "A": "ExternalOutput", "B": "ExternalInput"}[self.part]
        self.ogT = self.dram("ogT", [HEADS, 128, S], BF16, og_kind)
        N2 = 1 if self.slot else 2
        N4 = 1 if self.slot else DEPTH
        d = {}
        d["pos_tok"] = self.dram("pos_tok", [128, NT], I32, "ExternalInput")
        d["pos_row"] = self.dram("pos_row", [1, S], I32, "ExternalInput")
        d["invf_tok"] = self.dram("invf_tok", [128, 64], F32, "ExternalInput")
        d["invf_col"] = self.dram("invf_col", [64, 1], F32, "ExternalInput")
        d["ident"] = self.dram("ident", [128, 128], BF16, "ExternalInput")
        d["tri"] = self.dram("tri", [128, 128], BF16, "ExternalInput")
        d["norm_g"] = self.dram("norm_g", [N4, 128, 8], F32, "ExternalInput")
        d["final_g"] = self.dram("final_g", [1, D], F32, "ExternalInput")
        d["mla_w_in"] = self.dram("mla_w_in", [N2, D, MLA_IN_W], F32, "ExternalInput")
        d["mla_qg"] = self.dram("mla_qg", [N2, 128, 3], F32, "ExternalInput")
        d["mla_kvg"] = self.dram("mla_kvg", [N2, 128, 2], F32, "ExternalInput")
        d["mla_w_uq"] = self.dram("mla_w_uq", [N2, Q_LORA, HEADS * 192], F32, "ExternalInput")
        d["mla_w_ukv"] = self.dram("mla_w_ukv", [N2, KV_LORA, HEADS * 256], F32, "ExternalInput")
        d["mla_w_o"] = self.dram("mla_w_o", [N2, BW, D], F32, "ExternalInput")
        d["sgu_w_in"] = self.dram("sgu_w_in", [N2, D, SGU_IN_W], F32, "ExternalInput")
        d["sgu_ln_g"] = self.dram("sgu_ln_g", [N2, 1, BW], F32, "ExternalInput")
        d["sgu_ln_b"] = self.dram("sgu_ln_b", [N2, 1, BW], F32, "ExternalInput")
        d["sgu_w_sT"] = self.dram("sgu_w_sT", [N2, 128, 16, 128], F32, "ExternalInput")
        d["sgu_b_sT"] = self.dram("sgu_b_sT", [N2, 128, 16], F32, "ExternalInput")
        d["sgu_w_o"] = self.dram("sgu_w_o", [N2, BW, D], F32, "ExternalInput")
        self.d = d
        self.xres_b = [Buf("xres%d" % t) for t in range(NT)]
        self.out_b = [Buf("out%d" % t) for t in range(NT)]
        self.gate_b = [[Buf("gate%d_%d" % (h4, g)) for g in range(NG)] for h4 in range(4)]
        self.og_b = [[Buf("og%d_%d" % (h, g)) for g in range(NG)] for h in range(HEADS)]
        self.x_from_input = self.first_from_x

        with ExitStack() as top:
            self.ident, self.ident_b = self.sb(top, "ident", [128, 128], BF16)
            self.tri, self.tri_b = self.sb(top, "tri", [128, 128], BF16)
            self.epsn, self.epsn_b = self.sb(top, "epsn", [128, 1], F32)
            self.epsl, self.epsl_b = self.sb(top, "epsl", [128, 1], F32)
            self.dma(self.ident[:, :], d["ident"].ap(), self.ident_b, w=[self.ident_b])
            self.dma(self.tri[:, :], d["tri"].ap(), self.tri_b, w=[self.tri_b])
            self.sc.add("pool", lambda e: e.memset(self.epsn[:, :], EPS), w=[self.epsn_b])
            self.sc.add("pool", lambda e: e.memset(self.epsl[:, :], LN_EPS), w=[self.epsl_b])
            for li, l in enumerate(self.layers):
                last = (li == len(self.layers) - 1)
                mode = "res" if not last else ("final" if self.do_final else "out")
                ll, jj = (0, 0) if self.slot else (l, l // 2)
                if l % 2 == 0:
                    self.mla_layer(ll, jj, mode)
                else:
                    self.sgu_layer(ll, jj, mode)
            self.sc.add("sp", None, r=self.out_b)
            if self.part == "A":
                self.sc.add("sp", None, r=[b for row in self.og_b for b in row])
            self.sc.emit(nc, top)
        return nc

    def x_src(self, t):
        if self.x_from_input:
            return self.x_in.ap()[t * 128:(t + 1) * 128, :], []
        return self.xres.ap()[t * 128:(t + 1) * 128, :], [self.xres_b[t]]

    def rms_rstd(self, xt, xt_b, junk, junk_b, ss, ss_b, rstd, rstd_b, ncols, eps_t, eps_b):
        self.act(junk, xt, AF.Square, r=[xt_b], w=[junk_b, ss_b], scale=float(ncols ** -0.5), accum_out=ss)
        self.act(ss, ss, AF.Sqrt, r=[ss_b, eps_b], w=[ss_b], bias=eps_t)
        self.sc.add("dve", lambda e: e.reciprocal(rstd, ss), r=[ss_b], w=[rstd_b])

    def load_weight_rows(self, st_ring, dst, dst_b, src_ap, gcol, gcol_b, nk, ncols, piece):
        for kc in range(nk):
            for c0 in range(0, ncols, piece):
                c1 = min(ncols, c0 + piece)
                stg, stg_b = st_ring.next()
                self.dma(stg[:, 0:c1 - c0], src_ap[kc * 128:(kc + 1) * 128, c0:c1], stg_b, w=[stg_b])
                self._wl = getattr(self, "_wl", 0) + 1
                eng = "dve" if self._wl % 2 == 0 else "act"
                if gcol is None:
                    self.cp(eng, dst[:, kc, c0:c1], stg[:, 0:c1 - c0], r=[stg_b], w=[dst_b])
                elif eng == "dve":
                    self.ts("dve", dst[:, kc, c0:c1], stg[:, 0:c1 - c0], gcol[:, kc:kc + 1], None,
                            ALU.mult, None, r=[stg_b, gcol_b], w=[dst_b])
                else:
                    self.act(dst[:, kc, c0:c1], stg[:, 0:c1 - c0], AF.Copy, r=[stg_b, gcol_b], w=[dst_b],
                             scale=gcol[:, kc:kc + 1])

    def sgu_layer(self, l, j, mode):
        d = self.d
        final = (mode == 'final')
        with self.scope() as st:
            Win, Win_b = self.sb(st, "sWin", [128, 8, SGU_IN_W], BF16)
            Wo, Wo_b = self.sb(st, "sWo", [128, 16, D], BF16)
            WsT, WsT_b = self.sb(st, "sWsT", [128, 16, 128], BF16)
            Gb, Gb_b = self.sb(st, "sG", [128, BW], F32)
            Bb, Bb_b = self.sb(st, "sB", [128, BW], F32)
            bsT, bsT_b = self.sb(st, "sbs", [128, 16], F32)
            gcol, gcol_b = self.sb(st, "sgcol", [128, 8], F32)
            if final:
                Gf, Gf_b = self.sb(st, "sGf", [128, D], F32)
                self.dma(Gf[:, :], bass.AP(d["final_g"], 0, [[0, 128], [1, D]]), Gf_b, w=[Gf_b])
            self.dma(gcol[:, :], d["norm_g"].ap()[l], gcol_b, w=[gcol_b])
            self.dma(bsT[:, :], d["sgu_b_sT"].ap()[j], bsT_b, w=[bsT_b])
            self.dma(Gb[:, :], bass.AP(d["sgu_ln_g"], j * BW, [[0, 128], [1, BW]]), Gb_b, w=[Gb_b])
            self.dma(Bb[:, :], bass.AP(d["sgu_ln_b"], j * BW, [[0, 128], [1, BW]]), Bb_b, w=[Bb_b])
            with self.scope() as wst:
                stg = Ring([self.sb(wst, "sstg", [128, 2048], F32) for _ in range(4)])
                self.load_weight_rows(stg, Win, Win_b, d["sgu_w_in"].ap()[j], gcol, gcol_b, 8, SGU_IN_W, 2048)
                self.load_weight_rows(stg, Wo, Wo_b, d["sgu_w_o"].ap()[j], None, None, 16, D, 2048)
                sw, sw_b = stg.next()
                self.dma(sw[:, :], d["sgu_w_sT"].ap()[j].rearrange("s g t -> s (g t)"), sw_b, w=[sw_b])
                for g in range(16):
                    self.tt("pool", WsT[:, g, :], sw[:, g * 128:(g + 1) * 128], self.tri[:, :], ALU.mult,
                            r=[sw_b, self.tri_b], w=[WsT_b])

            xts = Ring([self.sb(st, "sxt", [128, D], F32) for _ in range(3)])
            xnr = Ring([self.sb(st, "sxn", [128, D], BF16) for _ in range(1)])
            xnTr = Ring([self.sb(st, "sxnT", [128, 8, 128], BF16) for _ in range(2)])
            gvr = Ring([self.sb(st, "sgv", [128, BW], F32) for _ in range(1)])
            sgr = Ring([self.sb(st, "ssg", [128, BW], BF16) for _ in range(1)])
            ur = Ring([self.sb(st, "su", [128, BW], BF16) for _ in range(2)])
            vlr = Ring([self.sb(st, "svl", [128, BW], BF16) for _ in range(2)])
            prr = Ring([self.sb(st, "spr", [128, BW], BF16) for _ in range(1)])
            prTr = Ring([self.sb(st, "sprT", [128, 16, 128], BF16) for _ in range(1)])
            str_ = Ring([self.sb(st, "sst", [128, 16], F32) for _ in range(3)])
            pT = Ring([self.ps(st, "spT", [128, 8, 128], BF16) for _ in range(2)])
            pm = Ring([self.ps(st, "spm", [128, 512], F32) for _ in range(3)])
            psp = Ring([self.ps(st, "spsp", [128, 512], F32) for _ in range(1)])
            py = [self.ps(st, "spy", [128, 512], F32) for _ in range(2)]

            state = {}
            order = [4, 5, 6, 7, 0, 1, 2, 3, 8, 9, 10, 11]

            def a1a(t):
                xt, xt_b = xts.next()
                xn, xn_b = xnr.next()
                stt_, st_b = str_.next()
                src, srcb = self.x_src(t)
                self.dma(xt[:, :], src, xt_b, r=srcb, w=[xt_b])
                state[t] = dict(xt=(xt, xt_b), xn=(xn, xn_b), st=(stt_, st_b))

            def a1n(t):
                xt, xt_b = state[t]["xt"]
                xn, xn_b = state[t]["xn"]
                stt_, st_b = state[t]["st"]
                self.rms_rstd(xt[:, :], xt_b, xn[:, :], xn_b, stt_[:, 0:1], st_b, stt_[:, 1:2], st_b, D,
                              self.epsn[:, 0:1], self.epsn_b)
                self.ts("dve", xn[:, :], xt[:, :], stt_[:, 1:2], None, ALU.mult, None, r=[xt_b, st_b], w=[xn_b])

            def a1b(t):
                xn, xn_b = state[t]["xn"]
                xnT, xnT_b = xnTr.next()
                p, p_b = pT.next()
                for kc in range(8):
                    self.tr(p[:, kc, :], xn[:, kc * 128:(kc + 1) * 128], self.ident[:, :],
                            r=[xn_b, self.ident_b], w=[p_b])
                self.cp("dve", xnT[:, :, :], p[:, :, :], r=[p_b], w=[xnT_b])
                state[t]["xnT"] = (xnT, xnT_b)
                state[t]["gv"] = gvr.next()
                state[t]["u"] = ur.next()
                state[t]["sg"] = sgr.next()
                state[t]["vl"] = vlr.next()

            def a_group(t, idx):
                stt = state[t]
                xnT, xnT_b = stt["xnT"]
                gv, gv_b = stt["gv"]
                u, u_b = stt["u"]
                sg, sg_b = stt["sg"]
                vl, vl_b = stt["vl"]
                nb = order[idx]
                pp, pp_b = pm.next()
                for kc in range(8):
                    self.mm(pp[:, :], xnT[:, kc, :], Win[:, kc, nb * 512:(nb + 1) * 512], kc == 0, kc == 7,
                            r=[xnT_b, Win_b], w=[pp_b])
                if nb < 4:
                    self.act(u[:, nb * 512:(nb + 1) * 512], pp[:, :], AF.Gelu, r=[pp_b], w=[u_b])
                elif nb < 8:
                    c = nb - 4
                    self.act(gv[:, c * 512:(c + 1) * 512], pp[:, :], AF.Gelu, r=[pp_b], w=[gv_b])
                else:
                    c = nb - 8
                    self.act(sg[:, c * 512:(c + 1) * 512], pp[:, :], AF.Silu, r=[pp_b], w=[sg_b])
                if nb == 7:
                    self.ln_chain(gv, gv_b, vl, vl_b, Gb, Gb_b, Bb, Bb_b, st)
                if idx == 11:
                    self.tt("pool", u[:, :], u[:, :], sg[:, :], ALU.mult, r=[u_b, sg_b], w=[u_b])

            def b1(t, q):
                stt = state[t]
                u, u_b = stt["u"]
                vl, vl_b = stt["vl"]
                if q == 0:
                    stt["pr"] = prr.next()
                    stt["prT"] = prTr.next()
                pr, pr_b = stt["pr"]
                sp, sp_b = psp.next()
                for gg in range(4):
                    g = q * 4 + gg
                    self.mm(sp[:, gg * 128:(gg + 1) * 128], WsT[:, g, :], vl[:, g * 128:(g + 1) * 128],
                            True, True, r=[WsT_b, vl_b], w=[sp_b])
                for gg in range(4):
                    g = q * 4 + gg
                    self.stt(pr[:, g * 128:(g + 1) * 128], sp[:, gg * 128:(gg + 1) * 128], bsT[:, g:g + 1],
                             u[:, g * 128:(g + 1) * 128], ALU.add, ALU.mult,
                             r=[sp_b, bsT_b, u_b], w=[pr_b])

            def b2(t, hf):
                stt = state[t]
                pr, pr_b = stt["pr"]
                prT, prT_b = stt["prT"]
                p, p_b = pT.next()
                for kk in range(8):
                    kc = hf * 8 + kk
                    self.tr(p[:, kk, :], pr[:, kc * 128:(kc + 1) * 128], self.ident[:, :],
                            r=[pr_b, self.ident_b], w=[p_b])
                self.cp("act" if hf == 0 else "dve", prT[:, hf * 8:(hf + 1) * 8, :], p[:, :, :],
                        r=[p_b], w=[prT_b])

            def b3(t, hf):
                stt = state[t]
                xt, xt_b = stt["xt"]
                pr, pr_b = stt["pr"]
                prT, prT_b = stt["prT"]
                y, y_b = py[hf]
                for kc in range(16):
                    self.mm(y[:, :], prT[:, kc, :], Wo[:, kc, hf * 512:(hf + 1) * 512], kc == 0, kc == 15,
                            r=[prT_b, Wo_b], w=[y_b])
                self.tt("dve", xt[:, hf * 512:(hf + 1) * 512], y[:, :], xt[:, hf * 512:(hf + 1) * 512], ALU.add,
                        r=[y_b, xt_b], w=[xt_b])
                if hf == 1:
                    if final:
                        self.final_norm_store(t, xt, xt_b, pr, pr_b, str_, Gf, Gf_b)
                    elif mode == 'out':
                        self.dma(self.out.ap()[t * 128:(t + 1) * 128, :], xt[:, :], xt_b, r=[xt_b], w=[self.out_b[t]])
                    else:
                        self.dma(self.xres.ap()[t * 128:(t + 1) * 128, :], xt[:, :], xt_b, r=[xt_b], w=[self.xres_b[t]])
                    del state[t]

            self._sgu_ln_tmp = Ring([self.sb(st, "slt", [128, 8], F32) for _ in range(2)])
            self._lnst = Ring([self.sb(st, "slnst", [128, 24], F32) for _ in range(2)])
            pieces = {0: ("b1", 0), 1: ("b1", 1), 2: ("b1", 2), 3: ("b1", 3), 4: ("b2", 0), 5: ("b2", 1),
                      8: ("b3", 0), 10: ("b3", 1)}
            a1a(0)
            a1n(0)
            a1b(0)
            for t in range(NT + 1):
                if t + 1 < NT:
                    a1a(t + 1)
                for idx in range(12):
                    if t < NT:
                        a_group(t, idx)
                    if t >= 1 and idx in pieces:
                        kind, arg = pieces[idx]
                        {"b1": b1, "b2": b2, "b3": b3}[kind](t - 1, arg)
                    if idx == 3 and t + 1 < NT:
                        a1n(t + 1)
                    if idx == 7 and t + 1 < NT:
                        a1b(t + 1)
        self.x_from_input = False

    def ln_chain(self, gv, gv_b, vl, vl_b, Gb, Gb_b, Bb, Bb_b, st):
        lt, lt_b = self._sgu_ln_tmp.next()
        stats, stats_b = self._lnst.next()
        for c in range(4):
            self.sc.add("dve", lambda e, c=c: e.bn_stats(stats[:, c * 6:(c + 1) * 6], gv[:, c * 512:(c + 1) * 512]),
                        r=[gv_b], w=[stats_b])
        self.sc.add("dve", lambda e: e.bn_aggr(lt[:, 0:2], stats[:, :]), r=[stats_b], w=[lt_b])
        self.act(lt[:, 2:3], lt[:, 1:2], AF.Sqrt, r=[lt_b, self.epsl_b], w=[lt_b], bias=self.epsl[:, 0:1])
        self.sc.add("dve", lambda e: e.reciprocal(lt[:, 3:4], lt[:, 2:3]), r=[lt_b], w=[lt_b])
        self.ts("dve", gv[:, :], gv[:, :], lt[:, 0:1], lt[:, 3:4], ALU.subtract, ALU.mult, r=[gv_b, lt_b], w=[gv_b])
        self.tt("pool", gv[:, :], gv[:, :], Gb[:, :], ALU.mult, r=[gv_b, Gb_b], w=[gv_b])
        self.tt("pool", vl[:, :], gv[:, :], Bb[:, :], ALU.add, r=[gv_b, Bb_b], w=[vl_b])

    def final_norm_store(self, t, xt, xt_b, junk, junk_b, str_, Gf, Gf_b):
        stt_, st_b = str_.next()
        self.rms_rstd(xt[:, :], xt_b, junk[:, 0:D], junk_b, stt_[:, 0:1], st_b, stt_[:, 1:2], st_b, D,
                      self.epsn[:, 0:1], self.epsn_b)
        self.stt(xt[:, :], xt[:, :], stt_[:, 1:2], Gf[:, :], ALU.mult, ALU.mult, r=[xt_b, st_b, Gf_b], w=[xt_b])
        self.dma(self.out.ap()[t * 128:(t + 1) * 128, :], xt[:, :], xt_b, r=[xt_b], w=[self.out_b[t]])

    def rope_tables(self, st):
        d = self.d
        cc, cc_b = self.sb(st, "cc", [128, NT, 64], F32)
        ss, ss_b = self.sb(st, "ss", [128, NT, 64], F32)
        c2T, c2T_b = self.sb(st, "c2T", [64, S], F32)
        s2T, s2T_b = self.sb(st, "s2T", [64, S], F32)
        with self.scope() as tmp:
            pi_, pi_b = self.sb(tmp, "posi", [128, NT], I32)
            pf, pf_b = self.sb(tmp, "posf", [128, NT], F32)
            ivt, ivt_b = self.sb(tmp, "ivt", [128, 64], F32)
            ang, ang_b = self.sb(tmp, "ang", [128, NT, 64], F32)
            ki, ki_b = self.sb(tmp, "ki", [128, NT * 64], I32)
            kf, kf_b = self.sb(tmp, "kf", [128, NT * 64], F32)
            self.dma(pi_[:, :], d["pos_tok"].ap(), pi_b, w=[pi_b])
            self.dma(ivt[:, :], d["invf_tok"].ap(), ivt_b, w=[ivt_b])
            self.cp("dve", pf[:, :], pi_[:, :], r=[pi_b], w=[pf_b])
            for t in range(NT):
                self.ts("dve", ang[:, t, :], ivt[:, :], pf[:, t:t + 1], None, ALU.mult, None,
                        r=[ivt_b, pf_b], w=[ang_b])
            angf = ang[:, :, :].rearrange("p t f -> p (t f)")
            self.trig(angf, ang_b, ki, ki_b, kf, kf_b, cc[:, :, :].rearrange("p t f -> p (t f)"), cc_b,
                      ss[:, :, :].rearrange("p t f -> p (t f)"), ss_b, 128, NT * 64)
        with self.scope() as tmp:
            pr_, pr_b = self.sb(tmp, "pri", [64, S], I32)
            ivc, ivc_b = self.sb(tmp, "ivc", [64, 1], F32)
            ang2, ang2_b = self.sb(tmp, "ang2", [64, S], F32)
            ki, ki_b = self.sb(tmp, "ki2", [64, S], I32)
            kf, kf_b = self.sb(tmp, "kf2", [64, S], F32)
            self.dma(pr_[:, :], bass.AP(d["pos_row"], 0, [[0, 64], [1, S]]), pr_b, w=[pr_b])
            self.dma(ivc[:, :], d["invf_col"].ap(), ivc_b, w=[ivc_b])
            self.cp("dve", ang2[:, :], pr_[:, :], r=[pr_b], w=[ang2_b])
            self.ts("dve", ang2[:, :], ang2[:, :], ivc[:, 0:1], None, ALU.mult, None, r=[ang2_b, ivc_b], w=[ang2_b])
            self.trig(ang2[:, :], ang2_b, ki, ki_b, kf, kf_b, c2T[:, :], c2T_b, s2T[:, :], s2T_b, 64, S)
        return (cc, cc_b, ss, ss_b, c2T, c2T_b, s2T, s2T_b)

    def trig(self, ang, ang_b, ki, ki_b, kf, kf_b, cosd, cos_b, sind, sin_b, P, N):
        C1 = 6.28125
        C2 = TWO_PI - C1
        PI_LO = 3.1415925
        self.ts("dve", kf[:, :], ang, 1.0 / TWO_PI, None, ALU.mult, None, r=[ang_b], w=[kf_b])
        self.cp("dve", ki[:, :], kf[:, :], r=[kf_b], w=[ki_b])
        self.cp("dve", kf[:, :], ki[:, :], r=[ki_b], w=[kf_b])
        self.stt(ang, kf[:, :], -C1, ang, ALU.mult, ALU.add, r=[kf_b, ang_b], w=[ang_b])
        self.stt(ang, kf[:, :], -C2, ang, ALU.mult, ALU.add, r=[kf_b, ang_b], w=[ang_b])
        self.ts("dve", kf[:, :], ang, math.pi, -TWO_PI, ALU.is_gt, ALU.mult, r=[ang_b], w=[kf_b])
        self.tt("dve", kf[:, :], kf[:, :], ang, ALU.add, r=[kf_b, ang_b], w=[kf_b])
        self.ts("dve", kf[:, :], kf[:, :], PI_LO, -PI_LO, ALU.min, ALU.max, r=[kf_b], w=[kf_b])
        self.act(sind, kf[:, :], AF.Sin, r=[kf_b], w=[sin_b])
        self.ts("dve", kf[:, :], ang, math.pi / 2, -TWO_PI, ALU.is_gt, ALU.mult, r=[ang_b, sin_b], w=[kf_b])
        self.stt(kf[:, :], ang, math.pi / 2, kf[:, :], ALU.add, ALU.add, r=[kf_b, ang_b], w=[kf_b])
        self.ts("dve", kf[:, :], kf[:, :], PI_LO, -PI_LO, ALU.min, ALU.max, r=[kf_b], w=[kf_b])
        self.act(cosd, kf[:, :], AF.Sin, r=[kf_b], w=[cos_b])

    def mla_layer(self, l, j, mode):
        d = self.d
        if self.part == "B":
            self.mla_phase3(l, j, mode)
            self.x_from_input = False
            return
        with self.scope() as st:
            cqT, cqT_b = self.sb(st, "cqT", [128, 3, S], BF16)
            ckvT, ckvT_b = self.sb(st, "ckvT", [128, 2, S], BF16)
            krT, krT_b = self.sb(st, "krT", [64, S], BF16)
            cc, cc_b, ss, ss_b, c2T, c2T_b, s2T, s2T_b = self.rope_tables(st)
            qg, qg_b = self.sb(st, "qg", [128, 3], F32)
            kvg, kvg_b = self.sb(st, "kvg", [128, 2], F32)
            gcol, gcol_b = self.sb(st, "mgcol", [128, 8], F32)
            self.dma(qg[:, :], d["mla_qg"].ap()[j], qg_b, w=[qg_b])
            self.dma(kvg[:, :], d["mla_kvg"].ap()[j], kvg_b, w=[kvg_b])
            self.dma(gcol[:, :], d["norm_g"].ap()[l], gcol_b, w=[gcol_b])
            import os
            stop = os.environ.get("MLA_STOP", "")
            if stop == "rope":
                return
            self.mla_phase1(st, l, j, cqT, cqT_b, ckvT, ckvT_b, krT, krT_b, cc, cc_b, ss, ss_b, gcol, gcol_b)
            if stop == "p1":
                return
            if stop != "skip2":
                self.mla_phase2(st, l, j, cqT, cqT_b, ckvT, ckvT_b, krT, krT_b, c2T, c2T_b, s2T, s2T_b,
                                qg, qg_b, kvg, kvg_b)
        if stop == "p2" or self.part == "A":
            return
        self.mla_phase3(l, j, mode)
        self.x_from_input = False

    def mla_phase1(self, st0, l, j, cqT, cqT_b, ckvT, ckvT_b, krT, krT_b, cc, cc_b, ss, ss_b, gcol, gcol_b):
        d = self.d
        with self.scope() as st:
            Win, Win_b = self.sb(st, "mWin", [128, 8, MLA_IN_W], BF16)
            with self.scope() as wst:
                stg = Ring([self.sb(wst, "mstg", [128, MLA_IN_W], F32) for _ in range(3)])
                self.load_weight_rows(stg, Win, Win_b, d["mla_w_in"].ap()[j], gcol, gcol_b, 8, MLA_IN_W, MLA_IN_W)
            xts = Ring([self.sb(st, "mxt", [128, D], F32) for _ in range(4)])
            xnr = Ring([self.sb(st, "mxn", [128, D], BF16) for _ in range(2)])
            xnTr = Ring([self.sb(st, "mxnT", [128, 8, 512], BF16) for _ in range(2)])
            str_ = Ring([self.sb(st, "mst", [128, 8], F32) for _ in range(4)])
            cqn_r = Ring([self.sb(st, "mcqn", [128, 704], BF16) for _ in range(2)])
            rtmp = Ring([self.sb(st, "mrt", [128, 128], F32) for _ in range(2)])
            gt_r = Ring([self.sb(st, "mgt", [128, 4, 512], BF16) for _ in range(2)])
            pT = Ring([self.ps(st, "mpT", [128, 8, 128], BF16) for _ in range(2)])
            pA = Ring([self.ps(st, "mpA", [128, 512], F32) for _ in range(2)])
            pB = Ring([self.ps(st, "mpB", [128, 512], F32) for _ in range(2)])
            pG = Ring([self.ps(st, "mpG", [128, 512], F32) for _ in range(2)])
            import os
            p1s = os.environ.get("P1_STOP", "")
            if p1s == "w":
                return
            xt_of = {}

            def issue_load(t):
                xt, xt_b = xts.next()
                src, srcb = self.x_src(t)
                self.dma(xt[:, :], src, xt_b, r=srcb, w=[xt_b])
                xt_of[t] = (xt, xt_b)

            issue_load(0)
            issue_load(1)
            for g in range(NG):
                xnT, xnT_b = xnTr.next()
                for tt_ in range(4):
                    t = g * 4 + tt_
                    if t + 2 < NT:
                        issue_load(t + 2)
                    xt, xt_b = xt_of.pop(t)
                    xn, xn_b = xnr.next()
                    s4, s4_b = str_.next()
                    self.rms_rstd(xt[:, :], xt_b, xn[:, :], xn_b, s4[:, 0:1], s4_b, s4[:, 1:2], s4_b, D,
                                  self.epsn[:, 0:1], self.epsn_b)
                    self.ts("dve", xn[:, :], xt[:, :], s4[:, 1:2], None, ALU.mult, None, r=[xt_b, s4_b], w=[xn_b])
                    p, p_b = pT.next()
                    for kc in range(8):
                        self.tr(p[:, kc, :], xn[:, kc * 128:(kc + 1) * 128], self.ident[:, :],
                                r=[xn_b, self.ident_b], w=[p_b])
                    self.cp("dve", xnT[:, :, tt_ * 128:(tt_ + 1) * 128], p[:, :, :], r=[p_b], w=[xnT_b])
                if p1s == "n":
                    return
                for tt_ in range(4):
                    t = g * 4 + tt_
                    a, a_b = pA.next()
                    b, b_b = pB.next()
                    for kc in range(8):
                        self.mm(a[:, 0:384], xnT[:, kc, tt_ * 128:(tt_ + 1) * 128], Win[:, kc, 0:384], kc == 0, kc == 7,
                                r=[xnT_b, Win_b], w=[a_b])
                    for kc in range(8):
                        self.mm(b[:, 0:320], xnT[:, kc, tt_ * 128:(tt_ + 1) * 128], Win[:, kc, 384:704], kc == 0, kc == 7,
                                r=[xnT_b, Win_b], w=[b_b])
                    s4, s4_b = str_.next()
                    cqn, cqn_b = cqn_r.next()
                    rt, rt_b = rtmp.next()
                    self.act(cqn[:, 0:384], a[:, 0:384], AF.Square, r=[a_b], w=[cqn_b, s4_b],
                             scale=float(Q_LORA ** -0.5), accum_out=s4[:, 0:1])
                    self.act(cqn[:, 384:640], b[:, 0:256], AF.Square, r=[b_b], w=[cqn_b, s4_b],
                             scale=float(KV_LORA ** -0.5), accum_out=s4[:, 1:2])
                    self.act(s4[:, 2:4], s4[:, 0:2], AF.Sqrt, r=[s4_b, self.epsn_b], w=[s4_b], bias=self.epsn[:, 0:1])
                    self.sc.add("dve", lambda e, s4=s4: e.reciprocal(s4[:, 4:6], s4[:, 2:4]), r=[s4_b], w=[s4_b])
                    self.ts("dve", cqn[:, 0:384], a[:, 0:384], s4[:, 4:5], None, ALU.mult, None,
                            r=[a_b, s4_b], w=[cqn_b])
                    self.ts("dve", cqn[:, 384:640], b[:, 0:256], s4[:, 5:6], None, ALU.mult, None,
                            r=[b_b, s4_b], w=[cqn_b])
                    self.tt("dve", rt[:, 0:64], b[:, 256:320], cc[:, t, :], ALU.mult, r=[b_b, cc_b], w=[rt_b])
                    self.tt("dve", rt[:, 64:128], b[:, 256:320], ss[:, t, :], ALU.mult, r=[b_b, ss_b], w=[rt_b])
                    self.tt("dve", cqn[:, 640:672], rt[:, 0:32], rt[:, 96:128], ALU.subtract, r=[rt_b], w=[cqn_b])
                    self.tt("dve", cqn[:, 672:704], rt[:, 32:64], rt[:, 64:96], ALU.add, r=[rt_b], w=[cqn_b])
                    p, p_b = pT.next()
                    for c in range(5):
                        self.tr(p[:, c, :], cqn[:, c * 128:(c + 1) * 128], self.ident[:, :],
                                r=[cqn_b, self.ident_b], w=[p_b])
                    self.tr(p[0:64, 5, :], cqn[:, 640:704], self.ident[:, :], r=[cqn_b, self.ident_b], w=[p_b])
                    self.cp("act", cqT[:, :, t * 128:(t + 1) * 128], p[:, 0:3, :], r=[p_b], w=[cqT_b])
                    self.cp("dve", ckvT[:, :, t * 128:(t + 1) * 128], p[:, 3:5, :], r=[p_b], w=[ckvT_b])
                    self.cp("dve", krT[0:64, t * 128:(t + 1) * 128], p[0:64, 5, :], r=[p_b], w=[krT_b])
                if p1s == "l":
                    return
                for h4 in range(4):
                    gt, gt_b = gt_r.next()
                    for hh in range(4):
                        h = h4 * 4 + hh
                        pg, pg_b = pG.next()
                        c0 = 704 + h * 128
                        for kc in range(8):
                            self.mm(pg[:, :], Win[:, kc, c0:c0 + 128], xnT[:, kc, :], kc == 0, kc == 7,
                                    r=[Win_b, xnT_b], w=[pg_b])
                        self.act(gt[:, hh, :], pg[:, :], AF.Silu, r=[pg_b], w=[gt_b])
                    self.dma(self.gateT.ap()[h4 * 4:(h4 + 1) * 4, :, g * 512:(g + 1) * 512].rearrange("h p n -> p h n"),
                             gt[:, :, :], gt_b, r=[gt_b], w=[self.gate_b[h4][g]])

    def mla_phase2(self, st0, l, j, cqT, cqT_b, ckvT, ckvT_b, krT, krT_b, c2T, c2T_b, s2T, s2T_b,
                   qg, qg_b, kvg, kvg_b):
        d = self.d
        with self.scope() as st:
            ones, ones_b = self.sb(st, "ones", [128, 128], BF16)
            self.sc.add("pool", lambda e: e.memset(ones[:, :], 1.0), w=[ones_b])
            hb = Ring([(self.sb(st, "KT", [128, S], BF16), self.sb(st, "V", [128, NT, 128], BF16),
                        self.sb(st, "QT", [128, S], BF16), self.sb(st, "QrT", [64, S], BF16)) for _ in range(2)])
            wq_r = Ring([self.sb(st, "wq", [128, 3, 256], BF16) for _ in range(2)])
            wkv_r = Ring([self.sb(st, "wkv", [128, 2, 256], BF16) for _ in range(2)])
            sq_r = Ring([self.sb(st, "sq", [128, 3, 192], F32) for _ in range(2)])
            skv_r = Ring([self.sb(st, "skv", [128, 2, 256], F32) for _ in range(2)])
            pt_r = Ring([self.sb(st, "pt", [128, 512], BF16) for _ in range(6)])
            rtm = Ring([self.sb(st, "rtm", [64, 2, 512], F32) for _ in range(2)])
            rc_r = Ring([self.sb(st, "rc", [128, 512], F32) for _ in range(2)])
            gl_r = Ring([self.sb(st, "gl", [128, 512], BF16) for _ in range(2)])
            og_r = Ring([self.sb(st, "ogs", [128, 512], BF16) for _ in range(2)])
            pw = Ring([self.ps(st, "pw", [128, 512], F32) for _ in range(4)])
            po_r = Ring([self.ps(st, "po", [128, 512], F32) for _ in range(2)])
            prs_r = Ring([self.ps(st, "prs", [128, 512], F32) for _ in range(2)])

            def prep(h):
                (KT, KT_b), (V, V_b), (QT, QT_b), (QrT, QrT_b) = hbufs[h]
                wq, wq_b = wq_r.next()
                wkv, wkv_b = wkv_r.next()
                sq, sq_b = sq_r.next()
                skv, skv_b = skv_r.next()
                self.dma(sq[:, :, :], d["mla_w_uq"].ap()[j][:, h * 192:(h + 1) * 192].rearrange("(c p) n -> p c n", p=128),
                         sq_b, w=[sq_b])
                self.dma(skv[:, :, :], d["mla_w_ukv"].ap()[j][:, h * 256:(h + 1) * 256].rearrange("(c p) n -> p c n", p=128),
                         skv_b, w=[skv_b])
                for c in range(3):
                    self.ts("pool", wq[:, c, 0:192], sq[:, c, :], qg[:, c:c + 1], None, ALU.mult, None,
                            r=[sq_b, qg_b], w=[wq_b])
                    self.ts("pool", wq[:, c, 192:224], sq[:, c, 160:192], qg[:, c:c + 1], -1.0, ALU.mult, ALU.mult,
                            r=[sq_b, qg_b], w=[wq_b])
                    self.ts("pool", wq[:, c, 224:256], sq[:, c, 128:160], qg[:, c:c + 1], None, ALU.mult, None,
                            r=[sq_b, qg_b], w=[wq_b])
                for c in range(2):
                    self.ts("pool", wkv[:, c, :], skv[:, c, :], kvg[:, c:c + 1], None, ALU.mult, None,
                            r=[skv_b, kvg_b], w=[wkv_b])
                for g in range(NG):
                    cs = slice(g * 512, (g + 1) * 512)
                    p1, p1_b = pw.next()
                    for c in range(3):
                        self.mm(p1[:, :], wq[:, c, 0:128], cqT[:, c, cs], c == 0, c == 2, r=[wq_b, cqT_b], w=[p1_b])
                    self.cp("act", QT[:, cs], p1[:, :], r=[p1_b], w=[QT_b])
                    p2, p2_b = pw.next()
                    for c in range(3):
                        self.mm(p2[0:64, :], wq[:, c, 128:192], cqT[:, c, cs], c == 0, c == 2, r=[wq_b, cqT_b], w=[p2_b])
                    p3, p3_b = pw.next()
                    for c in range(3):
                        self.mm(p3[0:64, :], wq[:, c, 192:256], cqT[:, c, cs], c == 0, c == 2, r=[wq_b, cqT_b], w=[p3_b])
                    rt, rt_b = rtm.next()
                    self.tt("dve", rt[:, 0, :], p2[0:64, :], c2T[:, cs], ALU.mult, r=[p2_b, c2T_b], w=[rt_b])
                    self.tt("dve", rt[:, 1, :], p3[0:64, :], s2T[:, cs], ALU.mult, r=[p3_b, s2T_b], w=[rt_b])
                    self.tt("pool", QrT[:, cs], rt[:, 0, :], rt[:, 1, :], ALU.add, r=[rt_b], w=[QrT_b])
                    p4, p4_b = pw.next()
                    for c in range(2):
                        self.mm(p4[:, :], wkv[:, c, 0:128], ckvT[:, c, cs], c == 0, c == 1, r=[wkv_b, ckvT_b], w=[p4_b])
                    self.cp("act", KT[:, cs], p4[:, :], r=[p4_b], w=[KT_b])
                    p5, p5_b = pw.next()
                    for tt_ in range(4):
                        t = g * 4 + tt_
                        for c in range(2):
                            self.mm(p5[:, tt_ * 128:(tt_ + 1) * 128], ckvT[:, c, t * 128:(t + 1) * 128],
                                    wkv[:, c, 128:256], c == 0, c == 1, r=[ckvT_b, wkv_b], w=[p5_b])
                    self.cp("dve", V[:, g * 4:(g + 1) * 4, :].rearrange("p t d -> p (t d)"), p5[:, :],
                            r=[p5_b], w=[V_b])

            def attention(h):
                (KT, KT_b), (V, V_b), (QT, QT_b), (QrT, QrT_b) = hbufs[h]
                blocks = []
                for jq in range(NG):
                    nkb = 4 * jq + 4
                    for kb in range(nkb):
                        blocks.append((jq, kb, nkb))
                LA = 3
                acc = {}
                info = {}
                for idx in range(len(blocks) + LA):
                    if idx < len(blocks):
                        jq, kb, nkb = blocks[idx]
                        i = kb - 4 * jq
                        off = 128 * i if i > 0 else 0
                        qs = slice(jq * 512 + off, (jq + 1) * 512)
                        ks = slice(kb * 128, (kb + 1) * 128)
                        sp_, sp_b = pw.next()
                        self.mm(sp_[:, off:512], KT[:, ks], QT[:, qs], True, False, r=[KT_b, QT_b], w=[sp_b])
                        self.mm(sp_[:, off:512], krT[0:64, ks], QrT[0:64, qs], False, True, r=[krT_b, QrT_b], w=[sp_b])
                        pt, pt_b = pt_r.next()
                        self.act(pt[:, off:512], sp_[:, off:512], AF.Exp, r=[sp_b], w=[pt_b], scale=float(ATTN_SCALE))
                        if i >= 0:
                            self.tt("pool", pt[:, off:off + 128], pt[:, off:off + 128], self.tri[:, :], ALU.mult,
                                    r=[pt_b, self.tri_b], w=[pt_b])
                        info[idx] = (pt, pt_b, off)
                    k = idx - LA
                    if k >= 0:
                        jq, kb, nkb = blocks[k]
                        pt, pt_b, off = info.pop(k)
                        if kb == 0:
                            acc[jq] = (po_r.next(), prs_r.next())
                        (po, po_b), (prs, prs_b) = acc[jq]
                        first = (kb == 0)
                        lastb = (kb == nkb - 1)
                        self.mm(po[:, off:512], V[:, kb, :], pt[:, off:512], first, lastb, r=[V_b, pt_b], w=[po_b])
                        self.mm(prs[:, off:512], ones[:, :], pt[:, off:512], first, lastb, r=[ones_b, pt_b], w=[prs_b])
                        if lastb:
                            rc, rc_b = rc_r.next()
                            gl, gl_b = gl_r.next()
                            og, og_b = og_r.next()
                            cs = slice(jq * 512, (jq + 1) * 512)
                            self.dma(gl[:, :], self.gateT.ap()[h, :, cs], gl_b, r=[self.gate_b[h // 4][jq]], w=[gl_b])
                            self.sc.add("dve", lambda e, rc=rc, prs=prs: e.reciprocal(rc[:, :], prs[:, :]),
                                        r=[prs_b], w=[rc_b])
                            self.tt("dve", rc[:, :], po[:, :], rc[:, :], ALU.mult, r=[po_b, rc_b], w=[rc_b])
                            self.tt("pool", og[:, :], rc[:, :], gl[:, :], ALU.mult, r=[rc_b, gl_b], w=[og_b])
                            self.dma(self.ogT.ap()[h, :, cs], og[:, :], og_b, r=[og_b], w=[self.og_b[h][jq]])
                            del acc[jq]

            hbufs = {}
            hbufs[0] = hb.next()
            prep(0)
            for h in range(HEADS):
                if h + 1 < HEADS:
                    hbufs[h + 1] = hb.next()
                    prep(h + 1)
                attention(h)
                del hbufs[h]

    def mla_phase3(self, l, j, mode):
        d = self.d
        final = (mode == 'final')
        with self.scope() as st:
            Wo, Wo_b = self.sb(st, "mWo", [128, 16, D], BF16)
            with self.scope() as wst:
                stg = Ring([self.sb(wst, "mostg", [128, D], F32) for _ in range(3)])
                self.load_weight_rows(stg, Wo, Wo_b, d["mla_w_o"].ap()[j], None, None, 16, D, D)
            if final:
                Gf, Gf_b = self.sb(st, "mGf", [128, D], F32)
                self.dma(Gf[:, :], bass.AP(d["final_g"], 0, [[0, 128], [1, D]]), Gf_b, w=[Gf_b])
                junk, junk_b = self.sb(st, "mjunk", [128, D], BF16)
                str_ = Ring([self.sb(st, "m3st", [128, 8], F32) for _ in range(3)])
            ogr = Ring([self.sb(st, "og3", [128, 16, 512], BF16) for _ in range(2)])
            xts = Ring([self.sb(st, "m3xt", [128, D], F32) for _ in range(3)])
            py = Ring([self.ps(st, "m3py", [128, 512], F32) for _ in range(4)])
            for g in range(NG):
                og, og_b = ogr.next()
                cs = slice(g * 512, (g + 1) * 512)
                for q in range(4):
                    self.dma(og[:, q * 4:(q + 1) * 4, :],
                             self.ogT.ap()[q * 4:(q + 1) * 4, :, cs].rearrange("h p n -> p h n"), og_b,
                             r=[self.og_b[q * 4 + hh][g] for hh in range(4)], w=[og_b])
                for tt_ in range(4):
                    t = g * 4 + tt_
                    xt, xt_b = xts.next()
                    src, srcb = self.x_src(t)
                    self.dma(xt[:, :], src, xt_b, r=srcb, w=[xt_b])
                    for hf in range(2):
                        y, y_b = py.next()
                        for h in range(16):
                            self.mm(y[:, :], og[:, h, tt_ * 128:(tt_ + 1) * 128], Wo[:, h, hf * 512:(hf + 1) * 512],
                                    h == 0, h == 15, r=[og_b, Wo_b], w=[y_b])
                        self.tt("dve", xt[:, hf * 512:(hf + 1) * 512], y[:, :], xt[:, hf * 512:(hf + 1) * 512], ALU.add,
                                r=[y_b, xt_b], w=[xt_b])
                    if final:
                        self.final_norm_store(t, xt, xt_b, junk, junk_b, str_, Gf, Gf_b)
                    elif mode == 'out':
                        self.dma(self.out.ap()[t * 128:(t + 1) * 128, :], xt[:, :], xt_b, r=[xt_b],
                                 w=[self.out_b[t]])
                    else:
                        self.dma(self.xres.ap()[t * 128:(t + 1) * 128, :], xt[:, :], xt_b, r=[xt_b],
                                 w=[self.xres_b[t]])


def _host_inputs(inp, b):
    f32 = np.float32
    pos = np.ascontiguousarray(inp["positions"][b]).astype(np.int32)
    inv_freq = (1.0 / (10000.0 ** (np.arange(0, ROPE, 2, dtype=np.float32) / ROPE))).astype(f32)
    m = {}
    m["x"] = np.ascontiguousarray(inp["x"][b], dtype=f32)
    m["pos_tok"] = np.ascontiguousarray(pos.reshape(NT, 128).T)
    m["pos_row"] = np.ascontiguousarray(pos.reshape(1, S))
    m["invf_tok"] = np.ascontiguousarray(np.broadcast_to(np.concatenate([inv_freq, inv_freq])[None, :], (128, 64)), dtype=f32)
    m["invf_col"] = np.ascontiguousarray(np.concatenate([inv_freq, inv_freq]).reshape(64, 1), dtype=f32)
    m["ident"] = np.eye(128, dtype=f32).astype(ml_dtypes.bfloat16)
    m["tri"] = np.triu(np.ones((128, 128), dtype=f32)).astype(ml_dtypes.bfloat16)
    m["norm_g"] = np.ascontiguousarray(np.asarray(inp["norm_g"], f32).reshape(DEPTH, 8, 128).transpose(0, 2, 1))
    m["final_g"] = np.ascontiguousarray(np.asarray(inp["final_g"], f32).reshape(1, D))
    m["mla_w_in"] = np.ascontiguousarray(inp["mla_w_in"], dtype=f32)
    m["mla_qg"] = np.ascontiguousarray(np.asarray(inp["mla_q_norm_g"], f32).reshape(2, 3, 128).transpose(0, 2, 1))
    m["mla_kvg"] = np.ascontiguousarray(np.asarray(inp["mla_kv_norm_g"], f32).reshape(2, 2, 128).transpose(0, 2, 1))
    m["mla_w_uq"] = np.ascontiguousarray(inp["mla_w_uq"], dtype=f32)
    m["mla_w_ukv"] = np.ascontiguousarray(inp["mla_w_ukv"], dtype=f32)
    m["mla_w_o"] = np.ascontiguousarray(inp["mla_w_o"], dtype=f32)
    m["sgu_w_in"] = np.ascontiguousarray(inp["sgu_w_in"], dtype=f32)
    m["sgu_ln_g"] = np.ascontiguousarray(np.asarray(inp["sgu_ln_g"], f32).reshape(2, 1, BW))
    m["sgu_ln_b"] = np.ascontiguousarray(np.asarray(inp["sgu_ln_b"], f32).reshape(2, 1, BW))
    m["sgu_w_sT"] = np.ascontiguousarray(np.asarray(inp["sgu_w_s"], f32).transpose(0, 3, 1, 2))
    m["sgu_b_sT"] = np.ascontiguousarray(np.asarray(inp["sgu_b_s"], f32).transpose(0, 2, 1))
    m["sgu_w_o"] = np.ascontiguousarray(inp["sgu_w_o"], dtype=f32)
    return m


_CACHE = {}
_LAYERED = ["mla_w_in", "mla_qg", "mla_kvg", "mla_w_uq", "mla_w_ukv", "mla_w_o", "sgu_w_in", "sgu_ln_g", "sgu_ln_b",
            "sgu_w_sT", "sgu_b_sT", "sgu_w_o"]


def get_prog(layers, first_from_x=True, do_final=True, part=None, slot=False):
    key = (tuple(layers), first_from_x, do_final, part, slot)
    if key not in _CACHE:
        p = Prog(list(layers), first_from_x, do_final, part, slot)
        p.build()
        _CACHE[key] = p
    return _CACHE[key]


def _slot_map(m, l):
    j = l // 2
    m2 = dict(m)
    for k in _LAYERED:
        m2[k] = np.ascontiguousarray(m[k][j:j + 1])
    m2["norm_g"] = np.ascontiguousarray(m["norm_g"][l:l + 1])
    return m2


def _launch(prog, maps):
    res = run_bass_kernel_spmd(prog.nc, maps, core_ids=list(range(len(maps))))
    return res.results


def kernel_unfused(**inputs):
    inp = {k: np.asarray(v) for k, v in inputs.items()}
    B = inp["x"].shape[0]
    base = _host_inputs(inp, 0)
    per_core = []
    for b in range(B):
        m = dict(base)
        pos = np.ascontiguousarray(inp["positions"][b]).astype(np.int32)
        m["pos_tok"] = np.ascontiguousarray(pos.reshape(NT, 128).T)
        m["pos_row"] = np.ascontiguousarray(pos.reshape(1, S))
        per_core.append(m)
    xs = [np.ascontiguousarray(inp["x"][b], dtype=np.float32) for b in range(B)]
    for l in range(DEPTH):
        last = (l == DEPTH - 1)
        if l % 2 == 0:
            pa = get_prog([l % 2], True, False, "A", True)
            maps = [dict(_slot_map(per_core[b], l), x=xs[b]) for b in range(B)]
            ra = _launch(pa, maps)
            pb = get_prog([l % 2], True, last, "B", True)
            maps = [dict(maps[b], ogT=np.asarray(ra[b]["ogT"])) for b in range(B)]
            rb = _launch(pb, maps)
            xs = [np.asarray(rb[b]["out"], dtype=np.float32) for b in range(B)]
        else:
            ps_ = get_prog([l % 2], True, last, None, True)
            maps = [dict(_slot_map(per_core[b], l), x=xs[b]) for b in range(B)]
            r = _launch(ps_, maps)
            xs = [np.asarray(r[b]["out"], dtype=np.float32) for b in range(B)]
    return np.stack(xs, axis=0)


def kernel(**inputs):
    inp = {k: np.asarray(v) for k, v in inputs.items()}
    B = inp["x"].shape[0]
    base = _host_inputs(inp, 0)
    maps = []
    for b in range(B):
        m = dict(base)
        pos = np.ascontiguousarray(inp["positions"][b]).astype(np.int32)
        m["pos_tok"] = np.ascontiguousarray(pos.reshape(NT, 128).T)
        m["pos_row"] = np.ascontiguousarray(pos.reshape(1, S))
        m["x"] = np.ascontiguousarray(inp["x"][b], dtype=np.float32)
        maps.append(m)
    prog = get_prog(range(DEPTH), True, True, None, False)
    res = _launch(prog, maps)
    return np.stack([np.asarray(res[b]["out"], dtype=np.float32) for b in range(B)], axis=0)
```

```python
import math
from contextlib import ExitStack, contextmanager

import numpy as np
import ml_dtypes

import concourse.bass as bass
import concourse.mybir as mybir
from concourse.bass_utils import run_bass_kernel_spmd

F32 = mybir.dt.float32
BF16 = mybir.dt.bfloat16
I32 = mybir.dt.int32
AF = mybir.ActivationFunctionType
ALU = mybir.AluOpType

S = 4096
D = 1024
NT = S // 128
NG = S // 512
DEPTH = 4
HEADS = 16
BW = 2048
Q_LORA = 384
KV_LORA = 256
ROPE = 64
MLA_IN_W = Q_LORA + KV_LORA + ROPE + BW
SGU_IN_W = 3 * BW
EPS = 1e-6
LN_EPS = 1e-5
ATTN_SCALE = (128 + 64) ** -0.5
TWO_PI = 2.0 * math.pi


class Buf:
    __slots__ = ("name", "lw", "rd", "psum")

    def __init__(self, name, psum=False):
        self.name = name
        self.lw = None
        self.rd = []
        self.psum = psum


class Sched:
    STREAMS = ("pe", "act", "dve", "pool", "sp")

    def __init__(self):
        self.ops = []

    def add(self, eng, fn, r=(), w=(), dma=None):
        self.ops.append((eng, fn, tuple(r), tuple(w), dma))

    def barrier(self):
        self.ops.append(("bar", None, (), (), None))

    def emit(self, nc, stack):
        ops = self.ops
        n = len(ops)
        deps = [None] * n
        last_dma = {}
        last_op = {}
        pending = {}
        for i, (eng, fn, r, w, dma) in enumerate(ops):
            if eng == "bar":
                deps[i] = []
                snap = set(last_op.values()) | set(last_dma.values())
                for s_ in self.STREAMS:
                    pending[s_] = snap
                continue
            kinds = {}
            for b in r:
                if b.lw is not None:
                    kinds.setdefault(b.lw, set()).add("raw")
                if b.psum:
                    for j in b.rd:
                        if ops[j][0] != eng:
                            kinds.setdefault(j, set()).add("rar")
            for b in w:
                if b.lw is not None:
                    kinds.setdefault(b.lw, set()).add("waw")
                for j in b.rd:
                    kinds.setdefault(j, set()).add("war")
            if dma is not None and dma in last_dma:
                kinds.setdefault(last_dma[dma], set()).add("raw")
            dd = []
            for j, ks in kinds.items():
                if j == i:
                    continue
                ej, dj = ops[j][0], ops[j][4]
                if dj is None and dma is None and ej == eng:
                    if eng == "pe" or "raw" not in ks:
                        continue
                dd.append(j)
            if eng in pending:
                for j in pending.pop(eng):
                    if ops[j][0] == eng and ops[j][4] is None:
                        continue
                    dd.append(j)
            if fn is not None:
                last_op[eng] = i
            best = {}
            for j in dd:
                k_ = ("d", ops[j][4]) if ops[j][4] is not None else ("e", ops[j][0])
                if k_ not in best or best[k_] < j:
                    best[k_] = j
            dd = list(best.values())
            deps[i] = dd
            for b in r:
                b.rd.append(i)
            for b in w:
                b.lw = i
                b.rd = []
            if dma is not None:
                last_dma[dma] = i
        needs = [False] * n
        for dd in deps:
            for j in dd:
                needs[j] = True
        sems = {}

        def get_sem(key):
            if key not in sems:
                sems[key] = stack.enter_context(nc.semaphore("s_%s" % key))
            return sems[key]

        token = [None] * n
        tick = {}
        for i, (eng, fn, r, w, dma) in enumerate(ops):
            if fn is None:
                continue
            if dma is not None:
                key = "d_" + dma.name
                tick[key] = tick.get(key, 0) + 16
                token[i] = (key, tick[key], 16)
            elif needs[i]:
                key = "e_" + eng
                tick[key] = tick.get(key, 0) + 1
                token[i] = (key, tick[key], 1)
        for key in tick:
            get_sem(key)
        self.n_sems = len(sems)
        self.max_tick = max(tick.values()) if tick else 0
        by_stream = {s: [] for s in self.STREAMS}
        for i, op in enumerate(ops):
            if op[0] != "bar":
                by_stream[op[0]].append(i)

        def simulate():
            val = {k: 0 for k in sems}
            pos = {s_: 0 for s_ in self.STREAMS}
            progress = True
            while progress:
                progress = False
                for s_ in self.STREAMS:
                    lst = by_stream[s_]
                    while pos[s_] < len(lst):
                        i = lst[pos[s_]]
                        ok = all(val[token[j][0]] >= token[j][1] for j in deps[i])
                        if not ok:
                            break
                        if token[i] is not None:
                            val[token[i][0]] += token[i][2]
                        pos[s_] += 1
                        progress = True
            stuck = {s_: (pos[s_], len(by_stream[s_])) for s_ in self.STREAMS if pos[s_] < len(by_stream[s_])}
            for s_, (p_, n_) in stuck.items():
                i = by_stream[s_][p_]
                print("STUCK", s_, p_, n_, "op", i, [(j, ops[j][0], token[j], val[token[j][0]]) for j in deps[i]
                                                      if val[token[j][0]] < token[j][1]])
            return not stuck

        self.sim_ok = simulate()

        def run_stream(e, stream):
            waited = {}
            for i in by_stream[stream]:
                eng, fn, r, w, dma = ops[i]
                need = {}
                for j in deps[i]:
                    key, val, _ = token[j]
                    if need.get(key, 0) < val:
                        need[key] = val
                for key, val in need.items():
                    if waited.get(key, 0) < val:
                        e.wait_ge(sems[key], val)
                        waited[key] = val
                if fn is None:
                    continue
                inst = fn(e)
                if token[i] is not None:
                    key, val, inc = token[i]
                    inst.then_inc(sems[key], inc)

        with nc.Block() as block:
            @block.tensor
            def _(e):
                run_stream(e, "pe")

            @block.scalar
            def _(e):
                run_stream(e, "act")

            @block.vector
            def _(e):
                run_stream(e, "dve")

            @block.gpsimd
            def _(e):
                run_stream(e, "pool")

            @block.sync
            def _(e):
                run_stream(e, "sp")


class Ring:
    def __init__(self, items):
        self.items = items
        self.i = 0

    def next(self):
        it = self.items[self.i % len(self.items)]
        self.i += 1
        return it


class Prog:
    def __init__(self, layers, first_from_x=True, do_final=True, part=None, slot=False):
        self.part = part
        self.slot = slot
        self.layers = layers
        self.do_final = do_final
        self.first_from_x = first_from_x
        self.nc = bass.Bass("TRN2", target_bir_lowering=False)
        self.sc = Sched()
        self.uid = 0

    @contextmanager
    def scope(self):
        with ExitStack() as st:
            yield st
        self.sc.barrier()

    def dram(self, name, shape, dt, kind):
        return self.nc.dram_tensor(name, list(shape), dt, kind=kind)

    def sb(self, st, name, shape, dt):
        self.uid += 1
        t = st.enter_context(self.nc.sbuf_tensor("%s_%d" % (name, self.uid), list(shape), dt))
        return t, Buf("%s_%d" % (name, self.uid))

    def ps(self, st, name, shape, dt):
        self.uid += 1
        t = st.enter_context(self.nc.psum_tensor("%s_%d" % (name, self.uid), list(shape), dt))
        return t, Buf("%s_%d" % (name, self.uid), psum=True)

    def dma(self, out, in_, key, r=(), w=(), eng="sp"):
        self.sc.add(eng, lambda e: e.dma_start(out=out, in_=in_), r=r, w=w, dma=key)

    def mm(self, out, lhsT, rhs, start, stop, r, w):
        self.sc.add("pe", lambda e: e.matmul(out, lhsT, rhs, start=start, stop=stop), r=r, w=w)

    def tr(self, out, in_, ident, r, w):
        self.sc.add("pe", lambda e: e.transpose(out, in_, ident), r=r, w=w)

    def act(self, out, in_, func, r, w, bias=None, scale=None, accum_out=None):
        kw = {}
        if bias is not None:
            kw["bias"] = bias
        if scale is not None:
            kw["scale"] = scale
        if accum_out is not None:
            kw["accum_out"] = accum_out
        self.sc.add("act", lambda e: e.activation(out, in_, func, **kw), r=r, w=w)

    def ts(self, eng, out, in0, s1, s2, op0, op1, r, w):
        if op1 is None:
            self.sc.add(eng, lambda e: e.tensor_scalar(out, in0, s1, None, op0), r=r, w=w)
        else:
            self.sc.add(eng, lambda e: e.tensor_scalar(out, in0, s1, s2, op0, op1), r=r, w=w)

    def tt(self, eng, out, in0, in1, op, r, w):
        self.sc.add(eng, lambda e: e.tensor_tensor(out, in0, in1, op), r=r, w=w)

    def stt(self, out, in0, scalar, in1, op0, op1, r, w):
        self.sc.add("dve", lambda e: e.scalar_tensor_tensor(out, in0, scalar, in1, op0, op1), r=r, w=w)

    def cp(self, eng, out, in_, r, w):
        if eng == "act":
            self.sc.add("act", lambda e: e.activation(out, in_, AF.Copy), r=r, w=w)
        else:
            self.sc.add(eng, lambda e: e.tensor_copy(out, in_), r=r, w=w)

    def build(self):
        nc = self.nc
        self.x_in = self.dram("x", [S, D], F32, "ExternalInput")
        self.out = self.dram("out", [S, D], F32, "ExternalOutput")
        self.xres = self.dram("xres", [S, D], F32, "Internal")
        self.gateT = self.dram("gateT", [HEADS, 128, S], BF16, "Internal")
        og_kind = {None: "Internal", "A": "ExternalOutput", "B": "ExternalInput"}[self.part]
        self.ogT = self.dram("ogT", [HEADS, 128, S], BF16, og_kind)
        N2 = 1 if self.slot else 2
        N4 = 1 if self.slot else DEPTH
        d = {}
        d["pos_tok"] = self.dram("pos_tok", [128, NT], I32, "ExternalInput")
        d["pos_row"] = self.dram("pos_row", [1, S], I32, "ExternalInput")
        d["invf_tok"] = self.dram("invf_tok", [128, 64], F32, "ExternalInput")
        d["invf_col"] = self.dram("invf_col", [64, 1], F32, "ExternalInput")
        d["ident"] = self.dram("ident", [128, 128], BF16, "ExternalInput")
        d["tri"] = self.dram("tri", [128, 128], BF16, "ExternalInput")
        d["norm_g"] = self.dram("norm_g", [N4, 128, 8], F32, "ExternalInput")
        d["final_g"] = self.dram("final_g", [1, D], F32, "ExternalInput")
        d["mla_w_in"] = self.dram("mla_w_in", [N2, D, MLA_IN_W], F32, "ExternalInput")
        d["mla_qg"] = self.dram("mla_qg", [N2, 128, 3], F32, "ExternalInput")
        d["mla_kvg"] = self.dram("mla_kvg", [N2, 128, 2], F32, "ExternalInput")
        d["mla_w_uq"] = self.dram("mla_w_uq", [N2, Q_LORA, HEADS * 192], F32, "ExternalInput")
        d["mla_w_ukv"] = self.dram("mla_w_ukv", [N2, KV_LORA, HEADS * 256], F32, "ExternalInput")
        d["mla_w_o"] = self.dram("mla_w_o", [N2, BW, D], F32, "ExternalInput")
        d["sgu_w_in"] = self.dram("sgu_w_in", [N2, D, SGU_IN_W], F32, "ExternalInput")
        d["sgu_ln_g"] = self.dram("sgu_ln_g", [N2, 1, BW], F32, "ExternalInput")
        d["sgu_ln_b"] = self.dram("sgu_ln_b", [N2, 1, BW], F32, "ExternalInput")
        d["sgu_w_sT"] = self.dram("sgu_w_sT", [N2, 128, 16, 128], F32, "ExternalInput")
        d["sgu_b_sT"] = self.dram("sgu_b_sT", [N2, 128, 16], F32, "ExternalInput")
        d["sgu_w_o"] = self.dram("sgu_w_o", [N2, BW, D], F32, "ExternalInput")
        self.d = d
        self.xres_b = [Buf("xres%d" % t) for t in range(NT)]
        self.out_b = [Buf("out%d" % t) for t in range(NT)]
        self.gate_b = [[Buf("gate%d_%d" % (h4, g)) for g in range(NG)] for h4 in range(4)]
        self.og_b = [[Buf("og%d_%d" % (h, g)) for g in range(NG)] for h in range(HEADS)]
        self.x_from_input = self.first_from_x

        with ExitStack() as top:
            self.ident, self.ident_b = self.sb(top, "ident", [128, 128], BF16)
            self.tri, self.tri_b = self.sb(top, "tri", [128, 128], BF16)
            self.epsn, self.epsn_b = self.sb(top, "epsn", [128, 1], F32)
            self.epsl, self.epsl_b = self.sb(top, "epsl", [128, 1], F32)
            self.dma(self.ident[:, :], d["ident"].ap(), self.ident_b, w=[self.ident_b])
            self.dma(self.tri[:, :], d["tri"].ap(), self.tri_b, w=[self.tri_b])
            self.sc.add("pool", lambda e: e.memset(self.epsn[:, :], EPS), w=[self.epsn_b])
            self.sc.add("pool", lambda e: e.memset(self.epsl[:, :], LN_EPS), w=[self.epsl_b])
            for li, l in enumerate(self.layers):
                last = (li == len(self.layers) - 1)
                mode = "res" if not last else ("final" if self.do_final else "out")
                ll, jj = (0, 0) if self.slot else (l, l // 2)
                if l % 2 == 0:
                    self.mla_layer(ll, jj, mode)
                else:
                    self.sgu_layer(ll, jj, mode)
            self.sc.add("sp", None, r=self.out_b)
            if self.part == "A":
                self.sc.add("sp", None, r=[b for row in self.og_b for b in row])
            self.sc.emit(nc, top)
        return nc

    def x_src(self, t):
        if self.x_from_input:
            return self.x_in.ap()[t * 128:(t + 1) * 128, :], []
        return self.xres.ap()[t * 128:(t + 1) * 128, :], [self.xres_b[t]]

    def rms_rstd(self, xt, xt_b, junk, junk_b, ss, ss_b, rstd, rstd_b, ncols, eps_t, eps_b):
        self.act(junk, xt, AF.Square, r=[xt_b], w=[junk_b, ss_b], scale=float(ncols ** -0.5), accum_out=ss)
        self.act(ss, ss, AF.Sqrt, r=[ss_b, eps_b], w=[ss_b], bias=eps_t)
        self.sc.add("dve", lambda e: e.reciprocal(rstd, ss), r=[ss_b], w=[rstd_b])

    def load_weight_rows(self, st_ring, dst, dst_b, src_ap, gcol, gcol_b, nk, ncols, piece):
        for kc in range(nk):
            for c0 in range(0, ncols, piece):
                c1 = min(ncols, c0 + piece)
                stg, stg_b = st_ring.next()
                self.dma(stg[:, 0:c1 - c0], src_ap[kc * 128:(kc + 1) * 128, c0:c1], stg_b, w=[stg_b])
                self._wl = getattr(self, "_wl", 0) + 1
                eng = "dve" if self._wl % 2 == 0 else "act"
                if gcol is None:
                    self.cp(eng, dst[:, kc, c0:c1], stg[:, 0:c1 - c0], r=[stg_b], w=[dst_b])
                elif eng == "dve":
                    self.ts("dve", dst[:, kc, c0:c1], stg[:, 0:c1 - c0], gcol[:, kc:kc + 1], None,
                            ALU.mult, None, r=[stg_b, gcol_b], w=[dst_b])
                else:
                    self.act(dst[:, kc, c0:c1], stg[:, 0:c1 - c0], AF.Copy, r=[stg_b, gcol_b], w=[dst_b],
                             scale=gcol[:, kc:kc + 1])

    def sgu_layer(self, l, j, mode):
        d = self.d
        final = (mode == 'final')
        with self.scope() as st:
            Win, Win_b = self.sb(st, "sWin", [128, 8, SGU_IN_W], BF16)
            Wo, Wo_b = self.sb(st, "sWo", [128, 16, D], BF16)
            WsT, WsT_b = self.sb(st, "sWsT", [128, 16, 128], BF16)
            Gb, Gb_b = self.sb(st, "sG", [128, BW], F32)
            Bb, Bb_b = self.sb(st, "sB", [128, BW], F32)
            bsT, bsT_b = self.sb(st, "sbs", [128, 16], F32)
            gcol, gcol_b = self.sb(st, "sgcol", [128, 8], F32)
            if final:
                Gf, Gf_b = self.sb(st, "sGf", [128, D], F32)
                self.dma(Gf[:, :], bass.AP(d["final_g"], 0, [[0, 128], [1, D]]), Gf_b, w=[Gf_b])
            self.dma(gcol[:, :], d["norm_g"].ap()[l], gcol_b, w=[gcol_b])
            self.dma(bsT[:, :], d["sgu_b_sT"].ap()[j], bsT_b, w=[bsT_b])
            self.dma(Gb[:, :], bass.AP(d["sgu_ln_g"], j * BW, [[0, 128], [1, BW]]), Gb_b, w=[Gb_b])
            self.dma(Bb[:, :], bass.AP(d["sgu_ln_b"], j * BW, [[0, 128], [1, BW]]), Bb_b, w=[Bb_b])
            with self.scope() as wst:
                stg = Ring([self.sb(wst, "sstg", [128, 2048], F32) for _ in range(4)])
                self.load_weight_rows(stg, Win, Win_b, d["sgu_w_in"].ap()[j], gcol, gcol_b, 8, SGU_IN_W, 2048)
                self.load_weight_rows(stg, Wo, Wo_b, d["sgu_w_o"].ap()[j], None, None, 16, D, 2048)
                sw, sw_b = stg.next()
                self.dma(sw[:, :], d["sgu_w_sT"].ap()[j].rearrange("s g t -> s (g t)"), sw_b, w=[sw_b])
                for g in range(16):
                    self.tt("pool", WsT[:, g, :], sw[:, g * 128:(g + 1) * 128], self.tri[:, :], ALU.mult,
                            r=[sw_b, self.tri_b], w=[WsT_b])

            xts = Ring([self.sb(st, "sxt", [128, D], F32) for _ in range(3)])
            xnr = Ring([self.sb(st, "sxn", [128, D], BF16) for _ in range(1)])
            xnTr = Ring([self.sb(st, "sxnT", [128, 8, 128], BF16) for _ in range(2)])
            gvr = Ring([self.sb(st, "sgv", [128, BW], F32) for _ in range(1)])
            sgr = Ring([self.sb(st, "ssg", [128, BW], BF16) for _ in range(1)])
            ur = Ring([self.sb(st, "su", [128, BW], BF16) for _ in range(2)])
            vlr = Ring([self.sb(st, "svl", [128, BW], BF16) for _ in range(2)])
            prr = Ring([self.sb(st, "spr", [128, BW], BF16) for _ in range(1)])
            prTr = Ring([self.sb(st, "sprT", [128, 16, 128], BF16) for _ in range(1)])
            str_ = Ring([self.sb(st, "sst", [128, 16], F32) for _ in range(3)])
            pT = Ring([self.ps(st, "spT", [128, 8, 128], BF16) for _ in range(2)])
            pm = Ring([self.ps(st, "spm", [128, 512], F32) for _ in range(3)])
            psp = Ring([self.ps(st, "spsp", [128, 512], F32) for _ in range(1)])
            py = [self.ps(st, "spy", [128, 512], F32) for _ in range(2)]

            state = {}
            order = [4, 5, 6, 7, 0, 1, 2, 3, 8, 9, 10, 11]

            def a1a(t):
                xt, xt_b = xts.next()
                xn, xn_b = xnr.next()
                stt_, st_b = str_.next()
                src, srcb = self.x_src(t)
                self.dma(xt[:, :], src, xt_b, r=srcb, w=[xt_b])
                state[t] = dict(xt=(xt, xt_b), xn=(xn, xn_b), st=(stt_, st_b))

            def a1n(t):
                xt, xt_b = state[t]["xt"]
                xn, xn_b = state[t]["xn"]
                stt_, st_b = state[t]["st"]
                self.rms_rstd(xt[:, :], xt_b, xn[:, :], xn_b, stt_[:, 0:1], st_b, stt_[:, 1:2], st_b, D,
                              self.epsn[:, 0:1], self.epsn_b)
                self.ts("dve", xn[:, :], xt[:, :], stt_[:, 1:2], None, ALU.mult, None, r=[xt_b, st_b], w=[xn_b])

            def a1b(t):
                xn, xn_b = state[t]["xn"]
                xnT, xnT_b = xnTr.next()
                p, p_b = pT.next()
                for kc in range(8):
                    self.tr(p[:, kc, :], xn[:, kc * 128:(kc + 1) * 128], self.ident[:, :],
                            r=[xn_b, self.ident_b], w=[p_b])
                self.cp("dve", xnT[:, :, :], p[:, :, :], r=[p_b], w=[xnT_b])
                state[t]["xnT"] = (xnT, xnT_b)
                state[t]["gv"] = gvr.next()
                state[t]["u"] = ur.next()
                state[t]["sg"] = sgr.next()
                state[t]["vl"] = vlr.next()

            def a_group(t, idx):
                stt = state[t]
                xnT, xnT_b = stt["xnT"]
                gv, gv_b = stt["gv"]
                u, u_b = stt["u"]
                sg, sg_b = stt["sg"]
                vl, vl_b = stt["vl"]
                nb = order[idx]
                pp, pp_b = pm.next()
                for kc in range(8):
                    self.mm(pp[:, :], xnT[:, kc, :], Win[:, kc, nb * 512:(nb + 1) * 512], kc == 0, kc == 7,
                            r=[xnT_b, Win_b], w=[pp_b])
                if nb < 4:
                    self.act(u[:, nb * 512:(nb + 1) * 512], pp[:, :], AF.Gelu, r=[pp_b], w=[u_b])
                elif nb < 8:
                    c = nb - 4
                    self.act(gv[:, c * 512:(c + 1) * 512], pp[:, :], AF.Gelu, r=[pp_b], w=[gv_b])
                else:
                    c = nb - 8
                    self.act(sg[:, c * 512:(c + 1) * 512], pp[:, :], AF.Silu, r=[pp_b], w=[sg_b])
                if nb == 7:
                    self.ln_chain(gv, gv_b, vl, vl_b, Gb, Gb_b, Bb, Bb_b, st)
                if idx == 11:
                    self.tt("pool", u[:, :], u[:, :], sg[:, :], ALU.mult, r=[u_b, sg_b], w=[u_b])

            def b1(t, q):
                stt = state[t]
                u, u_b = stt["u"]
                vl, vl_b = stt["vl"]
                if q == 0:
                    stt["pr"] = prr.next()
                    stt["prT"] = prTr.next()
                pr, pr_b = stt["pr"]
                sp, sp_b = psp.next()
                for gg in range(4):
                    g = q * 4 + gg
                    self.mm(sp[:, gg * 128:(gg + 1) * 128], WsT[:, g, :], vl[:, g * 128:(g + 1) * 128],
                            True, True, r=[WsT_b, vl_b], w=[sp_b])
                for gg in range(4):
                    g = q * 4 + gg
                    self.stt(pr[:, g * 128:(g + 1) * 128], sp[:, gg * 128:(gg + 1) * 128], bsT[:, g:g + 1],
                             u[:, g * 128:(g + 1) * 128], ALU.add, ALU.mult,
                             r=[sp_b, bsT_b, u_b], w=[pr_b])

            def b2(t, hf):
                stt = state[t]
                pr, pr_b = stt["pr"]
                prT, prT_b = stt["prT"]
                p, p_b = pT.next()
                for kk in range(8):
                    kc = hf * 8 + kk
                    self.tr(p[:, kk, :], pr[:, kc * 128:(kc + 1) * 128], self.ident[:, :],
                            r=[pr_b, self.ident_b], w=[p_b])
                self.cp("act" if hf == 0 else "dve", prT[:, hf * 8:(hf + 1) * 8, :], p[:, :, :],
                        r=[p_b], w=[prT_b])

            def b3(t, hf):
                stt = state[t]
                xt, xt_b = stt["xt"]
                pr, pr_b = stt["pr"]
                prT, prT_b = stt["prT"]
                y, y_b = py[hf]
                for kc in range(16):
                    self.mm(y[:, :], prT[:, kc, :], Wo[:, kc, hf * 512:(hf + 1) * 512], kc == 0, kc == 15,
                            r=[prT_b, Wo_b], w=[y_b])
                self.tt("dve", xt[:, hf * 512:(hf + 1) * 512], y[:, :], xt[:, hf * 512:(hf + 1) * 512], ALU.add,
                        r=[y_b, xt_b], w=[xt_b])
                if hf == 1:
                    if final:
                        self.final_norm_store(t, xt, xt_b, pr, pr_b, str_, Gf, Gf_b)
                    elif mode == 'out':
                        self.dma(self.out.ap()[t * 128:(t + 1) * 128, :], xt[:, :], xt_b, r=[xt_b], w=[self.out_b[t]])
                    else:
                        self.dma(self.xres.ap()[t * 128:(t + 1) * 128, :], xt[:, :], xt_b, r=[xt_b], w=[self.xres_b[t]])
                    del state[t]

            self._sgu_ln_tmp = Ring([self.sb(st, "slt", [128, 8], F32) for _ in range(2)])
            self._lnst = Ring([self.sb(st, "slnst", [128, 24], F32) for _ in range(2)])
            pieces = {1: ("b1", 0), 2: ("b1", 1), 3: ("b1", 2), 4: ("b1", 3), 5: ("b2", 0), 6: ("b2", 1),
                      8: ("b3", 0), 10: ("b3", 1)}
            a1a(0)
            a1n(0)
            a1b(0)
            for t in range(NT + 1):
                if t + 1 < NT:
                    a1a(t + 1)
                for idx in range(12):
                    if t < NT:
                        a_group(t, idx)
                    if t >= 1 and idx in pieces:
                        kind, arg = pieces[idx]
                        {"b1": b1, "b2": b2, "b3": b3}[kind](t - 1, arg)
                    if idx == 2 and t + 1 < NT:
                        a1n(t + 1)
                    if idx == 10 and t + 1 < NT:
                        a1b(t + 1)
        self.x_from_input = False

    def ln_chain(self, gv, gv_b, vl, vl_b, Gb, Gb_b, Bb, Bb_b, st):
        lt, lt_b = self._sgu_ln_tmp.next()
        stats, stats_b = self._lnst.next()
        for c in range(4):
            self.sc.add("dve", lambda e, c=c: e.bn_stats(stats[:, c * 6:(c + 1) * 6], gv[:, c * 512:(c + 1) * 512]),
                        r=[gv_b], w=[stats_b])
        self.sc.add("dve", lambda e: e.bn_aggr(lt[:, 0:2], stats[:, :]), r=[stats_b], w=[lt_b])
        self.act(lt[:, 2:3], lt[:, 1:2], AF.Sqrt, r=[lt_b, self.epsl_b], w=[lt_b], bias=self.epsl[:, 0:1])
        self.sc.add("dve", lambda e: e.reciprocal(lt[:, 3:4], lt[:, 2:3]), r=[lt_b], w=[lt_b])
        self.ts("dve", gv[:, :], gv[:, :], lt[:, 0:1], lt[:, 3:4], ALU.subtract, ALU.mult, r=[gv_b, lt_b], w=[gv_b])
        self.tt("pool", gv[:, :], gv[:, :], Gb[:, :], ALU.mult, r=[gv_b, Gb_b], w=[gv_b])
        self.tt("pool", vl[:, :], gv[:, :], Bb[:, :], ALU.add, r=[gv_b, Bb_b], w=[vl_b])

    def final_norm_store(self, t, xt, xt_b, junk, junk_b, str_, Gf, Gf_b):
        stt_, st_b = str_.next()
        self.rms_rstd(xt[:, :], xt_b, junk[:, 0:D], junk_b, stt_[:, 0:1], st_b, stt_[:, 1:2], st_b, D,
                      self.epsn[:, 0:1], self.epsn_b)
        self.stt(xt[:, :], xt[:, :], stt_[:, 1:2], Gf[:, :], ALU.mult, ALU.mult, r=[xt_b, st_b, Gf_b], w=[xt_b])
        self.dma(self.out.ap()[t * 128:(t + 1) * 128, :], xt[:, :], xt_b, r=[xt_b], w=[self.out_b[t]])

    def rope_tables(self, st):
        d = self.d
        cc, cc_b = self.sb(st, "cc", [128, NT, 64], F32)
        ss, ss_b = self.sb(st, "ss", [128, NT, 64], F32)
        c2T, c2T_b = self.sb(st, "c2T", [64, S], F32)
        s2T, s2T_b = self.sb(st, "s2T", [64, S], F32)
        with self.scope() as tmp:
            pi_, pi_b = self.sb(tmp, "posi", [128, NT], I32)
            pf, pf_b = self.sb(tmp, "posf", [128, NT], F32)
            ivt, ivt_b = self.sb(tmp, "ivt", [128, 64], F32)
            ang, ang_b = self.sb(tmp, "ang", [128, NT, 64], F32)
            ki, ki_b = self.sb(tmp, "ki", [128, NT * 64], I32)
            kf, kf_b = self.sb(tmp, "kf", [128, NT * 64], F32)
            self.dma(pi_[:, :], d["pos_tok"].ap(), pi_b, w=[pi_b])
            self.dma(ivt[:, :], d["invf_tok"].ap(), ivt_b, w=[ivt_b])
            self.cp("dve", pf[:, :], pi_[:, :], r=[pi_b], w=[pf_b])
            for t in range(NT):
                self.ts("dve", ang[:, t, :], ivt[:, :], pf[:, t:t + 1], None, ALU.mult, None,
                        r=[ivt_b, pf_b], w=[ang_b])
            angf = ang[:, :, :].rearrange("p t f -> p (t f)")
            self.trig(angf, ang_b, ki, ki_b, kf, kf_b, cc[:, :, :].rearrange("p t f -> p (t f)"), cc_b,
                      ss[:, :, :].rearrange("p t f -> p (t f)"), ss_b, 128, NT * 64)
        with self.scope() as tmp:
            pr_, pr_b = self.sb(tmp, "pri", [64, S], I32)
            ivc, ivc_b = self.sb(tmp, "ivc", [64, 1], F32)
            ang2, ang2_b = self.sb(tmp, "ang2", [64, S], F32)
            ki, ki_b = self.sb(tmp, "ki2", [64, S], I32)
            kf, kf_b = self.sb(tmp, "kf2", [64, S], F32)
            self.dma(pr_[:, :], bass.AP(d["pos_row"], 0, [[0, 64], [1, S]]), pr_b, w=[pr_b])
            self.dma(ivc[:, :], d["invf_col"].ap(), ivc_b, w=[ivc_b])
            self.cp("dve", ang2[:, :], pr_[:, :], r=[pr_b], w=[ang2_b])
            self.ts("dve", ang2[:, :], ang2[:, :], ivc[:, 0:1], None, ALU.mult, None, r=[ang2_b, ivc_b], w=[ang2_b])
            self.trig(ang2[:, :], ang2_b, ki, ki_b, kf, kf_b, c2T[:, :], c2T_b, s2T[:, :], s2T_b, 64, S)
        return (cc, cc_b, ss, ss_b, c2T, c2T_b, s2T, s2T_b)

    def trig(self, ang, ang_b, ki, ki_b, kf, kf_b, cosd, cos_b, sind, sin_b, P, N):
        C1 = 6.28125
        C2 = TWO_PI - C1
        PI_LO = 3.1415925
        self.ts("dve", kf[:, :], ang, 1.0 / TWO_PI, None, ALU.mult, None, r=[ang_b], w=[kf_b])
        self.cp("dve", ki[:, :], kf[:, :], r=[kf_b], w=[ki_b])
        self.cp("dve", kf[:, :], ki[:, :], r=[ki_b], w=[kf_b])
        self.stt(ang, kf[:, :], -C1, ang, ALU.mult, ALU.add, r=[kf_b, ang_b], w=[ang_b])
        self.stt(ang, kf[:, :], -C2, ang, ALU.mult, ALU.add, r=[kf_b, ang_b], w=[ang_b])
        self.ts("dve", kf[:, :], ang, math.pi, -TWO_PI, ALU.is_gt, ALU.mult, r=[ang_b], w=[kf_b])
        self.tt("dve", kf[:, :], kf[:, :], ang, ALU.add, r=[kf_b, ang_b], w=[kf_b])
        self.ts("dve", kf[:, :], kf[:, :], PI_LO, -PI_LO, ALU.min, ALU.max, r=[kf_b], w=[kf_b])
        self.act(sind, kf[:, :], AF.Sin, r=[kf_b], w=[sin_b])
        self.ts("dve", kf[:, :], ang, math.pi / 2, -TWO_PI, ALU.is_gt, ALU.mult, r=[ang_b, sin_b], w=[kf_b])
        self.stt(kf[:, :], ang, math.pi / 2, kf[:, :], ALU.add, ALU.add, r=[kf_b, ang_b], w=[kf_b])
        self.ts("dve", kf[:, :], kf[:, :], PI_LO, -PI_LO, ALU.min, ALU.max, r=[kf_b], w=[kf_b])
        self.act(cosd, kf[:, :], AF.Sin, r=[kf_b], w=[cos_b])

    def mla_layer(self, l, j, mode):
        d = self.d
        if self.part == "B":
            self.mla_phase3(l, j, mode)
            self.x_from_input = False
            return
        with self.scope() as st:
            cqT, cqT_b = self.sb(st, "cqT", [128, 3, S], BF16)
            ckvT, ckvT_b = self.sb(st, "ckvT", [128, 2, S], BF16)
            krT, krT_b = self.sb(st, "krT", [64, S], BF16)
            cc, cc_b, ss, ss_b, c2T, c2T_b, s2T, s2T_b = self.rope_tables(st)
            qg, qg_b = self.sb(st, "qg", [128, 3], F32)
            kvg, kvg_b = self.sb(st, "kvg", [128, 2], F32)
            gcol, gcol_b = self.sb(st, "mgcol", [128, 8], F32)
            self.dma(qg[:, :], d["mla_qg"].ap()[j], qg_b, w=[qg_b])
            self.dma(kvg[:, :], d["mla_kvg"].ap()[j], kvg_b, w=[kvg_b])
            self.dma(gcol[:, :], d["norm_g"].ap()[l], gcol_b, w=[gcol_b])
            import os
            stop = os.environ.get("MLA_STOP", "")
            if stop == "rope":
                return
            self.mla_phase1(st, l, j, cqT, cqT_b, ckvT, ckvT_b, krT, krT_b, cc, cc_b, ss, ss_b, gcol, gcol_b)
            if stop == "p1":
                return
            if stop != "skip2":
                self.mla_phase2(st, l, j, cqT, cqT_b, ckvT, ckvT_b, krT, krT_b, c2T, c2T_b, s2T, s2T_b,
                                qg, qg_b, kvg, kvg_b)
        if stop == "p2" or self.part == "A":
            return
        self.mla_phase3(l, j, mode)
        self.x_from_input = False

    def mla_phase1(self, st0, l, j, cqT, cqT_b, ckvT, ckvT_b, krT, krT_b, cc, cc_b, ss, ss_b, gcol, gcol_b):
        d = self.d
        with self.scope() as st:
            Win, Win_b = self.sb(st, "mWin", [128, 8, MLA_IN_W], BF16)
            with self.scope() as wst:
                stg = Ring([self.sb(wst, "mstg", [128, MLA_IN_W], F32) for _ in range(3)])
                self.load_weight_rows(stg, Win, Win_b, d["mla_w_in"].ap()[j], gcol, gcol_b, 8, MLA_IN_W, MLA_IN_W)
            xts = Ring([self.sb(st, "mxt", [128, D], F32) for _ in range(4)])
            xnr = Ring([self.sb(st, "mxn", [128, D], BF16) for _ in range(2)])
            xnTr = Ring([self.sb(st, "mxnT", [128, 8, 512], BF16) for _ in range(2)])
            str_ = Ring([self.sb(st, "mst", [128, 8], F32) for _ in range(4)])
            cqn_r = Ring([self.sb(st, "mcqn", [128, 704], BF16) for _ in range(2)])
            rtmp = Ring([self.sb(st, "mrt", [128, 128], F32) for _ in range(2)])
            gt_r = Ring([self.sb(st, "mgt", [128, 4, 512], BF16) for _ in range(2)])
            pT = Ring([self.ps(st, "mpT", [128, 8, 128], BF16) for _ in range(2)])
            pA = Ring([self.ps(st, "mpA", [128, 512], F32) for _ in range(2)])
            pB = Ring([self.ps(st, "mpB", [128, 512], F32) for _ in range(2)])
            pG = Ring([self.ps(st, "mpG", [128, 512], F32) for _ in range(2)])
            import os
            p1s = os.environ.get("P1_STOP", "")
            if p1s == "w":
                return
            xt_of = {}

            def issue_load(t):
                xt, xt_b = xts.next()
                src, srcb = self.x_src(t)
                self.dma(xt[:, :], src, xt_b, r=srcb, w=[xt_b])
                xt_of[t] = (xt, xt_b)

            xnT_of = {}

            def a1(g):
                xnT, xnT_b = xnTr.next()
                xnT_of[g] = (xnT, xnT_b)
                for tt_ in range(4):
                    t = g * 4 + tt_
                    xt, xt_b = xt_of.pop(t)
                    xn, xn_b = xnr.next()
                    s4, s4_b = str_.next()
                    self.rms_rstd(xt[:, :], xt_b, xn[:, :], xn_b, s4[:, 0:1], s4_b, s4[:, 1:2], s4_b, D,
                                  self.epsn[:, 0:1], self.epsn_b)
                    self.ts("dve", xn[:, :], xt[:, :], s4[:, 1:2], None, ALU.mult, None, r=[xt_b, s4_b], w=[xn_b])
                    p, p_b = pT.next()
                    for kc in range(8):
                        self.tr(p[:, kc, :], xn[:, kc * 128:(kc + 1) * 128], self.ident[:, :],
                                r=[xn_b, self.ident_b], w=[p_b])
                    self.cp("dve", xnT[:, :, tt_ * 128:(tt_ + 1) * 128], p[:, :, :], r=[p_b], w=[xnT_b])

            for t in range(4):
                issue_load(t)
            a1(0)
            for g in range(NG):
                if g + 1 < NG:
                    for t in range(4 * (g + 1), 4 * (g + 2)):
                        issue_load(t)
                xnT, xnT_b = xnT_of.pop(g)
                if p1s == "n":
                    return
                for tt_ in range(4):
                    t = g * 4 + tt_
                    a, a_b = pA.next()
                    b, b_b = pB.next()
                    for kc in range(8):
                        self.mm(a[:, 0:384], xnT[:, kc, tt_ * 128:(tt_ + 1) * 128], Win[:, kc, 0:384], kc == 0, kc == 7,
                                r=[xnT_b, Win_b], w=[a_b])
                    for kc in range(8):
                        self.mm(b[:, 0:320], xnT[:, kc, tt_ * 128:(tt_ + 1) * 128], Win[:, kc, 384:704], kc == 0, kc == 7,
                                r=[xnT_b, Win_b], w=[b_b])
                    s4, s4_b = str_.next()
                    cqn, cqn_b = cqn_r.next()
                    rt, rt_b = rtmp.next()
                    self.act(cqn[:, 0:384], a[:, 0:384], AF.Square, r=[a_b], w=[cqn_b, s4_b],
                             scale=float(Q_LORA ** -0.5), accum_out=s4[:, 0:1])
                    self.act(cqn[:, 384:640], b[:, 0:256], AF.Square, r=[b_b], w=[cqn_b, s4_b],
                             scale=float(KV_LORA ** -0.5), accum_out=s4[:, 1:2])
                    self.act(s4[:, 2:4], s4[:, 0:2], AF.Sqrt, r=[s4_b, self.epsn_b], w=[s4_b], bias=self.epsn[:, 0:1])
                    self.sc.add("dve", lambda e, s4=s4: e.reciprocal(s4[:, 4:6], s4[:, 2:4]), r=[s4_b], w=[s4_b])
                    self.ts("dve", cqn[:, 0:384], a[:, 0:384], s4[:, 4:5], None, ALU.mult, None,
                            r=[a_b, s4_b], w=[cqn_b])
                    self.ts("dve", cqn[:, 384:640], b[:, 0:256], s4[:, 5:6], None, ALU.mult, None,
                            r=[b_b, s4_b], w=[cqn_b])
                    self.tt("dve", rt[:, 0:64], b[:, 256:320], cc[:, t, :], ALU.mult, r=[b_b, cc_b], w=[rt_b])
                    self.tt("dve", rt[:, 64:128], b[:, 256:320], ss[:, t, :], ALU.mult, r=[b_b, ss_b], w=[rt_b])
                    self.tt("dve", cqn[:, 640:672], rt[:, 0:32], rt[:, 96:128], ALU.subtract, r=[rt_b], w=[cqn_b])
                    self.tt("dve", cqn[:, 672:704], rt[:, 32:64], rt[:, 64:96], ALU.add, r=[rt_b], w=[cqn_b])
                    p, p_b = pT.next()
                    for c in range(5):
                        self.tr(p[:, c, :], cqn[:, c * 128:(c + 1) * 128], self.ident[:, :],
                                r=[cqn_b, self.ident_b], w=[p_b])
                    self.tr(p[0:64, 5, :], cqn[:, 640:704], self.ident[:, :], r=[cqn_b, self.ident_b], w=[p_b])
                    self.cp("act", cqT[:, :, t * 128:(t + 1) * 128], p[:, 0:3, :], r=[p_b], w=[cqT_b])
                    self.cp("dve", ckvT[:, :, t * 128:(t + 1) * 128], p[:, 3:5, :], r=[p_b], w=[ckvT_b])
                    self.cp("dve", krT[0:64, t * 128:(t + 1) * 128], p[0:64, 5, :], r=[p_b], w=[krT_b])
                if p1s == "l":
                    return
                for h4 in range(4):
                    if h4 == 2 and g + 1 < NG:
                        a1(g + 1)
                    gt, gt_b = gt_r.next()
                    for hh in range(4):
                        h = h4 * 4 + hh
                        pg, pg_b = pG.next()
                        c0 = 704 + h * 128
                        for kc in range(8):
                            self.mm(pg[:, :], Win[:, kc, c0:c0 + 128], xnT[:, kc, :], kc == 0, kc == 7,
                                    r=[Win_b, xnT_b], w=[pg_b])
                        self.act(gt[:, hh, :], pg[:, :], AF.Silu, r=[pg_b], w=[gt_b])
                    self.dma(self.gateT.ap()[h4 * 4:(h4 + 1) * 4, :, g * 512:(g + 1) * 512].rearrange("h p n -> p h n"),
                             gt[:, :, :], gt_b, r=[gt_b], w=[self.gate_b[h4][g]])

    def mla_phase2(self, st0, l, j, cqT, cqT_b, ckvT, ckvT_b, krT, krT_b, c2T, c2T_b, s2T, s2T_b,
                   qg, qg_b, kvg, kvg_b):
        d = self.d
        with self.scope() as st:
            ones, ones_b = self.sb(st, "ones", [128, 128], BF16)
            self.sc.add("pool", lambda e: e.memset(ones[:, :], 1.0), w=[ones_b])
            hb = Ring([(self.sb(st, "KT", [128, S], BF16), self.sb(st, "V", [128, NT, 128], BF16),
                        self.sb(st, "QT", [128, S], BF16), self.sb(st, "QrT", [64, S], BF16)) for _ in range(2)])
            wq_r = Ring([self.sb(st, "wq", [128, 3, 256], BF16) for _ in range(2)])
            wkv_r = Ring([self.sb(st, "wkv", [128, 2, 256], BF16) for _ in range(2)])
            sq_r = Ring([self.sb(st, "sq", [128, 3, 192], F32) for _ in range(2)])
            skv_r = Ring([self.sb(st, "skv", [128, 2, 256], F32) for _ in range(2)])
            pt_r = Ring([self.sb(st, "pt", [128, 512], BF16) for _ in range(6)])
            rtm = Ring([self.sb(st, "rtm", [64, 2, 512], F32) for _ in range(2)])
            rc_r = Ring([self.sb(st, "rc", [128, 512], F32) for _ in range(2)])
            gl_r = Ring([self.sb(st, "gl", [128, 512], BF16) for _ in range(2)])
            og_r = Ring([self.sb(st, "ogs", [128, 512], BF16) for _ in range(2)])
            pw = Ring([self.ps(st, "pw", [128, 512], F32) for _ in range(4)])
            po_r = Ring([self.ps(st, "po", [128, 512], F32) for _ in range(2)])
            prs_r = Ring([self.ps(st, "prs", [128, 512], F32) for _ in range(2)])

            def prep(h):
                (KT, KT_b), (V, V_b), (QT, QT_b), (QrT, QrT_b) = hbufs[h]
                wq, wq_b = wq_r.next()
                wkv, wkv_b = wkv_r.next()
                sq, sq_b = sq_r.next()
                skv, skv_b = skv_r.next()
                self.dma(sq[:, :, :], d["mla_w_uq"].ap()[j][:, h * 192:(h + 1) * 192].rearrange("(c p) n -> p c n", p=128),
                         sq_b, w=[sq_b])
                self.dma(skv[:, :, :], d["mla_w_ukv"].ap()[j][:, h * 256:(h + 1) * 256].rearrange("(c p) n -> p c n", p=128),
                         skv_b, w=[skv_b])
                for c in range(3):
                    self.ts("pool", wq[:, c, 0:192], sq[:, c, :], qg[:, c:c + 1], None, ALU.mult, None,
                            r=[sq_b, qg_b], w=[wq_b])
                    self.ts("pool", wq[:, c, 192:224], sq[:, c, 160:192], qg[:, c:c + 1], -1.0, ALU.mult, ALU.mult,
                            r=[sq_b, qg_b], w=[wq_b])
                    self.ts("pool", wq[:, c, 224:256], sq[:, c, 128:160], qg[:, c:c + 1], None, ALU.mult, None,
                            r=[sq_b, qg_b], w=[wq_b])
                for c in range(2):
                    self.ts("pool", wkv[:, c, :], skv[:, c, :], kvg[:, c:c + 1], None, ALU.mult, None,
                            r=[skv_b, kvg_b], w=[wkv_b])
                for g in range(NG):
                    cs = slice(g * 512, (g + 1) * 512)
                    p1, p1_b = pw.next()
                    for c in range(3):
                        self.mm(p1[:, :], wq[:, c, 0:128], cqT[:, c, cs], c == 0, c == 2, r=[wq_b, cqT_b], w=[p1_b])
                    self.cp("act", QT[:, cs], p1[:, :], r=[p1_b], w=[QT_b])
                    p2, p2_b = pw.next()
                    for c in range(3):
                        self.mm(p2[0:64, :], wq[:, c, 128:192], cqT[:, c, cs], c == 0, c == 2, r=[wq_b, cqT_b], w=[p2_b])
                    p3, p3_b = pw.next()
                    for c in range(3):
                        self.mm(p3[0:64, :], wq[:, c, 192:256], cqT[:, c, cs], c == 0, c == 2, r=[wq_b, cqT_b], w=[p3_b])
                    rt, rt_b = rtm.next()
                    self.tt("dve", rt[:, 0, :], p2[0:64, :], c2T[:, cs], ALU.mult, r=[p2_b, c2T_b], w=[rt_b])
                    self.tt("dve", rt[:, 1, :], p3[0:64, :], s2T[:, cs], ALU.mult, r=[p3_b, s2T_b], w=[rt_b])
                    self.tt("pool", QrT[:, cs], rt[:, 0, :], rt[:, 1, :], ALU.add, r=[rt_b], w=[QrT_b])
                    p4, p4_b = pw.next()
                    for c in range(2):
                        self.mm(p4[:, :], wkv[:, c, 0:128], ckvT[:, c, cs], c == 0, c == 1, r=[wkv_b, ckvT_b], w=[p4_b])
                    self.cp("act", KT[:, cs], p4[:, :], r=[p4_b], w=[KT_b])
                    p5, p5_b = pw.next()
                    for tt_ in range(4):
                        t = g * 4 + tt_
                        for c in range(2):
                            self.mm(p5[:, tt_ * 128:(tt_ + 1) * 128], ckvT[:, c, t * 128:(t + 1) * 128],
                                    wkv[:, c, 128:256], c == 0, c == 1, r=[ckvT_b, wkv_b], w=[p5_b])
                    self.cp("dve", V[:, g * 4:(g + 1) * 4, :].rearrange("p t d -> p (t d)"), p5[:, :],
                            r=[p5_b], w=[V_b])

            def attention(h):
                (KT, KT_b), (V, V_b), (QT, QT_b), (QrT, QrT_b) = hbufs[h]
                blocks = []
                for jq in range(NG):
                    nkb = 4 * jq + 4
                    for kb in range(nkb):
                        blocks.append((jq, kb, nkb))
                LA = 3
                acc = {}
                info = {}
                for idx in range(len(blocks) + LA):
                    if idx < len(blocks):
                        jq, kb, nkb = blocks[idx]
                        i = kb - 4 * jq
                        off = 128 * i if i > 0 else 0
                        qs = slice(jq * 512 + off, (jq + 1) * 512)
                        ks = slice(kb * 128, (kb + 1) * 128)
                        sp_, sp_b = pw.next()
                        self.mm(sp_[:, off:512], KT[:, ks], QT[:, qs], True, False, r=[KT_b, QT_b], w=[sp_b])
                        self.mm(sp_[:, off:512], krT[0:64, ks], QrT[0:64, qs], False, True, r=[krT_b, QrT_b], w=[sp_b])
                        pt, pt_b = pt_r.next()
                        self.act(pt[:, off:512], sp_[:, off:512], AF.Exp, r=[sp_b], w=[pt_b], scale=float(ATTN_SCALE))
                        if i >= 0:
                            self.tt("pool", pt[:, off:off + 128], pt[:, off:off + 128], self.tri[:, :], ALU.mult,
                                    r=[pt_b, self.tri_b], w=[pt_b])
                        info[idx] = (pt, pt_b, off)
                    k = idx - LA
                    if k >= 0:
                        jq, kb, nkb = blocks[k]
                        pt, pt_b, off = info.pop(k)
                        if kb == 0:
                            acc[jq] = (po_r.next(), prs_r.next())
                        (po, po_b), (prs, prs_b) = acc[jq]
                        first = (kb == 0)
                        lastb = (kb == nkb - 1)
                        self.mm(po[:, off:512], V[:, kb, :], pt[:, off:512], first, lastb, r=[V_b, pt_b], w=[po_b])
                        self.mm(prs[:, off:512], ones[:, :], pt[:, off:512], first, lastb, r=[ones_b, pt_b], w=[prs_b])
                        if lastb:
                            rc, rc_b = rc_r.next()
                            gl, gl_b = gl_r.next()
                            og, og_b = og_r.next()
                            cs = slice(jq * 512, (jq + 1) * 512)
                            self.dma(gl[:, :], self.gateT.ap()[h, :, cs], gl_b, r=[self.gate_b[h // 4][jq]], w=[gl_b])
                            self.sc.add("dve", lambda e, rc=rc, prs=prs: e.reciprocal(rc[:, :], prs[:, :]),
                                        r=[prs_b], w=[rc_b])
                            self.tt("dve", rc[:, :], po[:, :], rc[:, :], ALU.mult, r=[po_b, rc_b], w=[rc_b])
                            self.tt("pool", og[:, :], rc[:, :], gl[:, :], ALU.mult, r=[rc_b, gl_b], w=[og_b])
                            self.dma(self.ogT.ap()[h, :, cs], og[:, :], og_b, r=[og_b], w=[self.og_b[h][jq]])
                            del acc[jq]

            hbufs = {}
            hbufs[0] = hb.next()
            prep(0)
            for h in range(HEADS):
                if h + 1 < HEADS:
                    hbufs[h + 1] = hb.next()
                    prep(h + 1)
                attention(h)
                del hbufs[h]

    def mla_phase3(self, l, j, mode):
        d = self.d
        final = (mode == 'final')
        with self.scope() as st:
            Wo, Wo_b = self.sb(st, "mWo", [128, 16, D], BF16)
            with self.scope() as wst:
                stg = Ring([self.sb(wst, "mostg", [128, D], F32) for _ in range(3)])
                self.load_weight_rows(stg, Wo, Wo_b, d["mla_w_o"].ap()[j], None, None, 16, D, D)
            if final:
                Gf, Gf_b = self.sb(st, "mGf", [128, D], F32)
                self.dma(Gf[:, :], bass.AP(d["final_g"], 0, [[0, 128], [1, D]]), Gf_b, w=[Gf_b])
                junk, junk_b = self.sb(st, "mjunk", [128, D], BF16)
                str_ = Ring([self.sb(st, "m3st", [128, 8], F32) for _ in range(3)])
            ogr = Ring([self.sb(st, "og3", [128, 16, 512], BF16) for _ in range(2)])
            xts = Ring([self.sb(st, "m3xt", [128, D], F32) for _ in range(3)])
            py = Ring([self.ps(st, "m3py", [128, 512], F32) for _ in range(4)])
            for g in range(NG):
                og, og_b = ogr.next()
                cs = slice(g * 512, (g + 1) * 512)
                for q in range(4):
                    self.dma(og[:, q * 4:(q + 1) * 4, :],
                             self.ogT.ap()[q * 4:(q + 1) * 4, :, cs].rearrange("h p n -> p h n"), og_b,
                             r=[self.og_b[q * 4 + hh][g] for hh in range(4)], w=[og_b])
                for tt_ in range(4):
                    t = g * 4 + tt_
                    xt, xt_b = xts.next()
                    src, srcb = self.x_src(t)
                    self.dma(xt[:, :], src, xt_b, r=srcb, w=[xt_b])
                    for hf in range(2):
                        y, y_b = py.next()
                        for h in range(16):
                            self.mm(y[:, :], og[:, h, tt_ * 128:(tt_ + 1) * 128], Wo[:, h, hf * 512:(hf + 1) * 512],
                                    h == 0, h == 15, r=[og_b, Wo_b], w=[y_b])
                        self.tt("dve", xt[:, hf * 512:(hf + 1) * 512], y[:, :], xt[:, hf * 512:(hf + 1) * 512], ALU.add,
                                r=[y_b, xt_b], w=[xt_b])
                    if final:
                        self.final_norm_store(t, xt, xt_b, junk, junk_b, str_, Gf, Gf_b)
                    elif mode == 'out':
                        self.dma(self.out.ap()[t * 128:(t + 1) * 128, :], xt[:, :], xt_b, r=[xt_b],
                                 w=[self.out_b[t]])
                    else:
                        self.dma(self.xres.ap()[t * 128:(t + 1) * 128, :], xt[:, :], xt_b, r=[xt_b],
                                 w=[self.xres_b[t]])


def _host_inputs(inp, b):
    f32 = np.float32
    pos = np.ascontiguousarray(inp["positions"][b]).astype(np.int32)
    inv_freq = (1.0 / (10000.0 ** (np.arange(0, ROPE, 2, dtype=np.float32) / ROPE))).astype(f32)
    m = {}
    m["x"] = np.ascontiguousarray(inp["x"][b], dtype=f32)
    m["pos_tok"] = np.ascontiguousarray(pos.reshape(NT, 128).T)
    m["pos_row"] = np.ascontiguousarray(pos.reshape(1, S))
    m["invf_tok"] = np.ascontiguousarray(np.broadcast_to(np.concatenate([inv_freq, inv_freq])[None, :], (128, 64)), dtype=f32)
    m["invf_col"] = np.ascontiguousarray(np.concatenate([inv_freq, inv_freq]).reshape(64, 1), dtype=f32)
    m["ident"] = np.eye(128, dtype=f32).astype(ml_dtypes.bfloat16)
    m["tri"] = np.triu(np.ones((128, 128), dtype=f32)).astype(ml_dtypes.bfloat16)
    m["norm_g"] = np.ascontiguousarray(np.asarray(inp["norm_g"], f32).reshape(DEPTH, 8, 128).transpose(0, 2, 1))
    m["final_g"] = np.ascontiguousarray(np.asarray(inp["final_g"], f32).reshape(1, D))
    m["mla_w_in"] = np.ascontiguousarray(inp["mla_w_in"], dtype=f32)
    m["mla_qg"] = np.ascontiguousarray(np.asarray(inp["mla_q_norm_g"], f32).reshape(2, 3, 128).transpose(0, 2, 1))
    m["mla_kvg"] = np.ascontiguousarray(np.asarray(inp["mla_kv_norm_g"], f32).reshape(2, 2, 128).transpose(0, 2, 1))
    m["mla_w_uq"] = np.ascontiguousarray(inp["mla_w_uq"], dtype=f32)
    m["mla_w_ukv"] = np.ascontiguousarray(inp["mla_w_ukv"], dtype=f32)
    m["mla_w_o"] = np.ascontiguousarray(inp["mla_w_o"], dtype=f32)
    m["sgu_w_in"] = np.ascontiguousarray(inp["sgu_w_in"], dtype=f32)
    m["sgu_ln_g"] = np.ascontiguousarray(np.asarray(inp["sgu_ln_g"], f32).reshape(2, 1, BW))
    m["sgu_ln_b"] = np.ascontiguousarray(np.asarray(inp["sgu_ln_b"], f32).reshape(2, 1, BW))
    m["sgu_w_sT"] = np.ascontiguousarray(np.asarray(inp["sgu_w_s"], f32).transpose(0, 3, 1, 2))
    m["sgu_b_sT"] = np.ascontiguousarray(np.asarray(inp["sgu_b_s"], f32).transpose(0, 2, 1))
    m["sgu_w_o"] = np.ascontiguousarray(inp["sgu_w_o"], dtype=f32)
    return m


_CACHE = {}
_LAYERED = ["mla_w_in", "mla_qg", "mla_kvg", "mla_w_uq", "mla_w_ukv", "mla_w_o", "sgu_w_in", "sgu_ln_g", "sgu_ln_b",
            "sgu_w_sT", "sgu_b_sT", "sgu_w_o"]


def get_prog(layers, first_from_x=True, do_final=True, part=None, slot=False):
    key = (tuple(layers), first_from_x, do_final, part, slot)
    if key not in _CACHE:
        p = Prog(list(layers), first_from_x, do_final, part, slot)
        p.build()
        _CACHE[key] = p
    return _CACHE[key]


def _slot_map(m, l):
    j = l // 2
    m2 = dict(m)
    for k in _LAYERED:
        m2[k] = np.ascontiguousarray(m[k][j:j + 1])
    m2["norm_g"] = np.ascontiguousarray(m["norm_g"][l:l + 1])
    return m2


def _launch(prog, maps):
    res = run_bass_kernel_spmd(prog.nc, maps, core_ids=list(range(len(maps))))
    return res.results


def kernel_unfused(**inputs):
    inp = {k: np.asarray(v) for k, v in inputs.items()}
    B = inp["x"].shape[0]
    base = _host_inputs(inp, 0)
    per_core = []
    for b in range(B):
        m = dict(base)
        pos = np.ascontiguousarray(inp["positions"][b]).astype(np.int32)
        m["pos_tok"] = np.ascontiguousarray(pos.reshape(NT, 128).T)
        m["pos_row"] = np.ascontiguousarray(pos.reshape(1, S))
        per_core.append(m)
    xs = [np.ascontiguousarray(inp["x"][b], dtype=np.float32) for b in range(B)]
    for l in range(DEPTH):
        last = (l == DEPTH - 1)
        if l % 2 == 0:
            pa = get_prog([l % 2], True, False, "A", True)
            maps = [dict(_slot_map(per_core[b], l), x=xs[b]) for b in range(B)]
            ra = _launch(pa, maps)
            pb = get_prog([l % 2], True, last, "B", True)
            maps = [dict(maps[b], ogT=np.asarray(ra[b]["ogT"])) for b in range(B)]
            rb = _launch(pb, maps)
            xs = [np.asarray(rb[b]["out"], dtype=np.float32) for b in range(B)]
        else:
            ps_ = get_prog([l % 2], True, last, None, True)
            maps = [dict(_slot_map(per_core[b], l), x=xs[b]) for b in range(B)]
            r = _launch(ps_, maps)
            xs = [np.asarray(r[b]["out"], dtype=np.float32) for b in range(B)]
    return np.stack(xs, axis=0)


def kernel(**inputs):
    inp = {k: np.asarray(v) for k, v in inputs.items()}
    B = inp["x"].shape[0]
    base = _host_inputs(inp, 0)
    maps = []
    for b in range(B):
        m = dict(base)
        pos = np.ascontiguousarray(inp["positions"][b]).astype(np.int32)
        m["pos_tok"] = np.ascontiguousarray(pos.reshape(NT, 128).T)
        m["pos_row"] = np.ascontiguousarray(pos.reshape(1, S))
        m["x"] = np.ascontiguousarray(inp["x"][b], dtype=np.float32)
        maps.append(m)
    prog = get_prog(range(DEPTH), True, True, None, False)
    res = _launch(prog, maps)
    return np.stack([np.asarray(res[b]["out"], dtype=np.float32) for b in range(B)], axis=0)
```

```python
import math
from contextlib import ExitStack, contextmanager

import numpy as np
import ml_dtypes

import concourse.bass as bass
import concourse.mybir as mybir
from concourse.bass_utils import run_bass_kernel_spmd

F32 = mybir.dt.float32
BF16 = mybir.dt.bfloat16
I32 = mybir.dt.int32
AF = mybir.ActivationFunctionType
ALU = mybir.AluOpType

S = 4096
D = 1024
NT = S // 128
NG = S // 512
DEPTH = 4
HEADS = 16
BW = 2048
Q_LORA = 384
KV_LORA = 256
ROPE = 64
MLA_IN_W = Q_LORA + KV_LORA + ROPE + BW
SGU_IN_W = 3 * BW
EPS = 1e-6
LN_EPS = 1e-5
ATTN_SCALE = (128 + 64) ** -0.5
TWO_PI = 2.0 * math.pi


class Buf:
    __slots__ = ("name", "lw", "rd", "psum")

    def __init__(self, name, psum=False):
        self.name = name
        self.lw = None
        self.rd = []
        self.psum = psum


class Sched:
    STREAMS = ("pe", "act", "dve", "pool", "sp")

    def __init__(self):
        self.ops = []

    def add(self, eng, fn, r=(), w=(), dma=None):
        self.ops.append((eng, fn, tuple(r), tuple(w), dma))

    def barrier(self):
        self.ops.append(("bar", None, (), (), None))

    def emit(self, nc, stack):
        ops = self.ops
        n = len(ops)
        deps = [None] * n
        last_dma = {}
        last_op = {}
        pending = {}
        for i, (eng, fn, r, w, dma) in enumerate(ops):
            if eng == "bar":
                deps[i] = []
                snap = set(last_op.values()) | set(last_dma.values())
                for s_ in self.STREAMS:
                    pending[s_] = snap
                continue
            kinds = {}
            for b in r:
                if b.lw is not None:
                    kinds.setdefault(b.lw, set()).add("raw")
                if b.psum:
                    for j in b.rd:
                        if ops[j][0] != eng:
                            kinds.setdefault(j, set()).add("rar")
            for b in w:
                if b.lw is not None:
                    kinds.setdefault(b.lw, set()).add("waw")
                for j in b.rd:
                    kinds.setdefault(j, set()).add("war")
            if dma is not None and dma in last_dma:
                kinds.setdefault(last_dma[dma], set()).add("raw")
            dd = []
            for j, ks in kinds.items():
                if j == i:
                    continue
                ej, dj = ops[j][0], ops[j][4]
                if dj is None and dma is None and ej == eng:
                    if eng == "pe" or "raw" not in ks:
                        continue
                dd.append(j)
            if eng in pending:
                for j in pending.pop(eng):
                    if ops[j][0] == eng and ops[j][4] is None:
                        continue
                    dd.append(j)
            if fn is not None:
                last_op[eng] = i
            best = {}
            for j in dd:
                k_ = ("d", ops[j][4]) if ops[j][4] is not None else ("e", ops[j][0])
                if k_ not in best or best[k_] < j:
                    best[k_] = j
            dd = list(best.values())
            deps[i] = dd
            for b in r:
                b.rd.append(i)
            for b in w:
                b.lw = i
                b.rd = []
            if dma is not None:
                last_dma[dma] = i
        needs = [False] * n
        for dd in deps:
            for j in dd:
                needs[j] = True
        sems = {}

        def get_sem(key):
            if key not in sems:
                sems[key] = stack.enter_context(nc.semaphore("s_%s" % key))
            return sems[key]

        token = [None] * n
        tick = {}
        for i, (eng, fn, r, w, dma) in enumerate(ops):
            if fn is None:
                continue
            if dma is not None:
                key = "d_" + dma.name
                tick[key] = tick.get(key, 0) + 16
                token[i] = (key, tick[key], 16)
            elif needs[i]:
                key = "e_" + eng
                tick[key] = tick.get(key, 0) + 1
                token[i] = (key, tick[key], 1)
        for key in tick:
            get_sem(key)
        self.n_sems = len(sems)
        self.max_tick = max(tick.values()) if tick else 0
        by_stream = {s: [] for s in self.STREAMS}
        for i, op in enumerate(ops):
            if op[0] != "bar":
                by_stream[op[0]].append(i)

        def simulate():
            val = {k: 0 for k in sems}
            pos = {s_: 0 for s_ in self.STREAMS}
            progress = True
            while progress:
                progress = False
                for s_ in self.STREAMS:
                    lst = by_stream[s_]
                    while pos[s_] < len(lst):
                        i = lst[pos[s_]]
                        ok = all(val[token[j][0]] >= token[j][1] for j in deps[i])
                        if not ok:
                            break
                        if token[i] is not None:
                            val[token[i][0]] += token[i][2]
                        pos[s_] += 1
                        progress = True
            stuck = {s_: (pos[s_], len(by_stream[s_])) for s_ in self.STREAMS if pos[s_] < len(by_stream[s_])}
            for s_, (p_, n_) in stuck.items():
                i = by_stream[s_][p_]
                print("STUCK", s_, p_, n_, "op", i, [(j, ops[j][0], token[j], val[token[j][0]]) for j in deps[i]
                                                      if val[token[j][0]] < token[j][1]])
            return not stuck

        self.sim_ok = simulate()

        def run_stream(e, stream):
            waited = {}
            for i in by_stream[stream]:
                eng, fn, r, w, dma = ops[i]
                need = {}
                for j in deps[i]:
                    key, val, _ = token[j]
                    if need.get(key, 0) < val:
                        need[key] = val
                for key, val in need.items():
                    if waited.get(key, 0) < val:
                        e.wait_ge(sems[key], val)
                        waited[key] = val
                if fn is None:
                    continue
                inst = fn(e)
                if token[i] is not None:
                    key, val, inc = token[i]
                    inst.then_inc(sems[key], inc)

        with nc.Block() as block:
            @block.tensor
            def _(e):
                run_stream(e, "pe")

            @block.scalar
            def _(e):
                run_stream(e, "act")

            @block.vector
            def _(e):
                run_stream(e, "dve")

            @block.gpsimd
            def _(e):
                run_stream(e, "pool")

            @block.sync
            def _(e):
                run_stream(e, "sp")


class Ring:
    def __init__(self, items):
        self.items = items
        self.i = 0

    def next(self):
        it = self.items[self.i % len(self.items)]
        self.i += 1
        return it


class Prog:
    def __init__(self, layers, first_from_x=True, do_final=True, part=None, slot=False):
        self.part = part
        self.slot = slot
        self.layers = layers
        self.do_final = do_final
        self.first_from_x = first_from_x
        self.nc = bass.Bass("TRN2", target_bir_lowering=False)
        self.sc = Sched()
        self.uid = 0

    @contextmanager
    def scope(self):
        with ExitStack() as st:
            yield st
        self.sc.barrier()

    def dram(self, name, shape, dt, kind):
        return self.nc.dram_tensor(name, list(shape), dt, kind=kind)

    def sb(self, st, name, shape, dt):
        self.uid += 1
        t = st.enter_context(self.nc.sbuf_tensor("%s_%d" % (name, self.uid), list(shape), dt))
        return t, Buf("%s_%d" % (name, self.uid))

    def ps(self, st, name, shape, dt):
        self.uid += 1
        t = st.enter_context(self.nc.psum_tensor("%s_%d" % (name, self.uid), list(shape), dt))
        return t, Buf("%s_%d" % (name, self.uid), psum=True)

    def dma(self, out, in_, key, r=(), w=(), eng="sp"):
        self.sc.add(eng, lambda e: e.dma_start(out=out, in_=in_), r=r, w=w, dma=key)

    def mm(self, out, lhsT, rhs, start, stop, r, w):
        self.sc.add("pe", lambda e: e.matmul(out, lhsT, rhs, start=start, stop=stop), r=r, w=w)

    def tr(self, out, in_, ident, r, w):
        self.sc.add("pe", lambda e: e.transpose(out, in_, ident), r=r, w=w)

    def act(self, out, in_, func, r, w, bias=None, scale=None, accum_out=None):
        kw = {}
        if bias is not None:
            kw["bias"] = bias
        if scale is not None:
            kw["scale"] = scale
        if accum_out is not None:
            kw["accum_out"] = accum_out
        self.sc.add("act", lambda e: e.activation(out, in_, func, **kw), r=r, w=w)

    def ts(self, eng, out, in0, s1, s2, op0, op1, r, w):
        if op1 is None:
            self.sc.add(eng, lambda e: e.tensor_scalar(out, in0, s1, None, op0), r=r, w=w)
        else:
            self.sc.add(eng, lambda e: e.tensor_scalar(out, in0, s1, s2, op0, op1), r=r, w=w)

    def tt(self, eng, out, in0, in1, op, r, w):
        self.sc.add(eng, lambda e: e.tensor_tensor(out, in0, in1, op), r=r, w=w)

    def stt(self, out, in0, scalar, in1, op0, op1, r, w):
        self.sc.add("dve", lambda e: e.scalar_tensor_tensor(out, in0, scalar, in1, op0, op1), r=r, w=w)

    def cp(self, eng, out, in_, r, w):
        if eng == "act":
            self.sc.add("act", lambda e: e.activation(out, in_, AF.Copy), r=r, w=w)
        else:
            self.sc.add(eng, lambda e: e.tensor_copy(out, in_), r=r, w=w)

    def build(self):
        nc = self.nc
        self.x_in = self.dram("x", [S, D], F32, "ExternalInput")
        self.out = self.dram("out", [S, D], F32, "ExternalOutput")
        self.xres = self.dram("xres", [S, D], F32, "Internal")
        self.gateT = self.dram("gateT", [HEADS, 128, S], BF16, "Internal")
        og_kind = {None: "Internal", "A": "ExternalOutput", "B": "ExternalInput"}[self.part]
        self.ogT = self.dram("ogT", [HEADS, 128, S], BF16, og_kind)
        N2 = 1 if self.slot else 2
        N4 = 1 if self.slot else DEPTH
        d = {}
        d["pos_tok"] = self.dram("pos_tok", [128, NT], I32, "ExternalInput")
        d["pos_row"] = self.dram("pos_row", [1, S], I32, "ExternalInput")
        d["invf_tok"] = self.dram("invf_tok", [128, 64], F32, "ExternalInput")
        d["invf_col"] = self.dram("invf_col", [64, 1], F32, "ExternalInput")
        d["ident"] = self.dram("ident", [128, 128], BF16, "ExternalInput")
        d["tri"] = self.dram("tri", [128, 128], BF16, "ExternalInput")
        d["norm_g"] = self.dram("norm_g", [N4, 128, 8], F32, "ExternalInput")
        d["final_g"] = self.dram("final_g", [1, D], F32, "ExternalInput")
        d["mla_w_in"] = self.dram("mla_w_in", [N2, D, MLA_IN_W], F32, "ExternalInput")
        d["mla_qg"] = self.dram("mla_qg", [N2, 128, 3], F32, "ExternalInput")
        d["mla_kvg"] = self.dram("mla_kvg", [N2, 128, 2], F32, "ExternalInput")
        d["mla_w_uq"] = self.dram("mla_w_uq", [N2, Q_LORA, HEADS * 192], F32, "ExternalInput")
        d["mla_w_ukv"] = self.dram("mla_w_ukv", [N2, KV_LORA, HEADS * 256], F32, "ExternalInput")
        d["mla_w_o"] = self.dram("mla_w_o", [N2, BW, D], F32, "ExternalInput")
        d["sgu_w_in"] = self.dram("sgu_w_in", [N2, D, SGU_IN_W], F32, "ExternalInput")
        d["sgu_ln_g"] = self.dram("sgu_ln_g", [N2, 1, BW], F32, "ExternalInput")
        d["sgu_ln_b"] = self.dram("sgu_ln_b", [N2, 1, BW], F32, "ExternalInput")
        d["sgu_w_sT"] = self.dram("sgu_w_sT", [N2, 128, 16, 128], F32, "ExternalInput")
        d["sgu_b_sT"] = self.dram("sgu_b_sT", [N2, 128, 16], F32, "ExternalInput")
        d["sgu_w_o"] = self.dram("sgu_w_o", [N2, BW, D], F32, "ExternalInput")
        self.d = d
        self.xres_b = [Buf("xres%d" % t) for t in range(NT)]
        self.out_b = [Buf("out%d" % t) for t in range(NT)]
        self.gate_b = [[Buf("gate%d_%d" % (h4, g)) for g in range(NG)] for h4 in range(4)]
        self.og_b = [[Buf("og%d_%d" % (h, g)) for g in range(NG)] for h in range(HEADS)]
        self.x_from_input = self.first_from_x

        with ExitStack() as top:
            self.ident, self.ident_b = self.sb(top, "ident", [128, 128], BF16)
            self.tri, self.tri_b = self.sb(top, "tri", [128, 128], BF16)
            self.epsn, self.epsn_b = self.sb(top, "epsn", [128, 1], F32)
            self.epsl, self.epsl_b = self.sb(top, "epsl", [128, 1], F32)
            self.dma(self.ident[:, :], d["ident"].ap(), self.ident_b, w=[self.ident_b])
            self.dma(self.tri[:, :], d["tri"].ap(), self.tri_b, w=[self.tri_b])
            self.sc.add("pool", lambda e: e.memset(self.epsn[:, :], EPS), w=[self.epsn_b])
            self.sc.add("pool", lambda e: e.memset(self.epsl[:, :], LN_EPS), w=[self.epsl_b])
            for li, l in enumerate(self.layers):
                last = (li == len(self.layers) - 1)
                mode = "res" if not last else ("final" if self.do_final else "out")
                ll, jj = (0, 0) if self.slot else (l, l // 2)
                if l % 2 == 0:
                    self.mla_layer(ll, jj, mode)
                else:
                    self.sgu_layer(ll, jj, mode)
            self.sc.add("sp", None, r=self.out_b)
            if self.part == "A":
                self.sc.add("sp", None, r=[b for row in self.og_b for b in row])
            self.sc.emit(nc, top)
        return nc

    def x_src(self, t):
        if self.x_from_input:
            return self.x_in.ap()[t * 128:(t + 1) * 128, :], []
        return self.xres.ap()[t * 128:(t + 1) * 128, :], [self.xres_b[t]]

    def rms_rstd(self, xt, xt_b, junk, junk_b, ss, ss_b, rstd, rstd_b, ncols, eps_t, eps_b):
        self.act(junk, xt, AF.Square, r=[xt_b], w=[junk_b, ss_b], scale=float(ncols ** -0.5), accum_out=ss)
        self.act(ss, ss, AF.Sqrt, r=[ss_b, eps_b], w=[ss_b], bias=eps_t)
        self.sc.add("dve", lambda e: e.reciprocal(rstd, ss), r=[ss_b], w=[rstd_b])

    def load_weight_rows(self, st_ring, dst, dst_b, src_ap, gcol, gcol_b, nk, ncols, piece):
        for kc in range(nk):
            for c0 in range(0, ncols, piece):
                c1 = min(ncols, c0 + piece)
                stg, stg_b = st_ring.next()
                self.dma(stg[:, 0:c1 - c0], src_ap[kc * 128:(kc + 1) * 128, c0:c1], stg_b, w=[stg_b])
                self._wl = getattr(self, "_wl", 0) + 1
                eng = "dve" if self._wl % 2 == 0 else "act"
                if gcol is None:
                    self.cp(eng, dst[:, kc, c0:c1], stg[:, 0:c1 - c0], r=[stg_b], w=[dst_b])
                elif eng == "dve":
                    self.ts("dve", dst[:, kc, c0:c1], stg[:, 0:c1 - c0], gcol[:, kc:kc + 1], None,
                            ALU.mult, None, r=[stg_b, gcol_b], w=[dst_b])
                else:
                    self.act(dst[:, kc, c0:c1], stg[:, 0:c1 - c0], AF.Copy, r=[stg_b, gcol_b], w=[dst_b],
                             scale=gcol[:, kc:kc + 1])

    def sgu_layer(self, l, j, mode):
        d = self.d
        final = (mode == 'final')
        with self.scope() as st:
            Win, Win_b = self.sb(st, "sWin", [128, 8, SGU_IN_W], BF16)
            Wo, Wo_b = self.sb(st, "sWo", [128, 16, D], BF16)
            WsT, WsT_b = self.sb(st, "sWsT", [128, 16, 128], BF16)
            Gb, Gb_b = self.sb(st, "sG", [128, BW], F32)
            Bb, Bb_b = self.sb(st, "sB", [128, BW], F32)
            bsT, bsT_b = self.sb(st, "sbs", [128, 16], F32)
            gcol, gcol_b = self.sb(st, "sgcol", [128, 8], F32)
            if final:
                Gf, Gf_b = self.sb(st, "sGf", [128, D], F32)
                self.dma(Gf[:, :], bass.AP(d["final_g"], 0, [[0, 128], [1, D]]), Gf_b, w=[Gf_b])
            self.dma(gcol[:, :], d["norm_g"].ap()[l], gcol_b, w=[gcol_b])
            self.dma(bsT[:, :], d["sgu_b_sT"].ap()[j], bsT_b, w=[bsT_b])
            self.dma(Gb[:, :], bass.AP(d["sgu_ln_g"], j * BW, [[0, 128], [1, BW]]), Gb_b, w=[Gb_b])
            self.dma(Bb[:, :], bass.AP(d["sgu_ln_b"], j * BW, [[0, 128], [1, BW]]), Bb_b, w=[Bb_b])
            with self.scope() as wst:
                stg = Ring([self.sb(wst, "sstg", [128, 2048], F32) for _ in range(4)])
                self.load_weight_rows(stg, Win, Win_b, d["sgu_w_in"].ap()[j], gcol, gcol_b, 8, SGU_IN_W, 2048)
                self.load_weight_rows(stg, Wo, Wo_b, d["sgu_w_o"].ap()[j], None, None, 16, D, 2048)
                sw, sw_b = stg.next()
                self.dma(sw[:, :], d["sgu_w_sT"].ap()[j].rearrange("s g t -> s (g t)"), sw_b, w=[sw_b])
                for g in range(16):
                    self.tt("pool", WsT[:, g, :], sw[:, g * 128:(g + 1) * 128], self.tri[:, :], ALU.mult,
                            r=[sw_b, self.tri_b], w=[WsT_b])

            xts = Ring([self.sb(st, "sxt", [128, D], F32) for _ in range(3)])
            xnr = Ring([self.sb(st, "sxn", [128, D], BF16) for _ in range(1)])
            xnTr = Ring([self.sb(st, "sxnT", [128, 8, 128], BF16) for _ in range(2)])
            gvr = Ring([self.sb(st, "sgv", [128, BW], F32) for _ in range(1)])
            sgr = Ring([self.sb(st, "ssg", [128, BW], BF16) for _ in range(1)])
            ur = Ring([self.sb(st, "su", [128, BW], BF16) for _ in range(2)])
            vlr = Ring([self.sb(st, "svl", [128, BW], BF16) for _ in range(2)])
            prr = Ring([self.sb(st, "spr", [128, BW], BF16) for _ in range(1)])
            prTr = Ring([self.sb(st, "sprT", [128, 16, 128], BF16) for _ in range(1)])
            str_ = Ring([self.sb(st, "sst", [128, 16], F32) for _ in range(3)])
            pT = Ring([self.ps(st, "spT", [128, 8, 128], BF16) for _ in range(2)])
            pm = Ring([self.ps(st, "spm", [128, 512], F32) for _ in range(3)])
            psp = Ring([self.ps(st, "spsp", [128, 512], F32) for _ in range(1)])
            py = [self.ps(st, "spy", [128, 512], F32) for _ in range(2)]

            state = {}
            order = [4, 5, 6, 7, 0, 1, 2, 3, 8, 9, 10, 11]

            def a1a(t):
                xt, xt_b = xts.next()
                xn, xn_b = xnr.next()
                stt_, st_b = str_.next()
                src, srcb = self.x_src(t)
                self.dma(xt[:, :], src, xt_b, r=srcb, w=[xt_b])
                state[t] = dict(xt=(xt, xt_b), xn=(xn, xn_b), st=(stt_, st_b))

            def a1n(t):
                xt, xt_b = state[t]["xt"]
                xn, xn_b = state[t]["xn"]
                stt_, st_b = state[t]["st"]
                self.rms_rstd(xt[:, :], xt_b, xn[:, :], xn_b, stt_[:, 0:1], st_b, stt_[:, 1:2], st_b, D,
                              self.epsn[:, 0:1], self.epsn_b)
                self.ts("dve", xn[:, :], xt[:, :], stt_[:, 1:2], None, ALU.mult, None, r=[xt_b, st_b], w=[xn_b])

            def a1b(t):
                xn, xn_b = state[t]["xn"]
                xnT, xnT_b = xnTr.next()
                p, p_b = pT.next()
                for kc in range(8):
                    self.tr(p[:, kc, :], xn[:, kc * 128:(kc + 1) * 128], self.ident[:, :],
                            r=[xn_b, self.ident_b], w=[p_b])
                self.cp("dve", xnT[:, :, :], p[:, :, :], r=[p_b], w=[xnT_b])
                state[t]["xnT"] = (xnT, xnT_b)
                state[t]["gv"] = gvr.next()
                state[t]["u"] = ur.next()
                state[t]["sg"] = sgr.next()
                state[t]["vl"] = vlr.next()

            def a_group(t, idx):
                stt = state[t]
                xnT, xnT_b = stt["xnT"]
                gv, gv_b = stt["gv"]
                u, u_b = stt["u"]
                sg, sg_b = stt["sg"]
                vl, vl_b = stt["vl"]
                nb = order[idx]
                pp, pp_b = pm.next()
                for kc in range(8):
                    self.mm(pp[:, :], xnT[:, kc, :], Win[:, kc, nb * 512:(nb + 1) * 512], kc == 0, kc == 7,
                            r=[xnT_b, Win_b], w=[pp_b])
                if nb < 4:
                    self.act(u[:, nb * 512:(nb + 1) * 512], pp[:, :], AF.Gelu, r=[pp_b], w=[u_b])
                elif nb < 8:
                    c = nb - 4
                    self.act(gv[:, c * 512:(c + 1) * 512], pp[:, :], AF.Gelu, r=[pp_b], w=[gv_b])
                else:
                    c = nb - 8
                    self.act(sg[:, c * 512:(c + 1) * 512], pp[:, :], AF.Silu, r=[pp_b], w=[sg_b])
                if nb == 7:
                    self.ln_chain(gv, gv_b, vl, vl_b, Gb, Gb_b, Bb, Bb_b, st)
                if idx == 11:
                    self.tt("pool", u[:, :], u[:, :], sg[:, :], ALU.mult, r=[u_b, sg_b], w=[u_b])

            def b1(t, q):
                stt = state[t]
                u, u_b = stt["u"]
                vl, vl_b = stt["vl"]
                if q == 0:
                    stt["pr"] = prr.next()
                    stt["prT"] = prTr.next()
                pr, pr_b = stt["pr"]
                sp, sp_b = psp.next()
                for gg in range(4):
                    g = q * 4 + gg
                    self.mm(sp[:, gg * 128:(gg + 1) * 128], WsT[:, g, :], vl[:, g * 128:(g + 1) * 128],
                            True, True, r=[WsT_b, vl_b], w=[sp_b])
                for gg in range(4):
                    g = q * 4 + gg
                    self.stt(pr[:, g * 128:(g + 1) * 128], sp[:, gg * 128:(gg + 1) * 128], bsT[:, g:g + 1],
                             u[:, g * 128:(g + 1) * 128], ALU.add, ALU.mult,
                             r=[sp_b, bsT_b, u_b], w=[pr_b])

            def b2(t, hf):
                stt = state[t]
                pr, pr_b = stt["pr"]
                prT, prT_b = stt["prT"]
                p, p_b = pT.next()
                for kk in range(8):
                    kc = hf * 8 + kk
                    self.tr(p[:, kk, :], pr[:, kc * 128:(kc + 1) * 128], self.ident[:, :],
                            r=[pr_b, self.ident_b], w=[p_b])
                self.cp("act" if hf == 0 else "dve", prT[:, hf * 8:(hf + 1) * 8, :], p[:, :, :],
                        r=[p_b], w=[prT_b])

            def b3(t, hf):
                stt = state[t]
                xt, xt_b = stt["xt"]
                pr, pr_b = stt["pr"]
                prT, prT_b = stt["prT"]
                y, y_b = py[hf]
                for kc in range(16):
                    self.mm(y[:, :], prT[:, kc, :], Wo[:, kc, hf * 512:(hf + 1) * 512], kc == 0, kc == 15,
                            r=[prT_b, Wo_b], w=[y_b])
                self.tt("dve", xt[:, hf * 512:(hf + 1) * 512], y[:, :], xt[:, hf * 512:(hf + 1) * 512], ALU.add,
                        r=[y_b, xt_b], w=[xt_b])
                if hf == 1:
                    if final:
                        self.final_norm_store(t, xt, xt_b, pr, pr_b, str_, Gf, Gf_b)
                    elif mode == 'out':
                        self.dma(self.out.ap()[t * 128:(t + 1) * 128, :], xt[:, :], xt_b, r=[xt_b], w=[self.out_b[t]])
                    else:
                        self.dma(self.xres.ap()[t * 128:(t + 1) * 128, :], xt[:, :], xt_b, r=[xt_b], w=[self.xres_b[t]])
                    del state[t]

            self._sgu_ln_tmp = Ring([self.sb(st, "slt", [128, 8], F32) for _ in range(2)])
            self._lnst = Ring([self.sb(st, "slnst", [128, 24], F32) for _ in range(2)])
            pieces = {4: ("b1", 0), 5: ("b1", 1), 6: ("b1", 2), 7: ("b1", 3), 8: ("b2", 0), 9: ("b2", 1),
                      10: ("b3", 0), 11: ("b3", 1)}
            a1a(0)
            a1n(0)
            a1b(0)
            for t in range(NT + 1):
                if t + 1 < NT:
                    a1a(t + 1)
                for idx in range(12):
                    if t < NT:
                        a_group(t, idx)
                    if t >= 1 and idx in pieces:
                        kind, arg = pieces[idx]
                        {"b1": b1, "b2": b2, "b3": b3}[kind](t - 1, arg)
                    if idx == 2 and t + 1 < NT:
                        a1n(t + 1)
                    if idx == 9 and t + 1 < NT:
                        a1b(t + 1)
        self.x_from_input = False

    def ln_chain(self, gv, gv_b, vl, vl_b, Gb, Gb_b, Bb, Bb_b, st):
        lt, lt_b = self._sgu_ln_tmp.next()
        stats, stats_b = self._lnst.next()
        for c in range(4):
            self.sc.add("dve", lambda e, c=c: e.bn_stats(stats[:, c * 6:(c + 1) * 6], gv[:, c * 512:(c + 1) * 512]),
                        r=[gv_b], w=[stats_b])
        self.sc.add("dve", lambda e: e.bn_aggr(lt[:, 0:2], stats[:, :]), r=[stats_b], w=[lt_b])
        self.act(lt[:, 2:3], lt[:, 1:2], AF.Sqrt, r=[lt_b, self.epsl_b], w=[lt_b], bias=self.epsl[:, 0:1])
        self.sc.add("dve", lambda e: e.reciprocal(lt[:, 3:4], lt[:, 2:3]), r=[lt_b], w=[lt_b])
        self.ts("dve", gv[:, :], gv[:, :], lt[:, 0:1], lt[:, 3:4], ALU.subtract, ALU.mult, r=[gv_b, lt_b], w=[gv_b])
        self.tt("pool", gv[:, :], gv[:, :], Gb[:, :], ALU.mult, r=[gv_b, Gb_b], w=[gv_b])
        self.tt("pool", vl[:, :], gv[:, :], Bb[:, :], ALU.add, r=[gv_b, Bb_b], w=[vl_b])

    def final_norm_store(self, t, xt, xt_b, junk, junk_b, str_, Gf, Gf_b):
        stt_, st_b = str_.next()
        self.rms_rstd(xt[:, :], xt_b, junk[:, 0:D], junk_b, stt_[:, 0:1], st_b, stt_[:, 1:2], st_b, D,
                      self.epsn[:, 0:1], self.epsn_b)
        self.stt(xt[:, :], xt[:, :], stt_[:, 1:2], Gf[:, :], ALU.mult, ALU.mult, r=[xt_b, st_b, Gf_b], w=[xt_b])
        self.dma(self.out.ap()[t * 128:(t + 1) * 128, :], xt[:, :], xt_b, r=[xt_b], w=[self.out_b[t]])

    def rope_tables(self, st):
        d = self.d
        cc, cc_b = self.sb(st, "cc", [128, NT, 64], F32)
        ss, ss_b = self.sb(st, "ss", [128, NT, 64], F32)
        c2T, c2T_b = self.sb(st, "c2T", [64, S], F32)
        s2T, s2T_b = self.sb(st, "s2T", [64, S], F32)
        with self.scope() as tmp:
            pi_, pi_b = self.sb(tmp, "posi", [128, NT], I32)
            pf, pf_b = self.sb(tmp, "posf", [128, NT], F32)
            ivt, ivt_b = self.sb(tmp, "ivt", [128, 64], F32)
            ang, ang_b = self.sb(tmp, "ang", [128, NT, 64], F32)
            ki, ki_b = self.sb(tmp, "ki", [128, NT * 64], I32)
            kf, kf_b = self.sb(tmp, "kf", [128, NT * 64], F32)
            self.dma(pi_[:, :], d["pos_tok"].ap(), pi_b, w=[pi_b])
            self.dma(ivt[:, :], d["invf_tok"].ap(), ivt_b, w=[ivt_b])
            self.cp("dve", pf[:, :], pi_[:, :], r=[pi_b], w=[pf_b])
            for t in range(NT):
                self.ts("dve", ang[:, t, :], ivt[:, :], pf[:, t:t + 1], None, ALU.mult, None,
                        r=[ivt_b, pf_b], w=[ang_b])
            angf = ang[:, :, :].rearrange("p t f -> p (t f)")
            self.trig(angf, ang_b, ki, ki_b, kf, kf_b, cc[:, :, :].rearrange("p t f -> p (t f)"), cc_b,
                      ss[:, :, :].rearrange("p t f -> p (t f)"), ss_b, 128, NT * 64)
        with self.scope() as tmp:
            pr_, pr_b = self.sb(tmp, "pri", [64, S], I32)
            ivc, ivc_b = self.sb(tmp, "ivc", [64, 1], F32)
            ang2, ang2_b = self.sb(tmp, "ang2", [64, S], F32)
            ki, ki_b = self.sb(tmp, "ki2", [64, S], I32)
            kf, kf_b = self.sb(tmp, "kf2", [64, S], F32)
            self.dma(pr_[:, :], bass.AP(d["pos_row"], 0, [[0, 64], [1, S]]), pr_b, w=[pr_b])
            self.dma(ivc[:, :], d["invf_col"].ap(), ivc_b, w=[ivc_b])
            self.cp("dve", ang2[:, :], pr_[:, :], r=[pr_b], w=[ang2_b])
            self.ts("dve", ang2[:, :], ang2[:, :], ivc[:, 0:1], None, ALU.mult, None, r=[ang2_b, ivc_b], w=[ang2_b])
            self.trig(ang2[:, :], ang2_b, ki, ki_b, kf, kf_b, c2T[:, :], c2T_b, s2T[:, :], s2T_b, 64, S)
        return (cc, cc_b, ss, ss_b, c2T, c2T_b, s2T, s2T_b)

    def trig(self, ang, ang_b, ki, ki_b, kf, kf_b, cosd, cos_b, sind, sin_b, P, N):
        C1 = 6.28125
        C2 = TWO_PI - C1
        PI_LO = 3.1415925
        self.ts("dve", kf[:, :], ang, 1.0 / TWO_PI, None, ALU.mult, None, r=[ang_b], w=[kf_b])
        self.cp("dve", ki[:, :], kf[:, :], r=[kf_b], w=[ki_b])
        self.cp("dve", kf[:, :], ki[:, :], r=[ki_b], w=[kf_b])
        self.stt(ang, kf[:, :], -C1, ang, ALU.mult, ALU.add, r=[kf_b, ang_b], w=[ang_b])
        self.stt(ang, kf[:, :], -C2, ang, ALU.mult, ALU.add, r=[kf_b, ang_b], w=[ang_b])
        self.ts("dve", kf[:, :], ang, math.pi, -TWO_PI, ALU.is_gt, ALU.mult, r=[ang_b], w=[kf_b])
        self.tt("dve", kf[:, :], kf[:, :], ang, ALU.add, r=[kf_b, ang_b], w=[kf_b])
        self.ts("dve", kf[:, :], kf[:, :], PI_LO, -PI_LO, ALU.min, ALU.max, r=[kf_b], w=[kf_b])
        self.act(sind, kf[:, :], AF.Sin, r=[kf_b], w=[sin_b])
        self.ts("dve", kf[:, :], ang, math.pi / 2, -TWO_PI, ALU.is_gt, ALU.mult, r=[ang_b, sin_b], w=[kf_b])
        self.stt(kf[:, :], ang, math.pi / 2, kf[:, :], ALU.add, ALU.add, r=[kf_b, ang_b], w=[kf_b])
        self.ts("dve", kf[:, :], kf[:, :], PI_LO, -PI_LO, ALU.min, ALU.max, r=[kf_b], w=[kf_b])
        self.act(cosd, kf[:, :], AF.Sin, r=[kf_b], w=[cos_b])

    def mla_layer(self, l, j, mode):
        d = self.d
        if self.part == "B":
            self.mla_phase3(l, j, mode)
            self.x_from_input = False
            return
        with self.scope() as st:
            cqT, cqT_b = self.sb(st, "cqT", [128, 3, S], BF16)
            ckvT, ckvT_b = self.sb(st, "ckvT", [128, 2, S], BF16)
            krT, krT_b = self.sb(st, "krT", [64, S], BF16)
            cc, cc_b, ss, ss_b, c2T, c2T_b, s2T, s2T_b = self.rope_tables(st)
            qg, qg_b = self.sb(st, "qg", [128, 3], F32)
            kvg, kvg_b = self.sb(st, "kvg", [128, 2], F32)
            gcol, gcol_b = self.sb(st, "mgcol", [128, 8], F32)
            self.dma(qg[:, :], d["mla_qg"].ap()[j], qg_b, w=[qg_b])
            self.dma(kvg[:, :], d["mla_kvg"].ap()[j], kvg_b, w=[kvg_b])
            self.dma(gcol[:, :], d["norm_g"].ap()[l], gcol_b, w=[gcol_b])
            import os
            stop = os.environ.get("MLA_STOP", "")
            if stop == "rope":
                return
            self.mla_phase1(st, l, j, cqT, cqT_b, ckvT, ckvT_b, krT, krT_b, cc, cc_b, ss, ss_b, gcol, gcol_b)
            if stop == "p1":
                return
            if stop != "skip2":
                self.mla_phase2(st, l, j, cqT, cqT_b, ckvT, ckvT_b, krT, krT_b, c2T, c2T_b, s2T, s2T_b,
                                qg, qg_b, kvg, kvg_b)
        if stop == "p2" or self.part == "A":
            return
        self.mla_phase3(l, j, mode)
        self.x_from_input = False

    def mla_phase1(self, st0, l, j, cqT, cqT_b, ckvT, ckvT_b, krT, krT_b, cc, cc_b, ss, ss_b, gcol, gcol_b):
        d = self.d
        with self.scope() as st:
            Win, Win_b = self.sb(st, "mWin", [128, 8, MLA_IN_W], BF16)
            with self.scope() as wst:
                stg = Ring([self.sb(wst, "mstg", [128, MLA_IN_W], F32) for _ in range(3)])
                self.load_weight_rows(stg, Win, Win_b, d["mla_w_in"].ap()[j], gcol, gcol_b, 8, MLA_IN_W, MLA_IN_W)
            xts = Ring([self.sb(st, "mxt", [128, D], F32) for _ in range(4)])
            xnr = Ring([self.sb(st, "mxn", [128, D], BF16) for _ in range(2)])
            xnTr = Ring([self.sb(st, "mxnT", [128, 8, 512], BF16) for _ in range(2)])
            str_ = Ring([self.sb(st, "mst", [128, 8], F32) for _ in range(4)])
            cqn_r = Ring([self.sb(st, "mcqn", [128, 704], BF16) for _ in range(2)])
            rtmp = Ring([self.sb(st, "mrt", [128, 128], F32) for _ in range(2)])
            gt_r = Ring([self.sb(st, "mgt", [128, 4, 512], BF16) for _ in range(2)])
            pT = Ring([self.ps(st, "mpT", [128, 8, 128], BF16) for _ in range(2)])
            pA = Ring([self.ps(st, "mpA", [128, 512], F32) for _ in range(2)])
            pB = Ring([self.ps(st, "mpB", [128, 512], F32) for _ in range(2)])
            pG = Ring([self.ps(st, "mpG", [128, 512], F32) for _ in range(2)])
            import os
            p1s = os.environ.get("P1_STOP", "")
            if p1s == "w":
                return
            xt_of = {}

            def issue_load(t):
                xt, xt_b = xts.next()
                src, srcb = self.x_src(t)
                self.dma(xt[:, :], src, xt_b, r=srcb, w=[xt_b])
                xt_of[t] = (xt, xt_b)

            xnT_of = {}

            def a1(g):
                xnT, xnT_b = xnTr.next()
                xnT_of[g] = (xnT, xnT_b)
                for tt_ in range(4):
                    t = g * 4 + tt_
                    xt, xt_b = xt_of.pop(t)
                    xn, xn_b = xnr.next()
                    s4, s4_b = str_.next()
                    self.rms_rstd(xt[:, :], xt_b, xn[:, :], xn_b, s4[:, 0:1], s4_b, s4[:, 1:2], s4_b, D,
                                  self.epsn[:, 0:1], self.epsn_b)
                    self.ts("dve", xn[:, :], xt[:, :], s4[:, 1:2], None, ALU.mult, None, r=[xt_b, s4_b], w=[xn_b])
                    p, p_b = pT.next()
                    for kc in range(8):
                        self.tr(p[:, kc, :], xn[:, kc * 128:(kc + 1) * 128], self.ident[:, :],
                                r=[xn_b, self.ident_b], w=[p_b])
                    self.cp("dve", xnT[:, :, tt_ * 128:(tt_ + 1) * 128], p[:, :, :], r=[p_b], w=[xnT_b])

            for t in range(4):
                issue_load(t)
            a1(0)
            for g in range(NG):
                if g + 1 < NG:
                    for t in range(4 * (g + 1), 4 * (g + 2)):
                        issue_load(t)
                xnT, xnT_b = xnT_of.pop(g)
                if p1s == "n":
                    return
                for tt_ in range(4):
                    t = g * 4 + tt_
                    a, a_b = pA.next()
                    b, b_b = pB.next()
                    for kc in range(8):
                        self.mm(a[:, 0:384], xnT[:, kc, tt_ * 128:(tt_ + 1) * 128], Win[:, kc, 0:384], kc == 0, kc == 7,
                                r=[xnT_b, Win_b], w=[a_b])
                    for kc in range(8):
                        self.mm(b[:, 0:320], xnT[:, kc, tt_ * 128:(tt_ + 1) * 128], Win[:, kc, 384:704], kc == 0, kc == 7,
                                r=[xnT_b, Win_b], w=[b_b])
                    s4, s4_b = str_.next()
                    cqn, cqn_b = cqn_r.next()
                    rt, rt_b = rtmp.next()
                    self.act(cqn[:, 0:384], a[:, 0:384], AF.Square, r=[a_b], w=[cqn_b, s4_b],
                             scale=float(Q_LORA ** -0.5), accum_out=s4[:, 0:1])
                    self.act(cqn[:, 384:640], b[:, 0:256], AF.Square, r=[b_b], w=[cqn_b, s4_b],
                             scale=float(KV_LORA ** -0.5), accum_out=s4[:, 1:2])
                    self.act(s4[:, 2:4], s4[:, 0:2], AF.Sqrt, r=[s4_b, self.epsn_b], w=[s4_b], bias=self.epsn[:, 0:1])
                    self.sc.add("dve", lambda e, s4=s4: e.reciprocal(s4[:, 4:6], s4[:, 2:4]), r=[s4_b], w=[s4_b])
                    self.ts("dve", cqn[:, 0:384], a[:, 0:384], s4[:, 4:5], None, ALU.mult, None,
                            r=[a_b, s4_b], w=[cqn_b])
                    self.ts("dve", cqn[:, 384:640], b[:, 0:256], s4[:, 5:6], None, ALU.mult, None,
                            r=[b_b, s4_b], w=[cqn_b])
                    self.tt("dve", rt[:, 0:64], b[:, 256:320], cc[:, t, :], ALU.mult, r=[b_b, cc_b], w=[rt_b])
                    self.tt("dve", rt[:, 64:128], b[:, 256:320], ss[:, t, :], ALU.mult, r=[b_b, ss_b], w=[rt_b])
                    self.tt("dve", cqn[:, 640:672], rt[:, 0:32], rt[:, 96:128], ALU.subtract, r=[rt_b], w=[cqn_b])
                    self.tt("dve", cqn[:, 672:704], rt[:, 32:64], rt[:, 64:96], ALU.add, r=[rt_b], w=[cqn_b])
                    p, p_b = pT.next()
                    for c in range(5):
                        self.tr(p[:, c, :], cqn[:, c * 128:(c + 1) * 128], self.ident[:, :],
                                r=[cqn_b, self.ident_b], w=[p_b])
                    self.tr(p[0:64, 5, :], cqn[:, 640:704], self.ident[:, :], r=[cqn_b, self.ident_b], w=[p_b])
                    self.cp("act", cqT[:, :, t * 128:(t + 1) * 128], p[:, 0:3, :], r=[p_b], w=[cqT_b])
                    self.cp("dve", ckvT[:, :, t * 128:(t + 1) * 128], p[:, 3:5, :], r=[p_b], w=[ckvT_b])
                    self.cp("dve", krT[0:64, t * 128:(t + 1) * 128], p[0:64, 5, :], r=[p_b], w=[krT_b])
                if p1s == "l":
                    return
                for h4 in range(4):
                    if h4 == 2 and g + 1 < NG:
                        a1(g + 1)
                    gt, gt_b = gt_r.next()
                    for hh in range(4):
                        h = h4 * 4 + hh
                        pg, pg_b = pG.next()
                        c0 = 704 + h * 128
                        for kc in range(8):
                            self.mm(pg[:, :], Win[:, kc, c0:c0 + 128], xnT[:, kc, :], kc == 0, kc == 7,
                                    r=[Win_b, xnT_b], w=[pg_b])
                        self.act(gt[:, hh, :], pg[:, :], AF.Silu, r=[pg_b], w=[gt_b])
                    self.dma(self.gateT.ap()[h4 * 4:(h4 + 1) * 4, :, g * 512:(g + 1) * 512].rearrange("h p n -> p h n"),
                             gt[:, :, :], gt_b, r=[gt_b], w=[self.gate_b[h4][g]])

    def mla_phase2(self, st0, l, j, cqT, cqT_b, ckvT, ckvT_b, krT, krT_b, c2T, c2T_b, s2T, s2T_b,
                   qg, qg_b, kvg, kvg_b):
        d = self.d
        with self.scope() as st:
            ones, ones_b = self.sb(st, "ones", [128, 128], BF16)
            self.sc.add("pool", lambda e: e.memset(ones[:, :], 1.0), w=[ones_b])
            hb = Ring([(self.sb(st, "KT", [128, S], BF16), self.sb(st, "V", [128, NT, 128], BF16),
                        self.sb(st, "QT", [128, S], BF16), self.sb(st, "QrT", [64, S], BF16)) for _ in range(2)])
            wq_r = Ring([self.sb(st, "wq", [128, 3, 256], BF16) for _ in range(2)])
            wkv_r = Ring([self.sb(st, "wkv", [128, 2, 256], BF16) for _ in range(2)])
            sq_r = Ring([self.sb(st, "sq", [128, 3, 192], F32) for _ in range(2)])
            skv_r = Ring([self.sb(st, "skv", [128, 2, 256], F32) for _ in range(2)])
            pt_r = Ring([self.sb(st, "pt", [128, 512], BF16) for _ in range(6)])
            rtm = Ring([self.sb(st, "rtm", [64, 2, 512], F32) for _ in range(2)])
            rc_r = Ring([self.sb(st, "rc", [128, 512], F32) for _ in range(2)])
            gl_r = Ring([self.sb(st, "gl", [128, 512], BF16) for _ in range(2)])
            og_r = Ring([self.sb(st, "ogs", [128, 512], BF16) for _ in range(2)])
            pw = Ring([self.ps(st, "pw", [128, 512], F32) for _ in range(4)])
            po_r = Ring([self.ps(st, "po", [128, 512], F32) for _ in range(2)])
            prs_r = Ring([self.ps(st, "prs", [128, 512], F32) for _ in range(2)])

            def prep(h):
                (KT, KT_b), (V, V_b), (QT, QT_b), (QrT, QrT_b) = hbufs[h]
                wq, wq_b = wq_r.next()
                wkv, wkv_b = wkv_r.next()
                sq, sq_b = sq_r.next()
                skv, skv_b = skv_r.next()
                self.dma(sq[:, :, :], d["mla_w_uq"].ap()[j][:, h * 192:(h + 1) * 192].rearrange("(c p) n -> p c n", p=128),
                         sq_b, w=[sq_b])
                self.dma(skv[:, :, :], d["mla_w_ukv"].ap()[j][:, h * 256:(h + 1) * 256].rearrange("(c p) n -> p c n", p=128),
                         skv_b, w=[skv_b])
                for c in range(3):
                    self.ts("pool", wq[:, c, 0:192], sq[:, c, :], qg[:, c:c + 1], None, ALU.mult, None,
                            r=[sq_b, qg_b], w=[wq_b])
                    self.ts("pool", wq[:, c, 192:224], sq[:, c, 160:192], qg[:, c:c + 1], -1.0, ALU.mult, ALU.mult,
                            r=[sq_b, qg_b], w=[wq_b])
                    self.ts("pool", wq[:, c, 224:256], sq[:, c, 128:160], qg[:, c:c + 1], None, ALU.mult, None,
                            r=[sq_b, qg_b], w=[wq_b])
                for c in range(2):
                    self.ts("pool", wkv[:, c, :], skv[:, c, :], kvg[:, c:c + 1], None, ALU.mult, None,
                            r=[skv_b, kvg_b], w=[wkv_b])
                for g in range(NG):
                    cs = slice(g * 512, (g + 1) * 512)
                    p1, p1_b = pw.next()
                    for c in range(3):
                        self.mm(p1[:, :], wq[:, c, 0:128], cqT[:, c, cs], c == 0, c == 2, r=[wq_b, cqT_b], w=[p1_b])
                    self.cp("act", QT[:, cs], p1[:, :], r=[p1_b], w=[QT_b])
                    p2, p2_b = pw.next()
                    for c in range(3):
                        self.mm(p2[0:64, :], wq[:, c, 128:192], cqT[:, c, cs], c == 0, c == 2, r=[wq_b, cqT_b], w=[p2_b])
                    p3, p3_b = pw.next()
                    for c in range(3):
                        self.mm(p3[0:64, :], wq[:, c, 192:256], cqT[:, c, cs], c == 0, c == 2, r=[wq_b, cqT_b], w=[p3_b])
                    rt, rt_b = rtm.next()
                    self.tt("dve", rt[:, 0, :], p2[0:64, :], c2T[:, cs], ALU.mult, r=[p2_b, c2T_b], w=[rt_b])
                    self.tt("dve", rt[:, 1, :], p3[0:64, :], s2T[:, cs], ALU.mult, r=[p3_b, s2T_b], w=[rt_b])
                    self.tt("pool", QrT[:, cs], rt[:, 0, :], rt[:, 1, :], ALU.add, r=[rt_b], w=[QrT_b])
                    p4, p4_b = pw.next()
                    for c in range(2):
                        self.mm(p4[:, :], wkv[:, c, 0:128], ckvT[:, c, cs], c == 0, c == 1, r=[wkv_b, ckvT_b], w=[p4_b])
                    self.cp("act", KT[:, cs], p4[:, :], r=[p4_b], w=[KT_b])
                    p5, p5_b = pw.next()
                    for tt_ in range(4):
                        t = g * 4 + tt_
                        for c in range(2):
                            self.mm(p5[:, tt_ * 128:(tt_ + 1) * 128], ckvT[:, c, t * 128:(t + 1) * 128],
                                    wkv[:, c, 128:256], c == 0, c == 1, r=[ckvT_b, wkv_b], w=[p5_b])
                    self.cp("dve", V[:, g * 4:(g + 1) * 4, :].rearrange("p t d -> p (t d)"), p5[:, :],
                            r=[p5_b], w=[V_b])

            def attention(h):
                (KT, KT_b), (V, V_b), (QT, QT_b), (QrT, QrT_b) = hbufs[h]
                blocks = []
                for jq in range(NG):
                    nkb = 4 * jq + 4
                    for kb in range(nkb):
                        blocks.append((jq, kb, nkb))
                LA = 3
                acc = {}
                info = {}
                for idx in range(len(blocks) + LA):
                    if idx < len(blocks):
                        jq, kb, nkb = blocks[idx]
                        i = kb - 4 * jq
                        off = 128 * i if i > 0 else 0
                        qs = slice(jq * 512 + off, (jq + 1) * 512)
                        ks = slice(kb * 128, (kb + 1) * 128)
                        sp_, sp_b = pw.next()
                        self.mm(sp_[:, off:512], KT[:, ks], QT[:, qs], True, False, r=[KT_b, QT_b], w=[sp_b])
                        self.mm(sp_[:, off:512], krT[0:64, ks], QrT[0:64, qs], False, True, r=[krT_b, QrT_b], w=[sp_b])
                        pt, pt_b = pt_r.next()
                        self.act(pt[:, off:512], sp_[:, off:512], AF.Exp, r=[sp_b], w=[pt_b], scale=float(ATTN_SCALE))
                        if i >= 0:
                            self.tt("pool", pt[:, off:off + 128], pt[:, off:off + 128], self.tri[:, :], ALU.mult,
                                    r=[pt_b, self.tri_b], w=[pt_b])
                        info[idx] = (pt, pt_b, off)
                    k = idx - LA
                    if k >= 0:
                        jq, kb, nkb = blocks[k]
                        pt, pt_b, off = info.pop(k)
                        if kb == 0:
                            acc[jq] = (po_r.next(), prs_r.next())
                        (po, po_b), (prs, prs_b) = acc[jq]
                        first = (kb == 0)
                        lastb = (kb == nkb - 1)
                        self.mm(po[:, off:512], V[:, kb, :], pt[:, off:512], first, lastb, r=[V_b, pt_b], w=[po_b])
                        self.mm(prs[:, off:512], ones[:, :], pt[:, off:512], first, lastb, r=[ones_b, pt_b], w=[prs_b])
                        if lastb:
                            rc, rc_b = rc_r.next()
                            gl, gl_b = gl_r.next()
                            og, og_b = og_r.next()
                            cs = slice(jq * 512, (jq + 1) * 512)
                            self.dma(gl[:, :], self.gateT.ap()[h, :, cs], gl_b, r=[self.gate_b[h // 4][jq]], w=[gl_b])
                            self.sc.add("dve", lambda e, rc=rc, prs=prs: e.reciprocal(rc[:, :], prs[:, :]),
                                        r=[prs_b], w=[rc_b])
                            self.tt("dve", rc[:, :], po[:, :], rc[:, :], ALU.mult, r=[po_b, rc_b], w=[rc_b])
                            self.tt("pool", og[:, :], rc[:, :], gl[:, :], ALU.mult, r=[rc_b, gl_b], w=[og_b])
                            self.dma(self.ogT.ap()[h, :, cs], og[:, :], og_b, r=[og_b], w=[self.og_b[h][jq]])
                            del acc[jq]

            hbufs = {}
            hbufs[0] = hb.next()
            prep(0)
            for h in range(HEADS):
                if h + 1 < HEADS:
                    hbufs[h + 1] = hb.next()
                    prep(h + 1)
                attention(h)
                del hbufs[h]

    def mla_phase3(self, l, j, mode):
        d = self.d
        final = (mode == 'final')
        with self.scope() as st:
            Wo, Wo_b = self.sb(st, "mWo", [128, 16, D], BF16)
            with self.scope() as wst:
                stg = Ring([self.sb(wst, "mostg", [128, D], F32) for _ in range(3)])
                self.load_weight_rows(stg, Wo, Wo_b, d["mla_w_o"].ap()[j], None, None, 16, D, D)
            if final:
                Gf, Gf_b = self.sb(st, "mGf", [128, D], F32)
                self.dma(Gf[:, :], bass.AP(d["final_g"], 0, [[0, 128], [1, D]]), Gf_b, w=[Gf_b])
                junk, junk_b = self.sb(st, "mjunk", [128, D], BF16)
                str_ = Ring([self.sb(st, "m3st", [128, 8], F32) for _ in range(3)])
            ogr = Ring([self.sb(st, "og3", [128, 16, 512], BF16) for _ in range(2)])
            xts = Ring([self.sb(st, "m3xt", [128, D], F32) for _ in range(3)])
            py = Ring([self.ps(st, "m3py", [128, 512], F32) for _ in range(4)])
            for g in range(NG):
                og, og_b = ogr.next()
                cs = slice(g * 512, (g + 1) * 512)
                for q in range(4):
                    self.dma(og[:, q * 4:(q + 1) * 4, :],
                             self.ogT.ap()[q * 4:(q + 1) * 4, :, cs].rearrange("h p n -> p h n"), og_b,
                             r=[self.og_b[q * 4 + hh][g] for hh in range(4)], w=[og_b])
                for tt_ in range(4):
                    t = g * 4 + tt_
                    xt, xt_b = xts.next()
                    src, srcb = self.x_src(t)
                    self.dma(xt[:, :], src, xt_b, r=srcb, w=[xt_b])
                    for hf in range(2):
                        y, y_b = py.next()
                        for h in range(16):
                            self.mm(y[:, :], og[:, h, tt_ * 128:(tt_ + 1) * 128], Wo[:, h, hf * 512:(hf + 1) * 512],
                                    h == 0, h == 15, r=[og_b, Wo_b], w=[y_b])
                        self.tt("dve", xt[:, hf * 512:(hf + 1) * 512], y[:, :], xt[:, hf * 512:(hf + 1) * 512], ALU.add,
                                r=[y_b, xt_b], w=[xt_b])
                    if final:
                        self.final_norm_store(t, xt, xt_b, junk, junk_b, str_, Gf, Gf_b)
                    elif mode == 'out':
                        self.dma(self.out.ap()[t * 128:(t + 1) * 128, :], xt[:, :], xt_b, r=[xt_b],
                                 w=[self.out_b[t]])
                    else:
                        self.dma(self.xres.ap()[t * 128:(t + 1) * 128, :], xt[:, :], xt_b, r=[xt_b],
                                 w=[self.xres_b[t]])


def _host_inputs(inp, b):
    f32 = np.float32
    pos = np.ascontiguousarray(inp["positions"][b]).astype(np.int32)
    inv_freq = (1.0 / (10000.0 ** (np.arange(0, ROPE, 2, dtype=np.float32) / ROPE))).astype(f32)
    m = {}
    m["x"] = np.ascontiguousarray(inp["x"][b], dtype=f32)
    m["pos_tok"] = np.ascontiguousarray(pos.reshape(NT, 128).T)
    m["pos_row"] = np.ascontiguousarray(pos.reshape(1, S))
    m["invf_tok"] = np.ascontiguousarray(np.broadcast_to(np.concatenate([inv_freq, inv_freq])[None, :], (128, 64)), dtype=f32)
    m["invf_col"] = np.ascontiguousarray(np.concatenate([inv_freq, inv_freq]).reshape(64, 1), dtype=f32)
    m["ident"] = np.eye(128, dtype=f32).astype(ml_dtypes.bfloat16)
    m["tri"] = np.triu(np.ones((128, 128), dtype=f32)).astype(ml_dtypes.bfloat16)
    m["norm_g"] = np.ascontiguousarray(np.asarray(inp["norm_g"], f32).reshape(DEPTH, 8, 128).transpose(0, 2, 1))
    m["final_g"] = np.ascontiguousarray(np.asarray(inp["final_g"], f32).reshape(1, D))
    m["mla_w_in"] = np.ascontiguousarray(inp["mla_w_in"], dtype=f32)
    m["mla_qg"] = np.ascontiguousarray(np.asarray(inp["mla_q_norm_g"], f32).reshape(2, 3, 128).transpose(0, 2, 1))
    m["mla_kvg"] = np.ascontiguousarray(np.asarray(inp["mla_kv_norm_g"], f32).reshape(2, 2, 128).transpose(0, 2, 1))
    m["mla_w_uq"] = np.ascontiguousarray(inp["mla_w_uq"], dtype=f32)
    m["mla_w_ukv"] = np.ascontiguousarray(inp["mla_w_ukv"], dtype=f32)
    m["mla_w_o"] = np.ascontiguousarray(inp["mla_w_o"], dtype=f32)
    m["sgu_w_in"] = np.ascontiguousarray(inp["sgu_w_in"], dtype=f32)
    m["sgu_ln_g"] = np.ascontiguousarray(np.asarray(inp["sgu_ln_g"], f32).reshape(2, 1, BW))
    m["sgu_ln_b"] = np.ascontiguousarray(np.asarray(inp["sgu_ln_b"], f32).reshape(2, 1, BW))
    m["sgu_w_sT"] = np.ascontiguousarray(np.asarray(inp["sgu_w_s"], f32).transpose(0, 3, 1, 2))
    m["sgu_b_sT"] = np.ascontiguousarray(np.asarray(inp["sgu_b_s"], f32).transpose(0, 2, 1))
    m["sgu_w_o"] = np.ascontiguousarray(inp["sgu_w_o"], dtype=f32)
    return m


_CACHE = {}
_LAYERED = ["mla_w_in", "mla_qg", "mla_kvg", "mla_w_uq", "mla_w_ukv", "mla_w_o", "sgu_w_in", "sgu_ln_g", "sgu_ln_b",
            "sgu_w_sT", "sgu_b_sT", "sgu_w_o"]


def get_prog(layers, first_from_x=True, do_final=True, part=None, slot=False):
    key = (tuple(layers), first_from_x, do_final, part, slot)
    if key not in _CACHE:
        p = Prog(list(layers), first_from_x, do_final, part, slot)
        p.build()
        _CACHE[key] = p
    return _CACHE[key]


def _slot_map(m, l):
    j = l // 2
    m2 = dict(m)
    for k in _LAYERED:
        m2[k] = np.ascontiguousarray(m[k][j:j + 1])
    m2["norm_g"] = np.ascontiguousarray(m["norm_g"][l:l + 1])
    return m2


def _launch(prog, maps):
    res = run_bass_kernel_spmd(prog.nc, maps, core_ids=list(range(len(maps))))
    return res.results


def kernel_unfused(**inputs):
    inp = {k: np.asarray(v) for k, v in inputs.items()}
    B = inp["x"].shape[0]
    base = _host_inputs(inp, 0)
    per_core = []
    for b in range(B):
        m = dict(base)
        pos = np.ascontiguousarray(inp["positions"][b]).astype(np.int32)
        m["pos_tok"] = np.ascontiguousarray(pos.reshape(NT, 128).T)
        m["pos_row"] = np.ascontiguousarray(pos.reshape(1, S))
        per_core.append(m)
    xs = [np.ascontiguousarray(inp["x"][b], dtype=np.float32) for b in range(B)]
    for l in range(DEPTH):
        last = (l == DEPTH - 1)
        if l % 2 == 0:
            pa = get_prog([l % 2], True, False, "A", True)
            maps = [dict(_slot_map(per_core[b], l), x=xs[b]) for b in range(B)]
            ra = _launch(pa, maps)
            pb = get_prog([l % 2], True, last, "B", True)
            maps = [dict(maps[b], ogT=np.asarray(ra[b]["ogT"])) for b in range(B)]
            rb = _launch(pb, maps)
            xs = [np.asarray(rb[b]["out"], dtype=np.float32) for b in range(B)]
        else:
            ps_ = get_prog([l % 2], True, last, None, True)
            maps = [dict(_slot_map(per_core[b], l), x=xs[b]) for b in range(B)]
            r = _launch(ps_, maps)
            xs = [np.asarray(r[b]["out"], dtype=np.float32) for b in range(B)]
    return np.stack(xs, axis=0)


def kernel(**inputs):
    inp = {k: np.asarray(v) for k, v in inputs.items()}
    B = inp["x"].shape[0]
    base = _host_inputs(inp, 0)
    maps = []
    for b in range(B):
        m = dict(base)
        pos = np.ascontiguousarray(inp["positions"][b]).astype(np.int32)
        m["pos_tok"] = np.ascontiguousarray(pos.reshape(NT, 128).T)
        m["pos_row"] = np.ascontiguousarray(pos.reshape(1, S))
        m["x"] = np.ascontiguousarray(inp["x"][b], dtype=np.float32)
        maps.append(m)
    prog = get_prog(range(DEPTH), True, True, None, False)
    res = _launch(prog, maps)
    return np.stack([np.asarray(res[b]["out"], dtype=np.float32) for b in range(B)], axis=0)
```
